# Optimizing a Trainium2 kernel written in Bass

```python
import math
import jax, jax.numpy as jnp
from jax import lax
import numpy as np

D_MODEL = 2048
BATCH = 4
SEQ = 4096
DEPTH = 2

D_MIX = D_MODEL
N_MIXERS = 4
MIX_W = D_MIX // N_MIXERS
IN_COLS = 6 * MIX_W
S5_GROUP_CH = 16
S5_GROUPS = MIX_W // S5_GROUP_CH
S5_STATE = 64
CONV_WIDTH = 31
LRU_HEADS = 8
LRU_HEAD_DIM = MIX_W // LRU_HEADS
LRU_CONV_WIDTH = 4
LRU_C = 8.0
POOL_WINDOWS = (2, 4, 8, 16)
POOL_GROUP_W = MIX_W // len(POOL_WINDOWS)
FFN_DIM = 5504
FFN_CONV_WIDTH = 3
EPS = 1e-6

kernel_name = "hybrid_parallel_s5_conformer_rglru_pool"


def rmsnorm(x, g):
    xf = x.astype(jnp.float32)
    y = xf * lax.rsqrt(jnp.mean(xf * xf, axis=-1, keepdims=True) + EPS)
    return (y * g.astype(jnp.float32)).astype(x.dtype)


def layernorm(x, g, b):
    xf = x.astype(jnp.float32)
    mu = jnp.mean(xf, axis=-1, keepdims=True)
    var = jnp.mean(jnp.square(xf - mu), axis=-1, keepdims=True)
    y = (xf - mu) * lax.rsqrt(var + EPS)
    return (y * g.astype(jnp.float32) + b.astype(jnp.float32)).astype(x.dtype)


def causal_dwconv(x, w, b):
    k, c = w.shape
    y = lax.conv_general_dilated(
        x, w[:, None, :].astype(x.dtype), window_strides=(1,), padding=((k - 1, 0),),
        dimension_numbers=('NWC', 'WIO', 'NWC'), feature_group_count=c)
    return y + b


def _linear_combine(left, right):
    a_l, b_l = left
    a_r, b_r = right
    return (a_l * a_r, a_r * b_l + b_r)


def s5_mixer(u, lam_re, lam_im, log_step, b_re, b_im, c_re, c_im, d, w_glu, b_glu):
    dt = u.dtype
    bsz, seqlen, _ = u.shape
    uf = u.astype(jnp.float32).reshape(bsz, seqlen, S5_GROUPS, S5_GROUP_CH)
    lam = lax.complex(lam_re.astype(jnp.float32), lam_im.astype(jnp.float32))
    step = jnp.exp(log_step.astype(jnp.float32))[:, None]
    lam_bar = jnp.exp(lam * step)
    bmat = lax.complex(b_re.astype(jnp.float32), b_im.astype(jnp.float32))
    b_bar = ((lam_bar - 1.0) / lam)[..., None] * bmat
    bu = jnp.einsum('blgh,gph->blgp', uf.astype(jnp.complex64), b_bar)
    a = jnp.broadcast_to(lam_bar, bu.shape)
    _, states = lax.associative_scan(_linear_combine, (a, bu), axis=1)
    cmat = lax.complex(c_re.astype(jnp.float32), c_im.astype(jnp.float32))
    y = jnp.einsum('blgp,ghp->blgh', states, cmat).real
    y = y + d.astype(jnp.float32).reshape(S5_GROUPS, S5_GROUP_CH) * uf
    y = y.reshape(bsz, seqlen, MIX_W)
    g = jax.nn.gelu(y, approximate=True)
    out = g * jax.nn.sigmoid(g @ w_glu.astype(jnp.float32) + b_glu.astype(jnp.float32))
    return out.astype(dt)


def conformer_conv_mixer(v, g, w_dw, b_dw, ln_g, ln_b, w_pw, b_pw):
    h = v * jax.nn.sigmoid(g)
    h = causal_dwconv(h, w_dw, b_dw)
    h = layernorm(h, ln_g, ln_b)
    h = jax.nn.silu(h)
    return h @ w_pw + b_pw


def rglru_mixer(xb, gb, w_conv, b_conv, w_r, b_r, w_i, b_i, lam):
    dt = xb.dtype
    bsz, seqlen, _ = xb.shape
    xc = causal_dwconv(xb, w_conv, b_conv)
    xh = xc.reshape(bsz, seqlen, LRU_HEADS, LRU_HEAD_DIM)
    r = jax.nn.sigmoid(jnp.einsum('blhd,hde->blhe', xh, w_r).reshape(bsz, seqlen, MIX_W) + b_r)
    i = jax.nn.sigmoid(jnp.einsum('blhd,hde->blhe', xh, w_i).reshape(bsz, seqlen, MIX_W) + b_i)
    log_a = -LRU_C * r.astype(jnp.float32) * jax.nn.softplus(-lam.astype(jnp.float32))
    a = jnp.exp(log_a)
    mult = jnp.sqrt(-jnp.expm1(2.0 * log_a))
    bt = mult * (i * xc).astype(jnp.float32)
    _, h = lax.associative_scan(_linear_combine, (a, bt), axis=1)
    return h.astype(dt) * jax.nn.gelu(gb, approximate=True)


def pool_mixer(xp, w, scale):
    dt = xp.dtype
    bsz, seqlen, _ = xp.shape
    xf = xp.astype(jnp.float32)
    cs = jnp.cumsum(xf, axis=1)
    cs_pad = jnp.concatenate([jnp.zeros((bsz, 1, MIX_W), jnp.float32), cs], axis=1)
    pos = jnp.arange(seqlen, dtype=jnp.float32) + 1.0
    diffs = []
    for gi, win in enumerate(POOL_WINDOWS):
        sl = slice(gi * POOL_GROUP_W, (gi + 1) * POOL_GROUP_W)
        upper = cs_pad[:, 1:, sl]
        lower = jnp.concatenate(
            [jnp.zeros((bsz, win - 1, POOL_GROUP_W), jnp.float32), cs_pad[:, :seqlen - win + 1, sl]], axis=1)
        count = jnp.minimum(pos, float(win))[None, :, None]
        diffs.append((upper - lower) / count - xf[:, :, sl])
    dg = jnp.stack(diffs, axis=2)
    y = jnp.einsum('blgc,gce->blge', dg, w.astype(jnp.float32)).reshape(bsz, seqlen, MIX_W)
    return (y * scale.astype(jnp.float32)).astype(dt)


def conv_gated_mlp(h, w_up, w_dw, b_dw, w_down):
    up = h @ w_up
    gate, val = jnp.split(up, 2, axis=-1)
    gate = causal_dwconv(gate, w_dw, b_dw)
    return (jax.nn.gelu(gate, approximate=True) * val) @ w_down


def setup_inputs(seed: int = 0) -> dict:
    key = jax.random.key(seed)
    ks = iter(jax.random.split(key, 40))
    nrm = lambda shape, std: std * jax.random.normal(next(ks), shape, jnp.float32)
    L_, G, P, H = DEPTH, S5_GROUPS, S5_STATE, S5_GROUP_CH
    x = jax.random.normal(next(ks), (BATCH, SEQ, D_MODEL), jnp.float32)
    lam_im_base = jnp.pi * jnp.arange(P, dtype=jnp.float32)
    a0 = jax.random.uniform(next(ks), (L_, MIX_W), jnp.float32, 0.9, 0.999)
    a_base = a0 ** (1.0 / LRU_C)
    return {
        "x": x,
        "norm_mix_g": 1.0 + nrm((L_, D_MODEL), 0.02),
        "w_in": nrm((L_, D_MODEL, IN_COLS), D_MODEL ** -0.5),
        "s5_lam_re": -0.5 + nrm((L_, G, P), 0.01),
        "s5_lam_im": lam_im_base + nrm((L_, G, P), 0.01),
        "s5_log_step": jax.random.uniform(next(ks), (L_, G), jnp.float32, math.log(0.001), math.log(0.1)),
        "s5_b_re": nrm((L_, G, P, H), (2.0 * H) ** -0.5),
        "s5_b_im": nrm((L_, G, P, H), (2.0 * H) ** -0.5),
        "s5_c_re": nrm((L_, G, H, P), (2.0 * P) ** -0.5),
        "s5_c_im": nrm((L_, G, H, P), (2.0 * P) ** -0.5),
        "s5_d": nrm((L_, MIX_W), 1.0),
        "s5_w_glu": nrm((L_, MIX_W, MIX_W), MIX_W ** -0.5),
        "s5_b_glu": nrm((L_, MIX_W), 0.01),
        "cv_w_dw": nrm((L_, CONV_WIDTH, MIX_W), CONV_WIDTH ** -0.5),
        "cv_b_dw": nrm((L_, MIX_W), 0.01),
        "cv_ln_g": 1.0 + nrm((L_, MIX_W), 0.02),
        "cv_ln_b": nrm((L_, MIX_W), 0.01),
        "cv_w_pw": nrm((L_, MIX_W, MIX_W), MIX_W ** -0.5),
        "cv_b_pw": nrm((L_, MIX_W), 0.01),
        "lru_w_conv": nrm((L_, LRU_CONV_WIDTH, MIX_W), LRU_CONV_WIDTH ** -0.5),
        "lru_b_conv": nrm((L_, MIX_W), 0.01),
        "lru_w_r": nrm((L_, LRU_HEADS, LRU_HEAD_DIM, LRU_HEAD_DIM), LRU_HEAD_DIM ** -0.5),
        "lru_b_r": nrm((L_, MIX_W), 0.01),
        "lru_w_i": nrm((L_, LRU_HEADS, LRU_HEAD_DIM, LRU_HEAD_DIM), LRU_HEAD_DIM ** -0.5),
        "lru_b_i": nrm((L_, MIX_W), 0.01),
        "lru_lam": jnp.log(a_base) - jnp.log1p(-a_base),
        "pool_w": nrm((L_, len(POOL_WINDOWS), POOL_GROUP_W, POOL_GROUP_W), POOL_GROUP_W ** -0.5),
        "pool_scale": 1.0 + nrm((L_, MIX_W), 0.02),
        "w_out": nrm((L_, D_MIX, D_MODEL), D_MIX ** -0.5),
        "norm_ffn_g": 1.0 + nrm((L_, D_MODEL), 0.02),
        "ffn_w_up": nrm((L_, D_MODEL, 2 * FFN_DIM), D_MODEL ** -0.5),
        "ffn_w_dw": nrm((L_, FFN_CONV_WIDTH, FFN_DIM), FFN_CONV_WIDTH ** -0.5),
        "ffn_b_dw": nrm((L_, FFN_DIM), 0.01),
        "ffn_w_down": nrm((L_, FFN_DIM, D_MODEL), FFN_DIM ** -0.5),
        "norm_final_g": 1.0 + nrm((D_MODEL,), 0.02),
    }


def reference(x, norm_mix_g, w_in, s5_lam_re, s5_lam_im, s5_log_step, s5_b_re, s5_b_im,
              s5_c_re, s5_c_im, s5_d, s5_w_glu, s5_b_glu, cv_w_dw, cv_b_dw, cv_ln_g, cv_ln_b,
              cv_w_pw, cv_b_pw, lru_w_conv, lru_b_conv, lru_w_r, lru_b_r, lru_w_i, lru_b_i,
              lru_lam, pool_w, pool_scale, w_out, norm_ffn_g, ffn_w_up, ffn_w_dw, ffn_b_dw,
              ffn_w_down, norm_final_g):
    split_idx = [MIX_W * k for k in range(1, 6)]
    for l in range(DEPTH):
        h = rmsnorm(x, norm_mix_g[l])
        proj = h @ w_in[l]
        s5_u, cv_v, cv_g, lru_x, lru_g, pool_x = jnp.split(proj, split_idx, axis=-1)
        y_s5 = s5_mixer(s5_u, s5_lam_re[l], s5_lam_im[l], s5_log_step[l], s5_b_re[l], s5_b_im[l],
                        s5_c_re[l], s5_c_im[l], s5_d[l], s5_w_glu[l], s5_b_glu[l])
        y_cv = conformer_conv_mixer(cv_v, cv_g, cv_w_dw[l], cv_b_dw[l], cv_ln_g[l], cv_ln_b[l],
                                    cv_w_pw[l], cv_b_pw[l])
        y_lru = rglru_mixer(lru_x, lru_g, lru_w_conv[l], lru_b_conv[l], lru_w_r[l], lru_b_r[l],
                            lru_w_i[l], lru_b_i[l], lru_lam[l])
        y_pool = pool_mixer(pool_x, pool_w[l], pool_scale[l])
        mixed = jnp.concatenate([y_s5, y_cv, y_lru, y_pool], axis=-1)
        x = x + mixed @ w_out[l]
        h = rmsnorm(x, norm_ffn_g[l])
        x = x + conv_gated_mlp(h, ffn_w_up[l], ffn_w_dw[l], ffn_b_dw[l], ffn_w_down[l])
    return rmsnorm(x, norm_final_g)
```

```python
import contextlib
import numpy as np
import concourse.bass as bass
import concourse.mybir as mybir
from concourse.bass_utils import run_bass_kernel_spmd

F32 = mybir.dt.float32
BF16 = mybir.dt.bfloat16
ALU = mybir.AluOpType
AF = mybir.ActivationFunctionType

D = 2048
NB = 4
SEQ = 4096
DEPTH = 2
FF = 5504
NFT = FF // 128
EPS = 1e-6
TT = 512
PI = float(np.pi)

VOFF = {}
_o = 0
for _n, _w in [("g_mix", 16), ("g_ffn", 16), ("g_fin", 16), ("s5_d", 4), ("s5_bglu", 4), ("cv_wdw", 124),
               ("cv_bdw", 4), ("cv_lng", 4), ("cv_lnb", 4), ("cv_bpw", 4), ("lru_wc", 16), ("lru_bc", 4),
               ("lru_br", 4), ("lru_bi", 4), ("lru_lam", 4), ("pool_scale", 4), ("ffn_wdw", 129),
               ("ffn_bdw", 43), ("invc", 64)]:
    VOFF[_n] = _o
    _o += _w
NV = _o


class Reg:
    __slots__ = ("w", "r")

    def __init__(self):
        self.w = None
        self.r = {}


class Eng:
    def __init__(self, name, e, sem):
        self.name, self.e, self.sem, self.cnt, self.seen = name, e, sem, 0, {}


class FW:
    def __init__(self, nc, es, ndq=20):
        self.nc = nc
        mk = lambda n: es.enter_context(nc.semaphore(n))
        self.pe = Eng("pe", nc.tensor, mk("s_pe"))
        self.act = Eng("act", nc.scalar, mk("s_act"))
        self.dve = Eng("dve", nc.vector, mk("s_dve"))
        self.pool = Eng("pool", nc.gpsimd, mk("s_pool"))
        self.sp = Eng("sp", nc.sync, mk("s_sp"))
        self.dq = {}
        self.dqi = {}
        for e in (self.sp, self.pool):
            self.dq[e.name] = [[mk("d_%s%d" % (e.name, i)), 0] for i in range(ndq)]
            self.dqi[e.name] = 0

    def _deps(self, eng, reads, writes):
        deps = {}

        def add(tok):
            if tok is None:
                return
            k, sem, val = tok
            if eng.name == "pe" and k == "pe":
                return
            if k not in deps or deps[k][1] < val:
                deps[k] = (sem, val)

        for r in reads:
            add(r.w)
        for w in writes:
            add(w.w)
            for tok in w.r.values():
                add(tok)
        return deps

    def _wait(self, eng, deps):
        for k, (sem, val) in deps.items():
            if eng.seen.get(k, 0) < val:
                eng.e.wait_ge(sem, val)
                eng.seen[k] = val

    def op(self, eng, fn, reads=(), writes=(), inc=True):
        self._wait(eng, self._deps(eng, reads, writes))
        inst = fn(eng.e)
        if inc:
            eng.cnt += 1
            inst.then_inc(eng.sem, 1)
            tok = (eng.name, eng.sem, eng.cnt)
        else:
            tok = (eng.name, eng.sem, eng.cnt + 1)
        for r in reads:
            old = r.r.get(eng.name)
            if old is None or old[2] < tok[2]:
                r.r[eng.name] = tok
        for w in writes:
            w.w = tok
            w.r = {}
        return inst

    def dma(self, eng, out, in_, reads=(), writes=()):
        pool = self.dq[eng.name]
        i = self.dqi[eng.name]
        self.dqi[eng.name] = (i + 1) % len(pool)
        ent = pool[i]
        key = "d_%s%d" % (eng.name, i)
        deps = self._deps(eng, reads, writes)
        if ent[1] > 0:
            deps[key] = (ent[0], ent[1])
        self._wait(eng, deps)
        ent[1] += 16
        eng.e.dma_start(out=out, in_=in_).then_inc(ent[0], 16)
        tok = (key, ent[0], ent[1])
        for r in reads:
            r.r[key] = tok
        for w in writes:
            w.w = tok
            w.r = {}

    def gather(self, out, in_, idx_ap, reads=(), writes=()):
        eng = self.pool
        pool = self.dq[eng.name]
        i = self.dqi[eng.name]
        self.dqi[eng.name] = (i + 1) % len(pool)
        ent = pool[i]
        key = "d_%s%d" % (eng.name, i)
        deps = self._deps(eng, reads, writes)
        if ent[1] > 0:
            deps[key] = (ent[0], ent[1])
        self._wait(eng, deps)
        ent[1] += 16
        eng.e.indirect_dma_start(out=out, out_offset=None, in_=in_,
                                 in_offset=bass.IndirectOffsetOnAxis(ap=idx_ap, axis=0)).then_inc(ent[0], 16)
        tok = (key, ent[0], ent[1])
        for r in reads:
            r.r[key] = tok
        for w in writes:
            w.w = tok
            w.r = {}

    def barrier(self):
        engs = [self.pe, self.act, self.dve, self.pool, self.sp]
        for a in engs:
            deps = {}
            for b in engs:
                if b is not a and b.cnt > 0:
                    deps[b.name] = (b.sem, b.cnt)
            for name, pool in self.dq.items():
                for i, (sem, val) in enumerate(pool):
                    if val > 0:
                        deps["d_%s%d" % (name, i)] = (sem, val)
            self._wait(a, deps)

    def finish(self):
        for name, pool in self.dq.items():
            for sem, val in pool:
                if val > 0:
                    self.sp.e.wait_ge(sem, val)


class Rot:
    def __init__(self, items):
        self.items = items
        self.i = 0

    def next(self):
        it = self.items[self.i]
        self.i = (self.i + 1) % len(self.items)
        return it


def build(L=SEQ, depth=DEPTH, debug=False, split=True):
    nc = bass.Bass("TRN2", target_bir_lowering=False)
    NT = L // TT
    NP = L // TT
    dt_in = lambda name, shape: nc.dram_tensor(name, shape, F32, kind="ExternalInput").ap()
    skind = "ExternalOutput" if debug else "Internal"
    xT = dt_in("xT", [D, L])
    w_in = dt_in("w_in", [depth, 24, 128, 2048])
    w_out = dt_in("w_out", [depth, 16, 128, 2048])
    w_up = dt_in("w_up", [depth, 86, 128, 2048])
    w_down = dt_in("w_down", [depth, 16, 128, NFT * 128])
    w_glu = dt_in("w_glu", [depth, 4, 128, 512])
    w_pw = dt_in("w_pw", [depth, 4, 128, 512])
    w_lr = dt_in("w_lr", [depth, 4, 128, 128])
    w_li = dt_in("w_li", [depth, 4, 128, 128])
    w_pool = dt_in("w_pool", [depth, 4, 128, 128])
    vec = dt_in("vec", [depth, 128, NV])
    s5_pp = dt_in("s5_pp", [depth, 128, 48])
    s5_row = dt_in("s5_row", [depth, 3, 2048])
    s5_bz = dt_in("s5_bz", [depth, 2, 128, 2048])
    s5_cz = dt_in("s5_cz", [depth, 2, 128, 2048])
    ident_d = dt_in("ident", [128, 128])
    LO = L // 2 if split else L
    NTL = NT // 2 if split else NT
    idx_d = nc.dram_tensor("gidx", [128, 64], mybir.dt.int32, kind="ExternalInput").ap()
    flag_d = dt_in("hflag", [128, 1])
    outT = nc.dram_tensor("outT", [D, LO], F32, kind="ExternalOutput").ap()
    projU = nc.dram_tensor("projU", [512, L], BF16, kind=skind).ap()
    projR = nc.dram_tensor("projR", [2560, L], F32, kind=skind).ap()
    mixT = nc.dram_tensor("mixT", [NT * 128, 16 * TT], BF16, kind=skind).ap()
    xs = nc.dram_tensor("xs", [NT * 128, 16 * TT], F32, kind=skind).ap()
    tile_view = lambda ap, j: ap[j * 128:(j + 1) * 128, :].rearrange("p (c n) -> p c n", n=TT)
    chunk_view = lambda ap, j, ci: ap[j * 128:(j + 1) * 128, ci * TT:(ci + 1) * TT]
    rows_view = lambda ap, ci: ap.rearrange("(j p) (c n) -> p j c n", p=128, n=TT)[:, :, ci, :]
    r_projU, r_projR, r_mixT, r_xs, r_out = Reg(), Reg(), Reg(), Reg(), Reg()

    es = contextlib.ExitStack()
    with es:
        fw = FW(nc, es)
        pe, act, dve, pool, sp = fw.pe, fw.act, fw.dve, fw.pool, fw.sp
        cnt = [0]

        def sbt(st, shape, dt, name=None):
            cnt[0] += 1
            return st.enter_context(nc.sbuf_tensor(name or ("t%d" % cnt[0]), shape, dt))

        PS = Rot([(es.enter_context(nc.psum_tensor("ps%d" % i, [128, TT], F32)), Reg()) for i in range(8)])
        ones_bf = sbt(es, [128, 128], BF16)
        ones_f = sbt(es, [128, 128], F32)
        epsT = sbt(es, [128, 1], F32)
        oneT = sbt(es, [128, 1], F32)
        hpiT = sbt(es, [128, 1], F32)
        ident_f = sbt(es, [128, 128], F32)
        ident_b = sbt(es, [128, 128], BF16)
        vecT = [sbt(es, [128, NV], F32) for _ in range(depth)]
        r_const = Reg()
        fw.op(dve, lambda e: e.memset(ones_bf[:], 1.0), writes=[r_const])
        fw.op(dve, lambda e: e.memset(ones_f[:], 1.0), writes=[r_const])
        fw.op(dve, lambda e: e.memset(epsT[:], EPS), writes=[r_const])
        fw.op(dve, lambda e: e.memset(oneT[:], 1.0), writes=[r_const])
        fw.op(dve, lambda e: e.memset(hpiT[:], PI / 2), writes=[r_const])
        r_ident = Reg()
        fw.dma(sp, ident_f[:], ident_d, writes=[r_ident])
        fw.op(dve, lambda e: e.tensor_copy(out=ident_b[:], in_=ident_f[:]), reads=[r_ident], writes=[r_const])
        gidx = sbt(es, [128, 64], mybir.dt.int32)
        flagT = sbt(es, [128, 1], F32)
        r_gidx = Reg()
        fw.dma(sp, gidx[:], idx_d, writes=[r_gidx])
        fw.dma(sp, flagT[:], flag_d, writes=[r_gidx])
        r_vec = [Reg() for _ in range(depth)]
        for l in range(depth):
            fw.dma(sp, vecT[l][:], vec[l], writes=[r_vec[l]])

        def vcol(l, name, i=0, n=1):
            o = VOFF[name] + i
            return vecT[l][:, o:o + n]

        NWS = 5
        wsm = Rot([(sbt(es, [128, 2048], BF16), Reg()) for _ in range(NWS)])

        def load_w(dram_ap, rot=None, ncol=2048):
            t, r = (rot or wsm).next()
            fw.dma(pool, t[:, 0:ncol], dram_ap, writes=[r])
            return t, r

        evac_flip = [0]

        def evac(out_ap, ps_ap, reads, writes, bias=None, scale=None, func=None, eng=None):
            if func is None and bias is None and scale is None and eng is None:
                evac_flip[0] ^= 1
                eng = act if evac_flip[0] else dve
            if eng is dve:
                fw.op(dve, lambda e: e.tensor_copy(out=out_ap, in_=ps_ap), reads=reads, writes=writes)
            else:
                kw = {}
                if bias is not None:
                    kw["bias"] = bias
                if scale is not None:
                    kw["scale"] = scale
                fw.op(act, lambda e: e.activation(out=out_ap, in_=ps_ap, func=func or AF.Identity, **kw),
                      reads=reads, writes=writes)

        def rmsnorm(st_bufs, xt, xr, gcols, hout, hregs, out_fn=None, post_fn=None, w=TT):
            sq, sqr, lnv, lnr, rstd, rsr = st_bufs
            ps, psr = PS.next()
            for c in range(16):
                b = c % 2
                fw.op(act, lambda e: e.activation(out=sq[b][:, 0:w], in_=xt[:, c, :], func=AF.Square),
                      reads=[xr[c]], writes=[sqr[b]])
                fw.op(pe, lambda e: e.matmul(ps[:, 0:w], lhsT=ones_bf[:], rhs=sq[b][:, 0:w], start=(c == 0), stop=(c == 15)),
                      reads=[sqr[b], r_const], writes=[psr], inc=True)
            fw.op(act, lambda e: e.activation(out=lnv[:, 0:w], in_=ps[:, 0:w], func=AF.Ln, scale=1.0 / D, bias=epsT[:, 0:1]),
                  reads=[psr, r_const], writes=[lnr])
            fw.op(act, lambda e: e.activation(out=rstd[:, 0:w], in_=lnv[:, 0:w], func=AF.Exp, scale=-0.5),
                  reads=[lnr], writes=[rsr])
            for c in range(16):
                if out_fn is None:
                    oap, oreg = hout[:, c, :], hregs[c]
                else:
                    oap, oreg = out_fn(c)
                fw.op(dve, lambda e: e.scalar_tensor_tensor(out=oap, in0=xt[:, c, :], scalar=gcols[:, c:c + 1],
                                                            in1=rstd[:, 0:w], op0=ALU.mult, op1=ALU.mult),
                      reads=[xr[c], rsr], writes=[oreg])
                if post_fn is not None:
                    post_fn(c, oap, oreg)

        def norm_bufs(st):
            sq = [sbt(st, [128, TT], BF16) for _ in range(2)]
            return (sq, [Reg(), Reg()], sbt(st, [128, TT], F32), Reg(), sbt(st, [128, TT], F32), Reg())

        xview = lambda ap: ap.rearrange("(c p) l -> p c l", p=128)

        for l in range(depth):
            fw.barrier()
            with contextlib.ExitStack() as st:
                nb = norm_bufs(st)
                xts = [(sbt(st, [128, 16, TT], F32), [Reg() for _ in range(16)]) for _ in range(1)]
                hts = [(sbt(st, [128, 16, TT], BF16), [Reg() for _ in range(16)]) for _ in range(2)]
                stg_b = Rot([(sbt(st, [128, TT], BF16), Reg()) for _ in range(3)])
                stg_f = Rot([(sbt(st, [128, TT], F32), Reg()) for _ in range(3)])
                wres = sbt(st, [128, 24, 2048], BF16)
                r_wres = [Reg() for _ in range(24)]
                for mt in range(24):
                    fw.dma(pool, wres[:, mt, :], w_in[l, mt], writes=[r_wres[mt]])
                for j in range(NT):
                    cs = slice(j * TT, (j + 1) * TT)
                    xt, xr = xts[0]
                    ht, hr = hts[j % 2]
                    if l == 0:
                        fw.dma(sp, xt[:], xview(xT)[:, :, cs], writes=xr)
                    else:
                        fw.dma(sp, xt[:], tile_view(xs, j), reads=[r_xs], writes=xr)
                    rmsnorm(nb, xt, xr, vcol(l, "g_mix", 0, 16), ht, hr)
                    for mt in range(24):
                        W, wr = wres[:, mt, :], r_wres[mt]
                        ps, psr = PS.next()
                        for kc in range(16):
                            fw.op(pe, lambda e: e.matmul(ps[:], lhsT=W[:, kc * 128:(kc + 1) * 128], rhs=ht[:, kc, :],
                                                         start=(kc == 0), stop=(kc == 15)),
                                  reads=[wr, hr[kc]], writes=[psr], inc=(kc == 15))
                        if mt < 4:
                            sg, sr = stg_b.next()
                            evac(sg[:], ps[:], [psr], [sr])
                            fw.dma(sp, projU[mt * 128:(mt + 1) * 128, cs], sg[:], reads=[sr], writes=[r_projU])
                        else:
                            sg, sr = stg_f.next()
                            evac(sg[:], ps[:], [psr], [sr])
                            fw.dma(sp, projR[(mt - 4) * 128:(mt - 3) * 128, cs], sg[:], reads=[sr], writes=[r_projR])

            fw.barrier()
            with contextlib.ExitStack() as st:
                xl = sbt(st, [128, 4 + L], F32); r_xl = Reg()
                gl = sbt(st, [128, L], F32); r_gl = Reg()
                xc = sbt(st, [128, L], F32); r_xc = Reg()
                rb = sbt(st, [128, L], F32); r_rb = Reg()
                ib = sbt(st, [128, L], F32); r_ib = Reg()
                tmp = sbt(st, [128, L], F32); r_tmp = Reg()
                xcb = sbt(st, [128, L], BF16); r_xcb = Reg()
                ybf = sbt(st, [128, L], BF16); r_y = Reg()
                cc = sbt(st, [128, 4], F32); r_cc = Reg()
                ce = sbt(st, [128, 4], F32); r_ce = Reg()
                fw.op(dve, lambda e: e.memset(xl[:, 0:4], 0.0), writes=[r_xl])
                fw.op(act, lambda e: e.activation(out=ce[:], in_=vcol(l, "lru_lam", 0, 4), func=AF.Exp, scale=-1.0),
                      reads=[r_vec[l]], writes=[r_ce])
                fw.op(act, lambda e: e.activation(out=cc[:], in_=ce[:], func=AF.Ln, bias=oneT[:, 0:1]),
                      reads=[r_ce, r_const], writes=[r_cc])
                fw.op(dve, lambda e: e.tensor_scalar(out=cc[:], in0=cc[:], scalar1=-8.0, scalar2=None, op0=ALU.mult),
                      reads=[r_cc], writes=[r_cc])
                for ct in range(4):
                    rx = 1024 + ct * 128
                    rg = 1536 + ct * 128
                    fw.dma(sp, xl[:, 4:], projR[rx:rx + 128, :], reads=[r_projR], writes=[r_xl])
                    fw.dma(sp, gl[:], projR[rg:rg + 128, :], reads=[r_projR], writes=[r_gl])
                    wc = lambda k: vcol(l, "lru_wc", ct * 4 + k)
                    fw.op(dve, lambda e: e.tensor_scalar(out=xc[:], in0=xl[:, 1:1 + L], scalar1=wc(0), scalar2=vcol(l, "lru_bc", ct),
                                                         op0=ALU.mult, op1=ALU.add), reads=[r_xl, r_vec[l]], writes=[r_xc])
                    for k in range(1, 4):
                        fw.op(dve, lambda e: e.scalar_tensor_tensor(out=xc[:], in0=xl[:, 1 + k:1 + k + L], scalar=wc(k), in1=xc[:],
                                                                    op0=ALU.mult, op1=ALU.add), reads=[r_xl, r_xc], writes=[r_xc])
                    fw.op(act, lambda e: e.copy(out=xcb[:], in_=xc[:]), reads=[r_xc], writes=[r_xcb])
                    Wr, wrr = load_w(w_lr[l, ct], ncol=128)
                    Wi, wir = load_w(w_li[l, ct], ncol=128)
                    for p in range(NP):
                        cs = slice(p * TT, (p + 1) * TT)
                        ps, psr = PS.next()
                        fw.op(pe, lambda e: e.matmul(ps[:], lhsT=Wr[:, 0:128], rhs=xcb[:, cs], start=True, stop=True),
                              reads=[wrr, r_xcb], writes=[psr])
                        evac(rb[:, cs], ps[:], [psr, r_vec[l]], [r_rb], bias=vcol(l, "lru_br", ct), func=AF.Sigmoid)
                        ps, psr = PS.next()
                        fw.op(pe, lambda e: e.matmul(ps[:], lhsT=Wi[:, 0:128], rhs=xcb[:, cs], start=True, stop=True),
                              reads=[wir, r_xcb], writes=[psr])
                        evac(ib[:, cs], ps[:], [psr, r_vec[l]], [r_ib], bias=vcol(l, "lru_bi", ct), func=AF.Sigmoid)
                    fw.op(act, lambda e: e.activation(out=rb[:], in_=rb[:], func=AF.Exp, scale=cc[:, ct:ct + 1]),
                          reads=[r_rb, r_cc], writes=[r_rb])
                    fw.op(act, lambda e: e.activation(out=tmp[:], in_=rb[:], func=AF.Square), reads=[r_rb], writes=[r_tmp])
                    fw.op(act, lambda e: e.activation(out=tmp[:], in_=tmp[:], func=AF.Sqrt, scale=-1.0, bias=oneT[:, 0:1]),
                          reads=[r_tmp, r_const], writes=[r_tmp])
                    fw.op(dve, lambda e: e.tensor_tensor(out=ib[:], in0=ib[:], in1=xc[:], op=ALU.mult), reads=[r_ib, r_xc], writes=[r_ib])
                    fw.op(dve, lambda e: e.tensor_tensor(out=ib[:], in0=ib[:], in1=tmp[:], op=ALU.mult), reads=[r_ib, r_tmp], writes=[r_ib])
                    fw.op(dve, lambda e: e.tensor_tensor_scan(out=xc[:], data0=rb[:], data1=ib[:], initial=0.0,
                                                              op0=ALU.mult, op1=ALU.add), reads=[r_rb, r_ib], writes=[r_xc])
                    fw.op(act, lambda e: e.activation(out=gl[:], in_=gl[:], func=AF.Gelu_apprx_tanh), reads=[r_gl], writes=[r_gl])
                    fw.op(dve, lambda e: e.tensor_tensor(out=ybf[:], in0=xc[:], in1=gl[:], op=ALU.mult), reads=[r_xc, r_gl], writes=[r_y])
                    fw.dma(sp, rows_view(mixT, 8 + ct), ybf[:].rearrange("p (j n) -> p j n", n=TT), reads=[r_y], writes=[r_mixT])

            fw.barrier()
            with contextlib.ExitStack() as st:
                vb = sbt(st, [128, L], F32); r_vb = Reg()
                gb = sbt(st, [128, L], F32); r_gb = Reg()
                hpad = sbt(st, [128, 32 + L], BF16); r_hp = Reg()
                diag = sbt(st, [128, 31, 128], BF16); r_dg = Reg()
                hc = sbt(st, [128, 4, L], F32); r_hc = [[Reg() for _ in range(NP)] for _ in range(4)]
                sqf = [sbt(st, [128, TT], F32) for _ in range(2)]; r_sqf = [Reg(), Reg()]
                mu = sbt(st, [128, TT], F32); r_mu = Reg()
                var = sbt(st, [128, TT], F32); r_var = Reg()
                rstd = sbt(st, [128, TT], F32); r_rstd = Reg()
                tn = [sbt(st, [128, TT], F32) for _ in range(2)]; r_tn = [Reg(), Reg()]
                sbf = sbt(st, [128, 4, TT], BF16); r_sbf = [Reg() for _ in range(4)]
                stg = Rot([(sbt(st, [128, TT], BF16), Reg()) for _ in range(3)])
                fw.op(dve, lambda e: e.memset(hpad[:, 0:32], 0.0), writes=[r_hp])
                pbufs = [sbt(st, [128, 16 + L], F32) for _ in range(3)]
                pbregs = [Reg() for _ in range(3)]
                pt16 = sbt(st, [128, 16], F32)
                r_pt16 = Reg()
                for b in range(3):
                    fw.op(dve, lambda e: e.memset(pbufs[b][:, 0:16], 0.0), writes=[pbregs[b]])
                def pool_gi(gi):
                    win = 2 ** (gi + 1)
                    row0 = 2048 + gi * 128
                    xp, xpr = pbufs[0], pbregs[0]
                    fw.dma(sp, xp[:, 16:], projR[row0:row0 + 128, :], reads=[r_projR], writes=[xpr])
                    cur, curr = xp, xpr
                    m = 1
                    k = 0
                    while m < win:
                        nx, nxr = pbufs[1 + (k % 2)], pbregs[1 + (k % 2)]
                        fw.op(dve, lambda e: e.tensor_tensor(out=nx[:, 16:], in0=cur[:, 16:], in1=cur[:, 16 - m:16 - m + L],
                                                             op=ALU.add), reads=[curr], writes=[nxr])
                        cur, curr = nx, nxr
                        m *= 2
                        k += 1
                    ob, r_pd = pbufs[1 + (k % 2)], pbregs[1 + (k % 2)]
                    pdfull = ob[:].bitcast(BF16)[:, 64:64 + L]
                    fw.op(dve, lambda e: e.scalar_tensor_tensor(out=pdfull, in0=cur[:, 16:], scalar=1.0 / win, in1=xp[:, 16:],
                                                                op0=ALU.mult, op1=ALU.subtract),
                          reads=[curr, xpr], writes=[r_pd])
                    fw.op(dve, lambda e: e.tensor_tensor(out=pt16[:], in0=cur[:, 16:32], in1=vcol(l, "invc", gi * 16, 16), op=ALU.mult),
                          reads=[curr, r_vec[l]], writes=[r_pt16])
                    fw.op(dve, lambda e: e.tensor_tensor(out=pdfull[:, 0:16], in0=pt16[:], in1=xp[:, 16:32], op=ALU.subtract),
                          reads=[r_pt16, xpr], writes=[r_pd])
                    W, wr = load_w(w_pool[l, gi], ncol=128)
                    for p in range(NP):
                        cs = slice(p * TT, (p + 1) * TT)
                        ps, psr = PS.next()
                        fw.op(pe, lambda e: e.matmul(ps[:], lhsT=W[:, 0:128], rhs=pdfull[:, cs], start=True, stop=True),
                              reads=[wr, r_pd], writes=[psr])
                        sg, sr = stg.next()
                        evac(sg[:], ps[:], [psr, r_vec[l]], [sr], scale=vcol(l, "pool_scale", gi))
                        fw.dma(sp, chunk_view(mixT, p, 12 + gi), sg[:], reads=[sr], writes=[r_mixT])


                for ct in range(4):
                    rv = ct * 128
                    rg = 512 + ct * 128
                    fw.dma(sp, vb[:], projR[rv:rv + 128, :], reads=[r_projR], writes=[r_vb])
                    fw.dma(sp, gb[:], projR[rg:rg + 128, :], reads=[r_projR], writes=[r_gb])
                    fw.op(act, lambda e: e.activation(out=gb[:], in_=gb[:], func=AF.Sigmoid), reads=[r_gb], writes=[r_gb])
                    fw.op(dve, lambda e: e.tensor_tensor(out=hpad[:, 32:], in0=vb[:], in1=gb[:], op=ALU.mult),
                          reads=[r_vb, r_gb], writes=[r_hp])
                    for k in range(31):
                        fw.op(dve, lambda e: e.tensor_scalar(out=diag[:, k, :], in0=ident_f[:], scalar1=vcol(l, "cv_wdw", ct * 31 + k),
                                                             scalar2=None, op0=ALU.mult), reads=[r_ident, r_vec[l]], writes=[r_dg])
                    for p in range(NP):
                        ps, psr = PS.next()
                        for k in range(31):
                            o = 2 + p * TT + k
                            fw.op(pe, lambda e: e.matmul(ps[:], lhsT=diag[:, k, :], rhs=hpad[:, o:o + TT], start=(k == 0), stop=(k == 30)),
                                  reads=[r_dg, r_hp], writes=[psr], inc=(k == 30))
                        evac(hc[:, ct, p * TT:(p + 1) * TT], ps[:], [psr, r_vec[l]], [r_hc[ct][p]], bias=vcol(l, "cv_bdw", ct))
                    pool_gi(ct)
                Wp = [load_w(w_pw[l, mt], ncol=512) for mt in range(4)]
                for p in range(NP):
                    cs = slice(p * TT, (p + 1) * TT)
                    ps1, ps1r = PS.next()
                    ps2, ps2r = PS.next()
                    for ct in range(4):
                        b = ct % 2
                        fw.op(pe, lambda e: e.matmul(ps1[:], lhsT=ones_f[:], rhs=hc[:, ct, cs], start=(ct == 0), stop=(ct == 3)),
                              reads=[r_hc[ct][p], r_const], writes=[ps1r], inc=(ct == 3))
                        fw.op(act, lambda e: e.activation(out=sqf[b][:], in_=hc[:, ct, cs], func=AF.Square),
                              reads=[r_hc[ct][p]], writes=[r_sqf[b]])
                        fw.op(pe, lambda e: e.matmul(ps2[:], lhsT=ones_f[:], rhs=sqf[b][:], start=(ct == 0), stop=(ct == 3)),
                              reads=[r_sqf[b], r_const], writes=[ps2r], inc=True)
                    fw.op(act, lambda e: e.mul(out=mu[:], in_=ps1[:], mul=1.0 / 512), reads=[ps1r], writes=[r_mu])
                    fw.op(dve, lambda e: e.tensor_tensor(out=var[:], in0=mu[:], in1=mu[:], op=ALU.mult), reads=[r_mu], writes=[r_var])
                    fw.op(dve, lambda e: e.scalar_tensor_tensor(out=var[:], in0=ps2[:], scalar=1.0 / 512, in1=var[:],
                                                                op0=ALU.mult, op1=ALU.subtract), reads=[ps2r, r_var], writes=[r_var])
                    fw.op(act, lambda e: e.activation(out=var[:], in_=var[:], func=AF.Ln, bias=epsT[:, 0:1]),
                          reads=[r_var, r_const], writes=[r_var])
                    fw.op(act, lambda e: e.activation(out=rstd[:], in_=var[:], func=AF.Exp, scale=-0.5), reads=[r_var], writes=[r_rstd])
                    for ct in range(4):
                        b = ct % 2
                        fw.op(dve, lambda e: e.tensor_tensor(out=tn[b][:], in0=hc[:, ct, cs], in1=mu[:], op=ALU.subtract),
                              reads=[r_hc[ct][p], r_mu], writes=[r_tn[b]])
                        fw.op(dve, lambda e: e.tensor_tensor(out=tn[b][:], in0=tn[b][:], in1=rstd[:], op=ALU.mult),
                              reads=[r_tn[b], r_rstd], writes=[r_tn[b]])
                        fw.op(act, lambda e: e.activation(out=sbf[:, ct, :], in_=tn[b][:], func=AF.Silu,
                                                          scale=vcol(l, "cv_lng", ct), bias=vcol(l, "cv_lnb", ct)),
                              reads=[r_tn[b], r_vec[l]], writes=[r_sbf[ct]])
                    for mt in range(4):
                        W, wr = Wp[mt]
                        ps, psr = PS.next()
                        for kc in range(4):
                            fw.op(pe, lambda e: e.matmul(ps[:], lhsT=W[:, kc * 128:(kc + 1) * 128], rhs=sbf[:, kc, :],
                                                         start=(kc == 0), stop=(kc == 3)),
                                  reads=[wr, r_sbf[kc]], writes=[psr], inc=(kc == 3))
                        sg, sr = stg.next()
                        evac(sg[:], ps[:], [psr, r_vec[l]], [sr], bias=vcol(l, "cv_bpw", mt))
                        fw.dma(sp, chunk_view(mixT, p, 4 + mt), sg[:], reads=[sr], writes=[r_mixT])

            fw.barrier()
            with contextlib.ExitStack() as st:
                LBr = sbt(st, [128, 2048], BF16); LBi = sbt(st, [128, 2048], BF16); r_LB = Reg()
                LCr = sbt(st, [128, 2048], BF16); nLCr = sbt(st, [128, 2048], BF16); LCi = sbt(st, [128, 2048], BF16); r_LC = Reg()
                ppc = sbt(st, [128, 16, 12], F32); pps = sbt(st, [128, 16, 12], F32); r_pp = Reg()
                rho_pp = sbt(st, [128, 16], F32)

                def lam_bar(st2, lre, lim, lst, n, r_in):
                    mk = lambda: sbt(st2, [128, n], F32)
                    stp, a, rho, th, c, s, t1, t2 = mk(), mk(), mk(), mk(), mk(), mk(), mk(), mk()
                    rr = Reg()
                    fw.op(act, lambda e: e.activation(out=stp[:], in_=lst, func=AF.Exp), reads=[r_in], writes=[rr])
                    fw.op(dve, lambda e: e.tensor_tensor(out=a[:], in0=lre, in1=stp[:], op=ALU.mult), reads=[r_in, rr], writes=[rr])
                    fw.op(act, lambda e: e.activation(out=rho[:], in_=a[:], func=AF.Exp), reads=[rr], writes=[rr])
                    fw.op(dve, lambda e: e.tensor_tensor(out=th[:], in0=lim, in1=stp[:], op=ALU.mult), reads=[r_in, rr], writes=[rr])
                    fw.op(act, lambda e: e.activation(out=s[:], in_=th[:], func=AF.Sin, scale=1.0 / 16), reads=[rr], writes=[rr])
                    fw.op(act, lambda e: e.activation(out=c[:], in_=th[:], func=AF.Sin, scale=1.0 / 16, bias=hpiT[:, 0:1]),
                          reads=[rr, r_const], writes=[rr])
                    for _ in range(4):
                        fw.op(dve, lambda e: e.tensor_tensor(out=t1[:], in0=c[:], in1=c[:], op=ALU.mult), reads=[rr], writes=[rr])
                        fw.op(dve, lambda e: e.tensor_tensor(out=t2[:], in0=s[:], in1=s[:], op=ALU.mult), reads=[rr], writes=[rr])
                        fw.op(dve, lambda e: e.scalar_tensor_tensor(out=s[:], in0=c[:], scalar=2.0, in1=s[:], op0=ALU.mult, op1=ALU.mult),
                              reads=[rr], writes=[rr])
                        fw.op(dve, lambda e: e.tensor_tensor(out=c[:], in0=t1[:], in1=t2[:], op=ALU.subtract), reads=[rr], writes=[rr])
                    return rho, c, s, rr, (t1, t2, a, th)

                with contextlib.ExitStack() as st2:
                    rowt = sbt(st2, [128, 3, 2048], F32); r_row = Reg()
                    fw.dma(sp, rowt[:], s5_row[l].partition_broadcast(128), writes=[r_row])
                    rho, c, s, rr, (t1, t2, t3, t4) = lam_bar(st2, rowt[:, 0, :], rowt[:, 1, :], rowt[:, 2, :], 2048, r_row)
                    lre, lim = rowt[:, 0, :], rowt[:, 1, :]
                    lbr = sbt(st2, [128, 2048], F32); lbi = sbt(st2, [128, 2048], F32)
                    qr = sbt(st2, [128, 2048], F32); qi = sbt(st2, [128, 2048], F32)
                    T = lambda o, a, b, op, rd=(): fw.op(dve, lambda e: e.tensor_tensor(out=o, in0=a, in1=b, op=op),
                                                         reads=[rr, r_row] + list(rd), writes=[rr])
                    T(lbr[:], rho[:], c[:], ALU.mult)
                    T(lbi[:], rho[:], s[:], ALU.mult)
                    fw.op(dve, lambda e: e.tensor_scalar(out=lbr[:], in0=lbr[:], scalar1=-1.0, scalar2=None, op0=ALU.add),
                          reads=[rr], writes=[rr])
                    T(t1[:], lre, lre, ALU.mult)
                    T(t2[:], lim, lim, ALU.mult)
                    T(t1[:], t1[:], t2[:], ALU.add)
                    fw.op(dve, lambda e: e.reciprocal(out=t1[:], in_=t1[:]), reads=[rr], writes=[rr])
                    T(t2[:], lbr[:], lre, ALU.mult)
                    T(t3[:], lbi[:], lim, ALU.mult)
                    T(t2[:], t2[:], t3[:], ALU.add)
                    T(qr[:], t2[:], t1[:], ALU.mult)
                    T(t2[:], lbi[:], lre, ALU.mult)
                    T(t3[:], lbr[:], lim, ALU.mult)
                    T(t2[:], t2[:], t3[:], ALU.subtract)
                    T(qi[:], t2[:], t1[:], ALU.mult)
                    bz = sbt(st2, [128, 2, 2048], F32); r_bz = Reg()
                    fw.dma(sp, bz[:], s5_bz[l].rearrange("k p n -> p k n"), writes=[r_bz])
                    T(t1[:], qr[:], bz[:, 0, :], ALU.mult, [r_bz])
                    T(t2[:], qi[:], bz[:, 1, :], ALU.mult, [r_bz])
                    fw.op(dve, lambda e: e.tensor_tensor(out=LBr[:], in0=t1[:], in1=t2[:], op=ALU.subtract), reads=[rr], writes=[rr, r_LB])
                    T(t1[:], qr[:], bz[:, 1, :], ALU.mult, [r_bz])
                    T(t2[:], qi[:], bz[:, 0, :], ALU.mult, [r_bz])
                    fw.op(dve, lambda e: e.tensor_tensor(out=LBi[:], in0=t1[:], in1=t2[:], op=ALU.add), reads=[rr], writes=[rr, r_LB])
                    fw.dma(sp, bz[:], s5_cz[l].rearrange("k p n -> p k n"), reads=[], writes=[r_bz, rr])
                    fw.op(act, lambda e: e.copy(out=LCr[:], in_=bz[:, 0, :]), reads=[r_bz], writes=[r_LC])
                    fw.op(act, lambda e: e.mul(out=LCi[:], in_=bz[:, 1, :], mul=-1.0), reads=[r_bz], writes=[r_LC])
                    fw.op(act, lambda e: e.mul(out=nLCr[:], in_=bz[:, 0, :], mul=-1.0), reads=[r_bz], writes=[r_LC])
                    ppt = sbt(st2, [128, 48], F32); r_ppt = Reg()
                    fw.dma(sp, ppt[:], s5_pp[l], writes=[r_ppt])
                    rho2, c2, s2, rr2, (u1, u2, _, _) = lam_bar(st2, ppt[:, 0:16], ppt[:, 16:32], ppt[:, 32:48], 16, r_ppt)
                    fw.op(dve, lambda e: e.tensor_copy(out=rho_pp[:], in_=rho2[:]), reads=[rr2], writes=[r_pp])
                    fw.op(dve, lambda e: e.tensor_copy(out=ppc[:, :, 0], in_=c2[:]), reads=[rr2], writes=[r_pp])
                    fw.op(dve, lambda e: e.tensor_copy(out=pps[:, :, 0], in_=s2[:]), reads=[rr2], writes=[r_pp])
                    for k in range(11):
                        fw.op(dve, lambda e: e.tensor_tensor(out=u1[:], in0=ppc[:, :, k], in1=ppc[:, :, k], op=ALU.mult), reads=[r_pp, rr2], writes=[rr2])
                        fw.op(dve, lambda e: e.tensor_tensor(out=u2[:], in0=pps[:, :, k], in1=pps[:, :, k], op=ALU.mult), reads=[r_pp, rr2], writes=[rr2])
                        fw.op(dve, lambda e: e.tensor_tensor(out=ppc[:, :, k + 1], in0=u1[:], in1=u2[:], op=ALU.subtract), reads=[rr2], writes=[r_pp])
                        fw.op(dve, lambda e: e.scalar_tensor_tensor(out=pps[:, :, k + 1], in0=ppc[:, :, k], scalar=2.0, in1=pps[:, :, k],
                                                                    op0=ALU.mult, op1=ALU.mult), reads=[r_pp], writes=[r_pp])

                fw.barrier()
                tcb = sbt(st, [128, L], F32); tsb = sbt(st, [128, L], F32); r_tab = Reg()
                wre = sbt(st, [128, L], F32); wim = sbt(st, [128, L], F32)
                r_wre = [Reg() for _ in range(NP)]; r_wim = [Reg() for _ in range(NP)]
                tq = [sbt(st, [128, TT], F32) for _ in range(8)]; r_tq = [Reg() for _ in range(8)]
                vq = [sbt(st, [128, TT], BF16) for _ in range(8)]; r_vq = [Reg() for _ in range(8)]
                ub = sbt(st, [128, L], BF16); r_ub = Reg()
                yacc = sbt(st, [128, L], F32); r_ya = [Reg() for _ in range(NP)]
                G = sbt(st, [128, 4, L], BF16); r_G = [[Reg() for _ in range(NP)] for _ in range(4)]
                ytmp = [sbt(st, [128, TT], F32) for _ in range(2)]; r_yt = [Reg(), Reg()]
                stg = Rot([(sbt(st, [128, TT], BF16), Reg()) for _ in range(3)])
                for pt in range(4):
                    fw.dma(sp, ub[:], projU[pt * 128:(pt + 1) * 128, :], reads=[r_projU], writes=[r_ub])
                    for qq in range(4):
                        q = pt * 4 + qq
                        qs = slice(q * 128, (q + 1) * 128)
                        fw.op(dve, lambda e: e.memset(tcb[:, 0:1], 1.0), writes=[r_tab])
                        fw.op(dve, lambda e: e.memset(tsb[:, 0:1], 0.0), writes=[r_tab])
                        n = 1
                        k = 0
                        while n < L:
                            cr = ppc[:, q, k:k + 1]
                            ci = pps[:, q, k:k + 1]
                            tmpa, tmpb = wre, wim
                            if n >= 64:
                                fw.op(act, lambda e: e.activation(out=tmpa[:, 0:n], in_=tsb[:, 0:n], func=AF.Copy, scale=ci),
                                      reads=[r_tab, r_pp], writes=r_wre)
                                fw.op(act, lambda e: e.activation(out=tmpb[:, 0:n], in_=tsb[:, 0:n], func=AF.Copy, scale=cr),
                                      reads=[r_tab, r_pp], writes=r_wim)
                            else:
                                fw.op(dve, lambda e: e.tensor_scalar(out=tmpa[:, 0:n], in0=tsb[:, 0:n], scalar1=ci, scalar2=None, op0=ALU.mult),
                                      reads=[r_tab, r_pp], writes=r_wre)
                                fw.op(dve, lambda e: e.tensor_scalar(out=tmpb[:, 0:n], in0=tsb[:, 0:n], scalar1=cr, scalar2=None, op0=ALU.mult),
                                      reads=[r_tab, r_pp], writes=r_wim)
                            fw.op(dve, lambda e: e.scalar_tensor_tensor(out=tsb[:, n:2 * n], in0=tcb[:, 0:n], scalar=ci, in1=tmpb[:, 0:n],
                                                                        op0=ALU.mult, op1=ALU.add), reads=[r_tab, r_pp] + r_wre + r_wim, writes=[r_tab])
                            fw.op(dve, lambda e: e.scalar_tensor_tensor(out=tcb[:, n:2 * n], in0=tcb[:, 0:n], scalar=cr, in1=tmpa[:, 0:n],
                                                                        op0=ALU.mult, op1=ALU.subtract), reads=[r_tab, r_pp] + r_wre + r_wim, writes=[r_tab])
                            n *= 2
                            k += 1
                        for p in range(NP):
                            cs = slice(p * TT, (p + 1) * TT)
                            psr_, psrr = PS.next()
                            psi_, psir = PS.next()
                            fw.op(pe, lambda e: e.matmul(psr_[:], lhsT=LBr[:, qs], rhs=ub[:, cs], start=True, stop=True),
                                  reads=[r_LB, r_ub], writes=[psrr])
                            fw.op(pe, lambda e: e.matmul(psi_[:], lhsT=LBi[:, qs], rhs=ub[:, cs], start=True, stop=True),
                                  reads=[r_LB, r_ub], writes=[psir])
                            TT_ = lambda o, orr, a, ar, b, op: fw.op(dve, lambda e: e.tensor_tensor(out=o, in0=a, in1=b, op=op),
                                                                     reads=[ar, r_tab], writes=[orr])
                            tb = (p % 2) * 4
                            TT_(tq[tb][:], r_tq[tb], psr_[:], psrr, tcb[:, cs], ALU.mult)
                            TT_(tq[tb + 1][:], r_tq[tb + 1], psi_[:], psir, tsb[:, cs], ALU.mult)
                            TT_(tq[tb + 2][:], r_tq[tb + 2], psi_[:], psir, tcb[:, cs], ALU.mult)
                            TT_(tq[tb + 3][:], r_tq[tb + 3], psr_[:], psrr, tsb[:, cs], ALU.mult)
                            fw.op(pool, lambda e: e.tensor_tensor(out=wre[:, cs], in0=tq[tb][:], in1=tq[tb + 1][:], op=ALU.add),
                                  reads=[r_tq[tb], r_tq[tb + 1]], writes=[r_wre[p]])
                            fw.op(pool, lambda e: e.tensor_tensor(out=wim[:, cs], in0=tq[tb + 2][:], in1=tq[tb + 3][:], op=ALU.subtract),
                                  reads=[r_tq[tb + 2], r_tq[tb + 3]], writes=[r_wim[p]])
                        rho_b = rho_pp[:, q:q + 1].to_broadcast([128, L])
                        fw.op(dve, lambda e: e.tensor_tensor_scan(out=wre[:], data0=rho_b, data1=wre[:], initial=0.0, op0=ALU.mult, op1=ALU.add),
                              reads=r_wre + [r_pp], writes=r_wre)
                        fw.op(dve, lambda e: e.tensor_tensor_scan(out=wim[:], data0=rho_b, data1=wim[:], initial=0.0, op0=ALU.mult, op1=ALU.add),
                              reads=r_wim + [r_pp], writes=r_wim)
                        for p in range(NP):
                            cs = slice(p * TT, (p + 1) * TT)
                            b4 = (p % 2) * 4
                            v1, v2, v3, v4 = vq[b4], vq[b4 + 1], vq[b4 + 2], vq[b4 + 3]
                            rv = r_vq[b4:b4 + 4]
                            fw.op(dve, lambda e: e.tensor_tensor(out=v1[:], in0=wre[:, cs], in1=tcb[:, cs], op=ALU.mult), reads=[r_wre[p], r_tab], writes=[rv[0]])
                            fw.op(dve, lambda e: e.tensor_tensor(out=v2[:], in0=wre[:, cs], in1=tsb[:, cs], op=ALU.mult), reads=[r_wre[p], r_tab], writes=[rv[1]])
                            fw.op(dve, lambda e: e.tensor_tensor(out=v3[:], in0=wim[:, cs], in1=tcb[:, cs], op=ALU.mult), reads=[r_wim[p], r_tab], writes=[rv[2]])
                            fw.op(pool, lambda e: e.tensor_tensor(out=v4[:], in0=wim[:, cs], in1=tsb[:, cs], op=ALU.mult), reads=[r_wim[p], r_tab], writes=[rv[3]])
                            ps, psr = PS.next()
                            fw.op(pe, lambda e: e.matmul(ps[:], lhsT=LCr[:, qs], rhs=v1[:], start=True, stop=False), reads=[r_LC, rv[0]], writes=[psr], inc=False)
                            fw.op(pe, lambda e: e.matmul(ps[:], lhsT=nLCr[:, qs], rhs=v4[:], start=False, stop=False), reads=[r_LC, rv[3]], writes=[psr], inc=False)
                            fw.op(pe, lambda e: e.matmul(ps[:], lhsT=LCi[:, qs], rhs=v2[:], start=False, stop=False), reads=[r_LC, rv[1]], writes=[psr], inc=False)
                            fw.op(pe, lambda e: e.matmul(ps[:], lhsT=LCi[:, qs], rhs=v3[:], start=False, stop=True), reads=[r_LC, rv[2]], writes=[psr], inc=True)
                            if qq == 0:
                                evac(yacc[:, cs], ps[:], [psr], [r_ya[p]], eng=act)
                            else:
                                fw.op(dve, lambda e: e.tensor_tensor(out=yacc[:, cs], in0=yacc[:, cs], in1=ps[:], op=ALU.add),
                                      reads=[psr, r_ya[p]], writes=[r_ya[p]])
                    for p in range(NP):
                        cs = slice(p * TT, (p + 1) * TT)
                        b = p % 2
                        fw.op(dve, lambda e: e.scalar_tensor_tensor(out=ytmp[b][:], in0=ub[:, cs], scalar=vcol(l, "s5_d", pt), in1=yacc[:, cs],
                                                                    op0=ALU.mult, op1=ALU.add), reads=[r_ub, r_ya[p], r_vec[l]], writes=[r_yt[b]])
                        fw.op(act, lambda e: e.activation(out=G[:, pt, cs], in_=ytmp[b][:], func=AF.Gelu_apprx_tanh),
                              reads=[r_yt[b]], writes=[r_G[pt][p]])
                Wg = [load_w(w_glu[l, mt], ncol=512) for mt in range(4)]
                for p in range(NP):
                    cs = slice(p * TT, (p + 1) * TT)
                    for mt in range(4):
                        W, wr = Wg[mt]
                        ps, psr = PS.next()
                        for kc in range(4):
                            fw.op(pe, lambda e: e.matmul(ps[:], lhsT=W[:, kc * 128:(kc + 1) * 128], rhs=G[:, kc, cs], start=(kc == 0), stop=(kc == 3)),
                                  reads=[wr, r_G[kc][p]], writes=[psr], inc=(kc == 3))
                        b = mt % 2
                        evac(ytmp[b][:], ps[:], [psr, r_vec[l]], [r_yt[b]], bias=vcol(l, "s5_bglu", mt), func=AF.Sigmoid)
                        sg, sr = stg.next()
                        fw.op(dve, lambda e: e.tensor_tensor(out=sg[:], in0=ytmp[b][:], in1=G[:, mt, cs], op=ALU.mult),
                              reads=[r_yt[b], r_G[mt][p]], writes=[sr])
                        fw.dma(sp, chunk_view(mixT, p, mt), sg[:], reads=[sr], writes=[r_mixT])

            fw.barrier()
            with contextlib.ExitStack() as st:
                nb = norm_bufs(st)
                nxb = 2 if (l == depth - 1 and split) else 1
                xts = [(sbt(st, [128, 16, TT], F32), [Reg() for _ in range(16)]) for _ in range(nxb)]
                mts = [(sbt(st, [128, 16, TT], BF16), Reg()) for _ in range(1)]
                ht, hr = sbt(st, [128, 16, TT], BF16), [Reg() for _ in range(16)]
                ostg = Rot([(sbt(st, [128, TT], F32), Reg()) for _ in range(2)])
                actb = sbt(st, [128, NFT, TT], BF16); r_act = [Reg() for _ in range(NFT)]
                gbuf = [sbt(st, [128, 2 + TT], F32) for _ in range(2)]; r_gb = [Reg(), Reg()]
                cb = [sbt(st, [128, TT], F32) for _ in range(2)]; r_cb = [Reg(), Reg()]
                halo = sbt(st, [128, NFT, 2], F32); r_halo = [Reg() for _ in range(NFT)]
                wbig = Rot([(sbt(st, [128, NFT * 128], BF16), Reg()) for _ in range(2)])
                fw.op(dve, lambda e: e.memset(halo[:], 0.0), writes=r_halo)
                last = (l == depth - 1)
                gat = last and split
                if gat:
                    assert l > 0
                    hx = sbt(st, [128, 16, 2], F32); r_hx = [Reg() for _ in range(16)]
                    hm = sbt(st, [128, 16, 2], BF16); r_hm = Reg()
                    hh = sbt(st, [128, 16, 2], BF16); r_hh = [Reg() for _ in range(16)]
                for j in range(NTL if gat else NT):
                    cs = slice(j * TT, (j + 1) * TT)
                    xt, xr = xts[j % nxb]
                    mtile, mr = mts[0]
                    dohalo = gat and j == 0
                    if gat:
                        if j == 0:
                            fw.gather(xt[:].rearrange("p c n -> p (c n)"), xs, gidx[:, j:j + 1], reads=[r_xs, r_gidx], writes=xr)
                            fw.gather(mtile[:].rearrange("p c n -> p (c n)"), mixT, gidx[:, j:j + 1], reads=[r_mixT, r_gidx], writes=[mr])
                    else:
                        if l == 0:
                            fw.dma(sp, xt[:], xview(xT)[:, :, cs], writes=xr)
                        else:
                            fw.dma(sp, xt[:], tile_view(xs, j), reads=[r_xs], writes=xr)
                        fw.dma(sp, mtile[:], tile_view(mixT, j), reads=[r_mixT], writes=[mr])
                    if dohalo:
                        fw.dma(sp, hx[:], tile_view(xs, NTL - 1)[:, :, TT - 2:TT], reads=[r_xs], writes=r_hx)
                        fw.dma(sp, hm[:], tile_view(mixT, NTL - 1)[:, :, TT - 2:TT], reads=[r_mixT], writes=[r_hm])
                    for mt in range(16):
                        W, wr = load_w(w_out[l, mt])
                        ps, psr = PS.next()
                        for kc in range(16):
                            fw.op(pe, lambda e: e.matmul(ps[:], lhsT=W[:, kc * 128:(kc + 1) * 128], rhs=mtile[:, kc, :],
                                                         start=(kc == 0), stop=(kc == 15)), reads=[wr, mr], writes=[psr], inc=(kc == 15))
                        fw.op(dve, lambda e: e.tensor_tensor(out=xt[:, mt, :], in0=xt[:, mt, :], in1=ps[:], op=ALU.add),
                              reads=[psr, xr[mt]], writes=[xr[mt]])
                        if dohalo:
                            psh, pshr = PS.next()
                            for kc in range(16):
                                fw.op(pe, lambda e: e.matmul(psh[:, 0:2], lhsT=W[:, kc * 128:(kc + 1) * 128], rhs=hm[:, kc, :],
                                                             start=(kc == 0), stop=(kc == 15)), reads=[wr, r_hm], writes=[pshr], inc=(kc == 15))
                            fw.op(dve, lambda e: e.tensor_tensor(out=hx[:, mt, :], in0=hx[:, mt, :], in1=psh[:, 0:2], op=ALU.add),
                                  reads=[pshr, r_hx[mt]], writes=[r_hx[mt]])
                            fw.op(dve, lambda e: e.tensor_scalar(out=hx[:, mt, :], in0=hx[:, mt, :], scalar1=flagT[:, 0:1], scalar2=None, op0=ALU.mult),
                                  reads=[r_hx[mt], r_gidx], writes=[r_hx[mt]])
                    if gat and j + 1 < NTL:
                        xt2, xr2 = xts[(j + 1) % nxb]
                        fw.gather(xt2[:].rearrange("p c n -> p (c n)"), xs, gidx[:, j + 1:j + 2], reads=[r_xs, r_gidx], writes=xr2)
                    rmsnorm(nb, xt, xr, vcol(l, "g_ffn", 0, 16), ht, hr)
                    if dohalo:
                        rmsnorm(nb, hx, r_hx, vcol(l, "g_ffn", 0, 16), hh, r_hh, w=2)
                    for ft in range(NFT):
                        Wg_, wgr = load_w(w_up[l, ft])
                        Wv_, wvr = load_w(w_up[l, NFT + ft])
                        psg, psgr = PS.next()
                        psv, psvr = PS.next()
                        for kc in range(16):
                            fw.op(pe, lambda e: e.matmul(psg[:], lhsT=Wg_[:, kc * 128:(kc + 1) * 128], rhs=ht[:, kc, :],
                                                         start=(kc == 0), stop=(kc == 15)), reads=[wgr, hr[kc]], writes=[psgr], inc=(kc == 15))
                        for kc in range(16):
                            fw.op(pe, lambda e: e.matmul(psv[:], lhsT=Wv_[:, kc * 128:(kc + 1) * 128], rhs=ht[:, kc, :],
                                                         start=(kc == 0), stop=(kc == 15)), reads=[wvr, hr[kc]], writes=[psvr], inc=(kc == 15))
                        b = ft % 2
                        gbt, gbr = gbuf[b], r_gb[b]
                        fw.op(act, lambda e: e.copy(out=gbt[:, 2:], in_=psg[:]), reads=[psgr], writes=[gbr])
                        if dohalo:
                            psh, pshr = PS.next()
                            for kc in range(16):
                                fw.op(pe, lambda e: e.matmul(psh[:, 0:2], lhsT=Wg_[:, kc * 128:(kc + 1) * 128], rhs=hh[:, kc, :],
                                                             start=(kc == 0), stop=(kc == 15)), reads=[wgr, r_hh[kc]], writes=[pshr], inc=(kc == 15))
                            fw.op(dve, lambda e: e.tensor_copy(out=gbt[:, 0:2], in_=psh[:, 0:2]), reads=[pshr], writes=[gbr])
                        else:
                            fw.op(dve, lambda e: e.tensor_copy(out=gbt[:, 0:2], in_=halo[:, ft, :]), reads=[r_halo[ft]], writes=[gbr])
                        fw.op(dve, lambda e: e.tensor_copy(out=halo[:, ft, :], in_=gbt[:, TT:TT + 2]), reads=[gbr], writes=[r_halo[ft]])
                        wd = lambda k: vcol(l, "ffn_wdw", ft * 3 + k)
                        fw.op(dve, lambda e: e.tensor_scalar(out=cb[b][:], in0=gbt[:, 0:TT], scalar1=wd(0), scalar2=vcol(l, "ffn_bdw", ft),
                                                             op0=ALU.mult, op1=ALU.add), reads=[gbr, r_vec[l]], writes=[r_cb[b]])
                        for k in (1, 2):
                            fw.op(dve, lambda e: e.scalar_tensor_tensor(out=cb[b][:], in0=gbt[:, k:k + TT], scalar=wd(k), in1=cb[b][:],
                                                                        op0=ALU.mult, op1=ALU.add), reads=[gbr, r_cb[b], r_vec[l]], writes=[r_cb[b]])
                        fw.op(act, lambda e: e.activation(out=cb[b][:], in_=cb[b][:], func=AF.Gelu_apprx_tanh), reads=[r_cb[b]], writes=[r_cb[b]])
                        fw.op(dve, lambda e: e.tensor_tensor(out=actb[:, ft, :], in0=cb[b][:], in1=psv[:], op=ALU.mult),
                              reads=[r_cb[b], psvr], writes=[r_act[ft]])
                    if gat and j + 1 < NTL:
                        fw.gather(mtile[:].rearrange("p c n -> p (c n)"), mixT, gidx[:, j + 1:j + 2], reads=[r_mixT, r_gidx], writes=[mr])
                    for mt in range(16):
                        W, wr = load_w(w_down[l, mt], rot=wbig, ncol=NFT * 128)
                        ps, psr = PS.next()
                        for kc in range(NFT):
                            fw.op(pe, lambda e: e.matmul(ps[:], lhsT=W[:, kc * 128:(kc + 1) * 128], rhs=actb[:, kc, :],
                                                         start=(kc == 0), stop=(kc == NFT - 1)), reads=[wr, r_act[kc]], writes=[psr], inc=(kc == NFT - 1))
                        fw.op(dve, lambda e: e.tensor_tensor(out=xt[:, mt, :], in0=xt[:, mt, :], in1=ps[:], op=ALU.add),
                              reads=[psr, xr[mt]], writes=[xr[mt]])
                    if last:
                        def _of(c):
                            t_, r_ = ostg.next()
                            return t_[:], r_

                        def _pf(c, oap, oreg):
                            fw.dma(sp, outT[c * 128:(c + 1) * 128, cs], oap, reads=[oreg], writes=[r_out])
                        rmsnorm(nb, xt, xr, vcol(l, "g_fin", 0, 16), None, None, out_fn=_of, post_fn=_pf)
                    else:
                        fw.dma(sp, tile_view(xs, j), xt[:], reads=xr, writes=[r_xs])
        fw.finish()
    return nc


def _cols(v, n):
    return np.ascontiguousarray(np.asarray(v, np.float32).reshape(n, 128).T)


def _wtile(w, nk, nm):
    w = np.asarray(w, np.float32).reshape(nk, 128, nm, 128)
    return np.ascontiguousarray(w.transpose(2, 1, 0, 3).reshape(nm, 128, nk * 128))


def prep_weights(inp, depth=DEPTH):
    f = lambda k: np.asarray(inp[k], np.float32)
    out = {}
    out["w_in"] = np.stack([_wtile(f("w_in")[l], 16, 24) for l in range(depth)])
    out["w_out"] = np.stack([_wtile(f("w_out")[l], 16, 16) for l in range(depth)])
    out["w_up"] = np.stack([_wtile(f("ffn_w_up")[l], 16, 86) for l in range(depth)])
    out["w_down"] = np.stack([_wtile(f("ffn_w_down")[l], NFT, 16) for l in range(depth)])
    out["w_glu"] = np.stack([_wtile(f("s5_w_glu")[l], 4, 4) for l in range(depth)])
    out["w_pw"] = np.stack([_wtile(f("cv_w_pw")[l], 4, 4) for l in range(depth)])

    def blk(w):
        o = np.zeros((4, 128, 128), np.float32)
        for h in range(8):
            ct, hh = divmod(h, 2)
            o[ct, hh * 64:(hh + 1) * 64, hh * 64:(hh + 1) * 64] = w[h]
        return o
    out["w_lr"] = np.stack([blk(f("lru_w_r")[l]) for l in range(depth)])
    out["w_li"] = np.stack([blk(f("lru_w_i")[l]) for l in range(depth)])
    out["w_pool"] = np.ascontiguousarray(f("pool_w")[:depth])
    vec = np.zeros((depth, 128, NV), np.float32)
    invc = np.zeros((4, 16), np.float32)
    for gi in range(4):
        invc[gi] = 1.0 / np.minimum(np.arange(16) + 1.0, 2.0 ** (gi + 1))
    for l in range(depth):
        def put(name, arr):
            arr = np.asarray(arr, np.float32)
            vec[l, :, VOFF[name]:VOFF[name] + arr.shape[1]] = arr
        put("g_mix", _cols(f("norm_mix_g")[l], 16))
        put("g_ffn", _cols(f("norm_ffn_g")[l], 16))
        put("g_fin", _cols(f("norm_final_g"), 16))
        put("s5_d", _cols(f("s5_d")[l], 4))
        put("s5_bglu", _cols(f("s5_b_glu")[l], 4))
        wdw = f("cv_w_dw")[l]
        put("cv_wdw", wdw.reshape(31, 4, 128).transpose(2, 1, 0).reshape(128, 124))
        put("cv_bdw", _cols(f("cv_b_dw")[l], 4))
        put("cv_lng", _cols(f("cv_ln_g")[l], 4))
        put("cv_lnb", _cols(f("cv_ln_b")[l], 4))
        put("cv_bpw", _cols(f("cv_b_pw")[l], 4))
        put("lru_wc", f("lru_w_conv")[l].reshape(4, 4, 128).transpose(2, 1, 0).reshape(128, 16))
        put("lru_bc", _cols(f("lru_b_conv")[l], 4))
        put("lru_br", _cols(f("lru_b_r")[l], 4))
        put("lru_bi", _cols(f("lru_b_i")[l], 4))
        put("lru_lam", _cols(f("lru_lam")[l], 4))
        put("pool_scale", _cols(f("pool_scale")[l], 4))
        put("ffn_wdw", f("ffn_w_dw")[l].reshape(3, NFT, 128).transpose(2, 1, 0).reshape(128, NFT * 3))
        put("ffn_bdw", _cols(f("ffn_b_dw")[l], NFT))
        put("invc", np.broadcast_to(invc.reshape(1, 64), (128, 64)))
    out["vec"] = vec
    s5_pp = np.zeros((depth, 128, 48), np.float32)
    s5_row = np.zeros((depth, 3, 2048), np.float32)
    s5_bz = np.zeros((depth, 2, 128, 2048), np.float32)
    s5_cz = np.zeros((depth, 2, 128, 2048), np.float32)
    for l in range(depth):
        lre, lim, lst = f("s5_lam_re")[l], f("s5_lam_im")[l], f("s5_log_step")[l]
        s5_pp[l, :, 0:16] = lre.reshape(16, 128).T
        s5_pp[l, :, 16:32] = lim.reshape(16, 128).T
        s5_pp[l, :, 32:48] = np.repeat(lst, 64).reshape(16, 128).T
        s5_row[l, 0] = lre.reshape(-1)
        s5_row[l, 1] = lim.reshape(-1)
        s5_row[l, 2] = np.repeat(lst, 64)
        for k, (bk, ck) in enumerate([("s5_b_re", "s5_c_re"), ("s5_b_im", "s5_c_im")]):
            B = f(bk)[l]
            C = f(ck)[l]
            for g in range(32):
                q, gi = divmod(g, 2)
                gl = g % 8
                s5_bz[l, k, gl * 16:(gl + 1) * 16, q * 128 + gi * 64:q * 128 + (gi + 1) * 64] = B[g].T
                s5_cz[l, k, gi * 64:(gi + 1) * 64, q * 128 + gl * 16:q * 128 + (gl + 1) * 16] = C[g].T
    out["s5_pp"], out["s5_row"], out["s5_bz"], out["s5_cz"] = s5_pp, s5_row, s5_bz, s5_cz
    out["ident"] = np.eye(128, dtype=np.float32)
    return out


_NC_CACHE = {}


def core_aux(r, L):
    nt = L // TT
    ntl = nt // 2
    gi = np.zeros((128, 64), np.int32)
    for i in range(ntl):
        gi[:, i] = (r * ntl + i) * 128 + np.arange(128)
    return gi, np.full((128, 1), float(r), np.float32)


def kernel(**inputs):
    x = np.asarray(inputs["x"], np.float32)
    nb, L, _ = x.shape
    wts = prep_weights(inputs)
    if L not in _NC_CACHE:
        _NC_CACHE[L] = build(L)
    nc = _NC_CACHE[L]
    in_maps = []
    for c in range(2 * nb):
        m = dict(wts)
        m["xT"] = np.ascontiguousarray(x[c // 2].T)
        m["gidx"], m["hflag"] = core_aux(c % 2, L)
        in_maps.append(m)
    res = run_bass_kernel_spmd(nc, in_maps, core_ids=list(range(2 * nb)))
    out = np.empty((nb, L, D), np.float32)
    for c in range(2 * nb):
        r = c % 2
        out[c // 2, r * (L // 2):(r + 1) * (L // 2), :] = res.results[c]["outT"].T
    return out.astype(np.float32)
```

```python
import contextlib
import numpy as np
import concourse.bass as bass
import concourse.mybir as mybir
from concourse.bass_utils import run_bass_kernel_spmd

F32 = mybir.dt.float32
BF16 = mybir.dt.bfloat16
ALU = mybir.AluOpType
AF = mybir.ActivationFunctionType

D = 2048
NB = 4
SEQ = 4096
DEPTH = 2
FF = 5504
NFT = FF // 128
EPS = 1e-6
TT = 512
PI = float(np.pi)

VOFF = {}
_o = 0
for _n, _w in [("g_mix", 16), ("g_ffn", 16), ("g_fin", 16), ("s5_d", 4), ("s5_bglu", 4), ("cv_wdw", 124),
               ("cv_bdw", 4), ("cv_lng", 4), ("cv_lnb", 4), ("cv_bpw", 4), ("lru_wc", 16), ("lru_bc", 4),
               ("lru_br", 4), ("lru_bi", 4), ("lru_lam", 4), ("pool_scale", 4), ("ffn_wdw", 129),
               ("ffn_bdw", 43), ("invc", 64)]:
    VOFF[_n] = _o
    _o += _w
NV = _o


class Reg:
    __slots__ = ("w", "r")

    def __init__(self):
        self.w = None
        self.r = {}


class Eng:
    def __init__(self, name, e, sem):
        self.name, self.e, self.sem, self.cnt, self.seen = name, e, sem, 0, {}


class FW:
    def __init__(self, nc, es, ndq=20):
        self.nc = nc
        mk = lambda n: es.enter_context(nc.semaphore(n))
        self.pe = Eng("pe", nc.tensor, mk("s_pe"))
        self.act = Eng("act", nc.scalar, mk("s_act"))
        self.dve = Eng("dve", nc.vector, mk("s_dve"))
        self.pool = Eng("pool", nc.gpsimd, mk("s_pool"))
        self.sp = Eng("sp", nc.sync, mk("s_sp"))
        self.dq = {}
        self.dqi = {}
        for e in (self.sp, self.pool):
            self.dq[e.name] = [[mk("d_%s%d" % (e.name, i)), 0] for i in range(ndq)]
            self.dqi[e.name] = 0

    def _deps(self, eng, reads, writes):
        deps = {}

        def add(tok):
            if tok is None:
                return
            k, sem, val = tok
            if eng.name == "pe" and k == "pe":
                return
            if k not in deps or deps[k][1] < val:
                deps[k] = (sem, val)

        for r in reads:
            add(r.w)
        for w in writes:
            add(w.w)
            for tok in w.r.values():
                add(tok)
        return deps

    def _wait(self, eng, deps):
        for k, (sem, val) in deps.items():
            if eng.seen.get(k, 0) < val:
                eng.e.wait_ge(sem, val)
                eng.seen[k] = val

    def op(self, eng, fn, reads=(), writes=(), inc=True):
        self._wait(eng, self._deps(eng, reads, writes))
        inst = fn(eng.e)
        if inc:
            eng.cnt += 1
            inst.then_inc(eng.sem, 1)
            tok = (eng.name, eng.sem, eng.cnt)
        else:
            tok = (eng.name, eng.sem, eng.cnt + 1)
        for r in reads:
            old = r.r.get(eng.name)
            if old is None or old[2] < tok[2]:
                r.r[eng.name] = tok
        for w in writes:
            w.w = tok
            w.r = {}
        return inst

    def dma(self, eng, out, in_, reads=(), writes=()):
        pool = self.dq[eng.name]
        i = self.dqi[eng.name]
        self.dqi[eng.name] = (i + 1) % len(pool)
        ent = pool[i]
        key = "d_%s%d" % (eng.name, i)
        deps = self._deps(eng, reads, writes)
        if ent[1] > 0:
            deps[key] = (ent[0], ent[1])
        self._wait(eng, deps)
        ent[1] += 16
        eng.e.dma_start(out=out, in_=in_).then_inc(ent[0], 16)
        tok = (key, ent[0], ent[1])
        for r in reads:
            r.r[key] = tok
        for w in writes:
            w.w = tok
            w.r = {}

    def gather(self, out, in_, idx_ap, reads=(), writes=()):
        eng = self.pool
        pool = self.dq[eng.name]
        i = self.dqi[eng.name]
        self.dqi[eng.name] = (i + 1) % len(pool)
        ent = pool[i]
        key = "d_%s%d" % (eng.name, i)
        deps = self._deps(eng, reads, writes)
        if ent[1] > 0:
            deps[key] = (ent[0], ent[1])
        self._wait(eng, deps)
        ent[1] += 16
        eng.e.indirect_dma_start(out=out, out_offset=None, in_=in_,
                                 in_offset=bass.IndirectOffsetOnAxis(ap=idx_ap, axis=0)).then_inc(ent[0], 16)
        tok = (key, ent[0], ent[1])
        for r in reads:
            r.r[key] = tok
        for w in writes:
            w.w = tok
            w.r = {}

    def barrier(self):
        engs = [self.pe, self.act, self.dve, self.pool, self.sp]
        for a in engs:
            deps = {}
            for b in engs:
                if b is not a and b.cnt > 0:
                    deps[b.name] = (b.sem, b.cnt)
            for name, pool in self.dq.items():
                for i, (sem, val) in enumerate(pool):
                    if val > 0:
                        deps["d_%s%d" % (name, i)] = (sem, val)
            self._wait(a, deps)

    def finish(self):
        for name, pool in self.dq.items():
            for sem, val in pool:
                if val > 0:
                    self.sp.e.wait_ge(sem, val)


class Rot:
    def __init__(self, items):
        self.items = items
        self.i = 0

    def next(self):
        it = self.items[self.i]
        self.i = (self.i + 1) % len(self.items)
        return it


def build(L=SEQ, depth=DEPTH, debug=False, split=True):
    nc = bass.Bass("TRN2", target_bir_lowering=False)
    NT = L // TT
    NP = L // TT
    dt_in = lambda name, shape: nc.dram_tensor(name, shape, F32, kind="ExternalInput").ap()
    skind = "ExternalOutput" if debug else "Internal"
    xT = dt_in("xT", [D, L])
    w_in = dt_in("w_in", [depth, 24, 128, 2048])
    w_out = dt_in("w_out", [depth, 16, 128, 2048])
    w_up = dt_in("w_up", [depth, 86, 128, 2048])
    w_down = dt_in("w_down", [depth, 16, 128, NFT * 128])
    w_glu = dt_in("w_glu", [depth, 4, 128, 512])
    w_pw = dt_in("w_pw", [depth, 4, 128, 512])
    w_lr = dt_in("w_lr", [depth, 4, 128, 128])
    w_li = dt_in("w_li", [depth, 4, 128, 128])
    w_pool = dt_in("w_pool", [depth, 4, 128, 128])
    vec = dt_in("vec", [depth, 128, NV])
    s5_pp = dt_in("s5_pp", [depth, 128, 48])
    s5_row = dt_in("s5_row", [depth, 3, 2048])
    s5_bz = dt_in("s5_bz", [depth, 2, 128, 2048])
    s5_cz = dt_in("s5_cz", [depth, 2, 128, 2048])
    ident_d = dt_in("ident", [128, 128])
    LO = L // 2 if split else L
    NTL = NT // 2 if split else NT
    idx_d = nc.dram_tensor("gidx", [128, 64], mybir.dt.int32, kind="ExternalInput").ap()
    flag_d = dt_in("hflag", [128, 1])
    outT = nc.dram_tensor("outT", [D, LO], F32, kind="ExternalOutput").ap()
    projU = nc.dram_tensor("projU", [512, L], BF16, kind=skind).ap()
    projR = nc.dram_tensor("projR", [2560, L], F32, kind=skind).ap()
    mixT = nc.dram_tensor("mixT", [NT * 128, 16 * TT], BF16, kind=skind).ap()
    xs = nc.dram_tensor("xs", [NT * 128, 16 * TT], F32, kind=skind).ap()
    tile_view = lambda ap, j: ap[j * 128:(j + 1) * 128, :].rearrange("p (c n) -> p c n", n=TT)
    chunk_view = lambda ap, j, ci: ap[j * 128:(j + 1) * 128, ci * TT:(ci + 1) * TT]
    rows_view = lambda ap, ci: ap.rearrange("(j p) (c n) -> p j c n", p=128, n=TT)[:, :, ci, :]
    r_projU, r_projR, r_mixT, r_xs, r_out = Reg(), Reg(), Reg(), Reg(), Reg()

    es = contextlib.ExitStack()
    with es:
        fw = FW(nc, es)
        pe, act, dve, pool, sp = fw.pe, fw.act, fw.dve, fw.pool, fw.sp
        cnt = [0]

        def sbt(st, shape, dt, name=None):
            cnt[0] += 1
            return st.enter_context(nc.sbuf_tensor(name or ("t%d" % cnt[0]), shape, dt))

        PS = Rot([(es.enter_context(nc.psum_tensor("ps%d" % i, [128, TT], F32)), Reg()) for i in range(8)])
        ones_bf = sbt(es, [128, 128], BF16)
        ones_f = sbt(es, [128, 128], F32)
        epsT = sbt(es, [128, 1], F32)
        oneT = sbt(es, [128, 1], F32)
        hpiT = sbt(es, [128, 1], F32)
        ident_f = sbt(es, [128, 128], F32)
        ident_b = sbt(es, [128, 128], BF16)
        vecT = [sbt(es, [128, NV], F32) for _ in range(depth)]
        r_const = Reg()
        fw.op(dve, lambda e: e.memset(ones_bf[:], 1.0), writes=[r_const])
        fw.op(dve, lambda e: e.memset(ones_f[:], 1.0), writes=[r_const])
        fw.op(dve, lambda e: e.memset(epsT[:], EPS), writes=[r_const])
        fw.op(dve, lambda e: e.memset(oneT[:], 1.0), writes=[r_const])
        fw.op(dve, lambda e: e.memset(hpiT[:], PI / 2), writes=[r_const])
        r_ident = Reg()
        fw.dma(sp, ident_f[:], ident_d, writes=[r_ident])
        fw.op(dve, lambda e: e.tensor_copy(out=ident_b[:], in_=ident_f[:]), reads=[r_ident], writes=[r_const])
        gidx = sbt(es, [128, 64], mybir.dt.int32)
        flagT = sbt(es, [128, 1], F32)
        r_gidx = Reg()
        fw.dma(sp, gidx[:], idx_d, writes=[r_gidx])
        fw.dma(sp, flagT[:], flag_d, writes=[r_gidx])
        r_vec = [Reg() for _ in range(depth)]
        for l in range(depth):
            fw.dma(sp, vecT[l][:], vec[l], writes=[r_vec[l]])

        def vcol(l, name, i=0, n=1):
            o = VOFF[name] + i
            return vecT[l][:, o:o + n]

        NWS = 5
        wsm = Rot([(sbt(es, [128, 2048], BF16), Reg()) for _ in range(NWS)])

        def load_w(dram_ap, rot=None, ncol=2048):
            t, r = (rot or wsm).next()
            fw.dma(pool, t[:, 0:ncol], dram_ap, writes=[r])
            return t, r

        evac_flip = [0]

        def evac(out_ap, ps_ap, reads, writes, bias=None, scale=None, func=None, eng=None):
            if func is None and bias is None and scale is None and eng is None:
                evac_flip[0] ^= 1
                eng = act if evac_flip[0] else dve
            if eng is dve:
                fw.op(dve, lambda e: e.tensor_copy(out=out_ap, in_=ps_ap), reads=reads, writes=writes)
            else:
                kw = {}
                if bias is not None:
                    kw["bias"] = bias
                if scale is not None:
                    kw["scale"] = scale
                fw.op(act, lambda e: e.activation(out=out_ap, in_=ps_ap, func=func or AF.Identity, **kw),
                      reads=reads, writes=writes)

        def rmsnorm(st_bufs, xt, xr, gcols, hout, hregs, out_fn=None, post_fn=None, w=TT):
            sq, sqr, lnv, lnr, rstd, rsr = st_bufs
            ps, psr = PS.next()
            for c in range(16):
                b = c % 2
                fw.op(act, lambda e: e.activation(out=sq[b][:, 0:w], in_=xt[:, c, :], func=AF.Square),
                      reads=[xr[c]], writes=[sqr[b]])
                fw.op(pe, lambda e: e.matmul(ps[:, 0:w], lhsT=ones_bf[:], rhs=sq[b][:, 0:w], start=(c == 0), stop=(c == 15)),
                      reads=[sqr[b], r_const], writes=[psr], inc=True)
            fw.op(act, lambda e: e.activation(out=lnv[:, 0:w], in_=ps[:, 0:w], func=AF.Ln, scale=1.0 / D, bias=epsT[:, 0:1]),
                  reads=[psr, r_const], writes=[lnr])
            fw.op(act, lambda e: e.activation(out=rstd[:, 0:w], in_=lnv[:, 0:w], func=AF.Exp, scale=-0.5),
                  reads=[lnr], writes=[rsr])
            for c in range(16):
                if out_fn is None:
                    oap, oreg = hout[:, c, :], hregs[c]
                else:
                    oap, oreg = out_fn(c)
                fw.op(dve, lambda e: e.scalar_tensor_tensor(out=oap, in0=xt[:, c, :], scalar=gcols[:, c:c + 1],
                                                            in1=rstd[:, 0:w], op0=ALU.mult, op1=ALU.mult),
                      reads=[xr[c], rsr], writes=[oreg])
                if post_fn is not None:
                    post_fn(c, oap, oreg)

        def norm_bufs(st):
            sq = [sbt(st, [128, TT], BF16) for _ in range(2)]
            return (sq, [Reg(), Reg()], sbt(st, [128, TT], F32), Reg(), sbt(st, [128, TT], F32), Reg())

        xview = lambda ap: ap.rearrange("(c p) l -> p c l", p=128)

        for l in range(depth):
            fw.barrier()
            with contextlib.ExitStack() as st:
                nb = norm_bufs(st)
                xts = [(sbt(st, [128, 16, TT], F32), [Reg() for _ in range(16)]) for _ in range(1)]
                hts = [(sbt(st, [128, 16, TT], BF16), [Reg() for _ in range(16)]) for _ in range(2)]
                stg_b = Rot([(sbt(st, [128, TT], BF16), Reg()) for _ in range(3)])
                stg_f = Rot([(sbt(st, [128, TT], F32), Reg()) for _ in range(3)])
                wres = sbt(st, [128, 24, 2048], BF16)
                r_wres = [Reg() for _ in range(24)]
                for mt in range(24):
                    fw.dma(pool, wres[:, mt, :], w_in[l, mt], writes=[r_wres[mt]])
                for j in range(NT):
                    cs = slice(j * TT, (j + 1) * TT)
                    xt, xr = xts[0]
                    ht, hr = hts[j % 2]
                    if l == 0:
                        fw.dma(sp, xt[:], xview(xT)[:, :, cs], writes=xr)
                    else:
                        fw.dma(sp, xt[:], tile_view(xs, j), reads=[r_xs], writes=xr)
                    rmsnorm(nb, xt, xr, vcol(l, "g_mix", 0, 16), ht, hr)
                    for mt in range(24):
                        W, wr = wres[:, mt, :], r_wres[mt]
                        ps, psr = PS.next()
                        for kc in range(16):
                            fw.op(pe, lambda e: e.matmul(ps[:], lhsT=W[:, kc * 128:(kc + 1) * 128], rhs=ht[:, kc, :],
                                                         start=(kc == 0), stop=(kc == 15)),
                                  reads=[wr, hr[kc]], writes=[psr], inc=(kc == 15))
                        if mt < 4:
                            sg, sr = stg_b.next()
                            evac(sg[:], ps[:], [psr], [sr])
                            fw.dma(sp, projU[mt * 128:(mt + 1) * 128, cs], sg[:], reads=[sr], writes=[r_projU])
                        else:
                            sg, sr = stg_f.next()
                            evac(sg[:], ps[:], [psr], [sr])
                            fw.dma(sp, projR[(mt - 4) * 128:(mt - 3) * 128, cs], sg[:], reads=[sr], writes=[r_projR])

            fw.barrier()
            with contextlib.ExitStack() as st:
                xl = sbt(st, [128, 4 + L], F32); r_xl = Reg()
                gl = sbt(st, [128, L], F32); r_gl = Reg()
                xc = sbt(st, [128, L], F32); r_xc = Reg()
                rb = sbt(st, [128, L], F32); r_rb = Reg()
                ib = sbt(st, [128, L], F32); r_ib = Reg()
                tmp = sbt(st, [128, L], F32); r_tmp = Reg()
                xcb = sbt(st, [128, L], BF16); r_xcb = Reg()
                ybf = sbt(st, [128, L], BF16); r_y = Reg()
                cc = sbt(st, [128, 4], F32); r_cc = Reg()
                ce = sbt(st, [128, 4], F32); r_ce = Reg()
                fw.op(dve, lambda e: e.memset(xl[:, 0:4], 0.0), writes=[r_xl])
                fw.op(act, lambda e: e.activation(out=ce[:], in_=vcol(l, "lru_lam", 0, 4), func=AF.Exp, scale=-1.0),
                      reads=[r_vec[l]], writes=[r_ce])
                fw.op(act, lambda e: e.activation(out=cc[:], in_=ce[:], func=AF.Ln, bias=oneT[:, 0:1]),
                      reads=[r_ce, r_const], writes=[r_cc])
                fw.op(dve, lambda e: e.tensor_scalar(out=cc[:], in0=cc[:], scalar1=-8.0, scalar2=None, op0=ALU.mult),
                      reads=[r_cc], writes=[r_cc])
                for ct in range(4):
                    rx = 1024 + ct * 128
                    rg = 1536 + ct * 128
                    fw.dma(sp, xl[:, 4:], projR[rx:rx + 128, :], reads=[r_projR], writes=[r_xl])
                    fw.dma(sp, gl[:], projR[rg:rg + 128, :], reads=[r_projR], writes=[r_gl])
                    wc = lambda k: vcol(l, "lru_wc", ct * 4 + k)
                    fw.op(dve, lambda e: e.tensor_scalar(out=xc[:], in0=xl[:, 1:1 + L], scalar1=wc(0), scalar2=vcol(l, "lru_bc", ct),
                                                         op0=ALU.mult, op1=ALU.add), reads=[r_xl, r_vec[l]], writes=[r_xc])
                    for k in range(1, 4):
                        fw.op(dve, lambda e: e.scalar_tensor_tensor(out=xc[:], in0=xl[:, 1 + k:1 + k + L], scalar=wc(k), in1=xc[:],
                                                                    op0=ALU.mult, op1=ALU.add), reads=[r_xl, r_xc], writes=[r_xc])
                    fw.op(act, lambda e: e.copy(out=xcb[:], in_=xc[:]), reads=[r_xc], writes=[r_xcb])
                    Wr, wrr = load_w(w_lr[l, ct], ncol=128)
                    Wi, wir = load_w(w_li[l, ct], ncol=128)
                    for p in range(NP):
                        cs = slice(p * TT, (p + 1) * TT)
                        ps, psr = PS.next()
                        fw.op(pe, lambda e: e.matmul(ps[:], lhsT=Wr[:, 0:128], rhs=xcb[:, cs], start=True, stop=True),
                              reads=[wrr, r_xcb], writes=[psr])
                        evac(rb[:, cs], ps[:], [psr, r_vec[l]], [r_rb], bias=vcol(l, "lru_br", ct), func=AF.Sigmoid)
                        ps, psr = PS.next()
                        fw.op(pe, lambda e: e.matmul(ps[:], lhsT=Wi[:, 0:128], rhs=xcb[:, cs], start=True, stop=True),
                              reads=[wir, r_xcb], writes=[psr])
                        evac(ib[:, cs], ps[:], [psr, r_vec[l]], [r_ib], bias=vcol(l, "lru_bi", ct), func=AF.Sigmoid)
                    fw.op(act, lambda e: e.activation(out=rb[:], in_=rb[:], func=AF.Exp, scale=cc[:, ct:ct + 1]),
                          reads=[r_rb, r_cc], writes=[r_rb])
                    fw.op(act, lambda e: e.activation(out=tmp[:], in_=rb[:], func=AF.Square), reads=[r_rb], writes=[r_tmp])
                    fw.op(act, lambda e: e.activation(out=tmp[:], in_=tmp[:], func=AF.Sqrt, scale=-1.0, bias=oneT[:, 0:1]),
                          reads=[r_tmp, r_const], writes=[r_tmp])
                    fw.op(dve, lambda e: e.tensor_tensor(out=ib[:], in0=ib[:], in1=xc[:], op=ALU.mult), reads=[r_ib, r_xc], writes=[r_ib])
                    fw.op(dve, lambda e: e.tensor_tensor(out=ib[:], in0=ib[:], in1=tmp[:], op=ALU.mult), reads=[r_ib, r_tmp], writes=[r_ib])
                    fw.op(dve, lambda e: e.tensor_tensor_scan(out=xc[:], data0=rb[:], data1=ib[:], initial=0.0,
                                                              op0=ALU.mult, op1=ALU.add), reads=[r_rb, r_ib], writes=[r_xc])
                    fw.op(act, lambda e: e.activation(out=gl[:], in_=gl[:], func=AF.Gelu_apprx_tanh), reads=[r_gl], writes=[r_gl])
                    fw.op(dve, lambda e: e.tensor_tensor(out=ybf[:], in0=xc[:], in1=gl[:], op=ALU.mult), reads=[r_xc, r_gl], writes=[r_y])
                    fw.dma(sp, rows_view(mixT, 8 + ct), ybf[:].rearrange("p (j n) -> p j n", n=TT), reads=[r_y], writes=[r_mixT])

            fw.barrier()
            with contextlib.ExitStack() as st:
                vb = sbt(st, [128, L], F32); r_vb = Reg()
                gb = sbt(st, [128, L], F32); r_gb = Reg()
                hpad = sbt(st, [128, 32 + L], BF16); r_hp = Reg()
                diag = sbt(st, [128, 31, 128], BF16); r_dg = Reg()
                hc = sbt(st, [128, 4, L], F32); r_hc = [[Reg() for _ in range(NP)] for _ in range(4)]
                sqf = [sbt(st, [128, TT], F32) for _ in range(2)]; r_sqf = [Reg(), Reg()]
                mu = sbt(st, [128, TT], F32); r_mu = Reg()
                var = sbt(st, [128, TT], F32); r_var = Reg()
                rstd = sbt(st, [128, TT], F32); r_rstd = Reg()
                tn = [sbt(st, [128, TT], F32) for _ in range(2)]; r_tn = [Reg(), Reg()]
                sbf = sbt(st, [128, 4, TT], BF16); r_sbf = [Reg() for _ in range(4)]
                stg = Rot([(sbt(st, [128, TT], BF16), Reg()) for _ in range(3)])
                fw.op(dve, lambda e: e.memset(hpad[:, 0:32], 0.0), writes=[r_hp])
                pbufs = [sbt(st, [128, 16 + L], F32) for _ in range(3)]
                pbregs = [Reg() for _ in range(3)]
                pt16 = sbt(st, [128, 16], F32)
                r_pt16 = Reg()
                for b in range(3):
                    fw.op(dve, lambda e: e.memset(pbufs[b][:, 0:16], 0.0), writes=[pbregs[b]])
                def pool_gi(gi):
                    win = 2 ** (gi + 1)
                    row0 = 2048 + gi * 128
                    xp, xpr = pbufs[0], pbregs[0]
                    fw.dma(sp, xp[:, 16:], projR[row0:row0 + 128, :], reads=[r_projR], writes=[xpr])
                    cur, curr = xp, xpr
                    m = 1
                    k = 0
                    while m < win:
                        nx, nxr = pbufs[1 + (k % 2)], pbregs[1 + (k % 2)]
                        fw.op(dve, lambda e: e.tensor_tensor(out=nx[:, 16:], in0=cur[:, 16:], in1=cur[:, 16 - m:16 - m + L],
                                                             op=ALU.add), reads=[curr], writes=[nxr])
                        cur, curr = nx, nxr
                        m *= 2
                        k += 1
                    ob, r_pd = pbufs[1 + (k % 2)], pbregs[1 + (k % 2)]
                    pdfull = ob[:].bitcast(BF16)[:, 64:64 + L]
                    fw.op(dve, lambda e: e.scalar_tensor_tensor(out=pdfull, in0=cur[:, 16:], scalar=1.0 / win, in1=xp[:, 16:],
                                                                op0=ALU.mult, op1=ALU.subtract),
                          reads=[curr, xpr], writes=[r_pd])
                    fw.op(dve, lambda e: e.tensor_tensor(out=pt16[:], in0=cur[:, 16:32], in1=vcol(l, "invc", gi * 16, 16), op=ALU.mult),
                          reads=[curr, r_vec[l]], writes=[r_pt16])
                    fw.op(dve, lambda e: e.tensor_tensor(out=pdfull[:, 0:16], in0=pt16[:], in1=xp[:, 16:32], op=ALU.subtract),
                          reads=[r_pt16, xpr], writes=[r_pd])
                    W, wr = load_w(w_pool[l, gi], ncol=128)
                    for p in range(NP):
                        cs = slice(p * TT, (p + 1) * TT)
                        ps, psr = PS.next()
                        fw.op(pe, lambda e: e.matmul(ps[:], lhsT=W[:, 0:128], rhs=pdfull[:, cs], start=True, stop=True),
                              reads=[wr, r_pd], writes=[psr])
                        sg, sr = stg.next()
                        evac(sg[:], ps[:], [psr, r_vec[l]], [sr], scale=vcol(l, "pool_scale", gi))
                        fw.dma(sp, chunk_view(mixT, p, 12 + gi), sg[:], reads=[sr], writes=[r_mixT])


                for ct in range(4):
                    rv = ct * 128
                    rg = 512 + ct * 128
                    fw.dma(sp, vb[:], projR[rv:rv + 128, :], reads=[r_projR], writes=[r_vb])
                    fw.dma(sp, gb[:], projR[rg:rg + 128, :], reads=[r_projR], writes=[r_gb])
                    fw.op(act, lambda e: e.activation(out=gb[:], in_=gb[:], func=AF.Sigmoid), reads=[r_gb], writes=[r_gb])
                    fw.op(dve, lambda e: e.tensor_tensor(out=hpad[:, 32:], in0=vb[:], in1=gb[:], op=ALU.mult),
                          reads=[r_vb, r_gb], writes=[r_hp])
                    for k in range(31):
                        fw.op(dve, lambda e: e.tensor_scalar(out=diag[:, k, :], in0=ident_f[:], scalar1=vcol(l, "cv_wdw", ct * 31 + k),
                                                             scalar2=None, op0=ALU.mult), reads=[r_ident, r_vec[l]], writes=[r_dg])
                    for p in range(NP):
                        ps, psr = PS.next()
                        for k in range(31):
                            o = 2 + p * TT + k
                            fw.op(pe, lambda e: e.matmul(ps[:], lhsT=diag[:, k, :], rhs=hpad[:, o:o + TT], start=(k == 0), stop=(k == 30)),
                                  reads=[r_dg, r_hp], writes=[psr], inc=(k == 30))
                        evac(hc[:, ct, p * TT:(p + 1) * TT], ps[:], [psr, r_vec[l]], [r_hc[ct][p]], bias=vcol(l, "cv_bdw", ct))
                    pool_gi(ct)
                Wp = [load_w(w_pw[l, mt], ncol=512) for mt in range(4)]
                for p in range(NP):
                    cs = slice(p * TT, (p + 1) * TT)
                    ps1, ps1r = PS.next()
                    ps2, ps2r = PS.next()
                    for ct in range(4):
                        b = ct % 2
                        fw.op(pe, lambda e: e.matmul(ps1[:], lhsT=ones_f[:], rhs=hc[:, ct, cs], start=(ct == 0), stop=(ct == 3)),
                              reads=[r_hc[ct][p], r_const], writes=[ps1r], inc=(ct == 3))
                        fw.op(act, lambda e: e.activation(out=sqf[b][:], in_=hc[:, ct, cs], func=AF.Square),
                              reads=[r_hc[ct][p]], writes=[r_sqf[b]])
                        fw.op(pe, lambda e: e.matmul(ps2[:], lhsT=ones_f[:], rhs=sqf[b][:], start=(ct == 0), stop=(ct == 3)),
                              reads=[r_sqf[b], r_const], writes=[ps2r], inc=True)
                    fw.op(act, lambda e: e.mul(out=mu[:], in_=ps1[:], mul=1.0 / 512), reads=[ps1r], writes=[r_mu])
                    fw.op(dve, lambda e: e.tensor_tensor(out=var[:], in0=mu[:], in1=mu[:], op=ALU.mult), reads=[r_mu], writes=[r_var])
                    fw.op(dve, lambda e: e.scalar_tensor_tensor(out=var[:], in0=ps2[:], scalar=1.0 / 512, in1=var[:],
                                                                op0=ALU.mult, op1=ALU.subtract), reads=[ps2r, r_var], writes=[r_var])
                    fw.op(act, lambda e: e.activation(out=var[:], in_=var[:], func=AF.Ln, bias=epsT[:, 0:1]),
                          reads=[r_var, r_const], writes=[r_var])
                    fw.op(act, lambda e: e.activation(out=rstd[:], in_=var[:], func=AF.Exp, scale=-0.5), reads=[r_var], writes=[r_rstd])
                    for ct in range(4):
                        b = ct % 2
                        fw.op(dve, lambda e: e.tensor_tensor(out=tn[b][:], in0=hc[:, ct, cs], in1=mu[:], op=ALU.subtract),
                              reads=[r_hc[ct][p], r_mu], writes=[r_tn[b]])
                        fw.op(dve, lambda e: e.tensor_tensor(out=tn[b][:], in0=tn[b][:], in1=rstd[:], op=ALU.mult),
                              reads=[r_tn[b], r_rstd], writes=[r_tn[b]])
                        fw.op(act, lambda e: e.activation(out=sbf[:, ct, :], in_=tn[b][:], func=AF.Silu,
                                                          scale=vcol(l, "cv_lng", ct), bias=vcol(l, "cv_lnb", ct)),
                              reads=[r_tn[b], r_vec[l]], writes=[r_sbf[ct]])
                    for mt in range(4):
                        W, wr = Wp[mt]
                        ps, psr = PS.next()
                        for kc in range(4):
                            fw.op(pe, lambda e: e.matmul(ps[:], lhsT=W[:, kc * 128:(kc + 1) * 128], rhs=sbf[:, kc, :],
                                                         start=(kc == 0), stop=(kc == 3)),
                                  reads=[wr, r_sbf[kc]], writes=[psr], inc=(kc == 3))
                        sg, sr = stg.next()
                        evac(sg[:], ps[:], [psr, r_vec[l]], [sr], bias=vcol(l, "cv_bpw", mt))
                        fw.dma(sp, chunk_view(mixT, p, 4 + mt), sg[:], reads=[sr], writes=[r_mixT])

            fw.barrier()
            with contextlib.ExitStack() as st:
                LBr = sbt(st, [128, 2048], BF16); LBi = sbt(st, [128, 2048], BF16); r_LB = Reg()
                LCr = sbt(st, [128, 2048], BF16); nLCr = sbt(st, [128, 2048], BF16); LCi = sbt(st, [128, 2048], BF16); r_LC = Reg()
                ppc = sbt(st, [128, 16, 12], F32); pps = sbt(st, [128, 16, 12], F32); r_pp = Reg()
                rho_pp = sbt(st, [128, 16], F32)

                def lam_bar(st2, lre, lim, lst, n, r_in):
                    mk = lambda: sbt(st2, [128, n], F32)
                    stp, a, rho, th, c, s, t1, t2 = mk(), mk(), mk(), mk(), mk(), mk(), mk(), mk()
                    rr = Reg()
                    fw.op(act, lambda e: e.activation(out=stp[:], in_=lst, func=AF.Exp), reads=[r_in], writes=[rr])
                    fw.op(dve, lambda e: e.tensor_tensor(out=a[:], in0=lre, in1=stp[:], op=ALU.mult), reads=[r_in, rr], writes=[rr])
                    fw.op(act, lambda e: e.activation(out=rho[:], in_=a[:], func=AF.Exp), reads=[rr], writes=[rr])
                    fw.op(dve, lambda e: e.tensor_tensor(out=th[:], in0=lim, in1=stp[:], op=ALU.mult), reads=[r_in, rr], writes=[rr])
                    fw.op(act, lambda e: e.activation(out=s[:], in_=th[:], func=AF.Sin, scale=1.0 / 16), reads=[rr], writes=[rr])
                    fw.op(act, lambda e: e.activation(out=c[:], in_=th[:], func=AF.Sin, scale=1.0 / 16, bias=hpiT[:, 0:1]),
                          reads=[rr, r_const], writes=[rr])
                    for _ in range(4):
                        fw.op(dve, lambda e: e.tensor_tensor(out=t1[:], in0=c[:], in1=c[:], op=ALU.mult), reads=[rr], writes=[rr])
                        fw.op(dve, lambda e: e.tensor_tensor(out=t2[:], in0=s[:], in1=s[:], op=ALU.mult), reads=[rr], writes=[rr])
                        fw.op(dve, lambda e: e.scalar_tensor_tensor(out=s[:], in0=c[:], scalar=2.0, in1=s[:], op0=ALU.mult, op1=ALU.mult),
                              reads=[rr], writes=[rr])
                        fw.op(dve, lambda e: e.tensor_tensor(out=c[:], in0=t1[:], in1=t2[:], op=ALU.subtract), reads=[rr], writes=[rr])
                    return rho, c, s, rr, (t1, t2, a, th)

                with contextlib.ExitStack() as st2:
                    rowt = sbt(st2, [128, 3, 2048], F32); r_row = Reg()
                    fw.dma(sp, rowt[:], s5_row[l].partition_broadcast(128), writes=[r_row])
                    rho, c, s, rr, (t1, t2, t3, t4) = lam_bar(st2, rowt[:, 0, :], rowt[:, 1, :], rowt[:, 2, :], 2048, r_row)
                    lre, lim = rowt[:, 0, :], rowt[:, 1, :]
                    lbr = sbt(st2, [128, 2048], F32); lbi = sbt(st2, [128, 2048], F32)
                    qr = sbt(st2, [128, 2048], F32); qi = sbt(st2, [128, 2048], F32)
                    T = lambda o, a, b, op, rd=(): fw.op(dve, lambda e: e.tensor_tensor(out=o, in0=a, in1=b, op=op),
                                                         reads=[rr, r_row] + list(rd), writes=[rr])
                    T(lbr[:], rho[:], c[:], ALU.mult)
                    T(lbi[:], rho[:], s[:], ALU.mult)
                    fw.op(dve, lambda e: e.tensor_scalar(out=lbr[:], in0=lbr[:], scalar1=-1.0, scalar2=None, op0=ALU.add),
                          reads=[rr], writes=[rr])
                    T(t1[:], lre, lre, ALU.mult)
                    T(t2[:], lim, lim, ALU.mult)
                    T(t1[:], t1[:], t2[:], ALU.add)
                    fw.op(dve, lambda e: e.reciprocal(out=t1[:], in_=t1[:]), reads=[rr], writes=[rr])
                    T(t2[:], lbr[:], lre, ALU.mult)
                    T(t3[:], lbi[:], lim, ALU.mult)
                    T(t2[:], t2[:], t3[:], ALU.add)
                    T(qr[:], t2[:], t1[:], ALU.mult)
                    T(t2[:], lbi[:], lre, ALU.mult)
                    T(t3[:], lbr[:], lim, ALU.mult)
                    T(t2[:], t2[:], t3[:], ALU.subtract)
                    T(qi[:], t2[:], t1[:], ALU.mult)
                    bz = sbt(st2, [128, 2, 2048], F32); r_bz = Reg()
                    fw.dma(sp, bz[:], s5_bz[l].rearrange("k p n -> p k n"), writes=[r_bz])
                    T(t1[:], qr[:], bz[:, 0, :], ALU.mult, [r_bz])
                    T(t2[:], qi[:], bz[:, 1, :], ALU.mult, [r_bz])
                    fw.op(dve, lambda e: e.tensor_tensor(out=LBr[:], in0=t1[:], in1=t2[:], op=ALU.subtract), reads=[rr], writes=[rr, r_LB])
                    T(t1[:], qr[:], bz[:, 1, :], ALU.mult, [r_bz])
                    T(t2[:], qi[:], bz[:, 0, :], ALU.mult, [r_bz])
                    fw.op(dve, lambda e: e.tensor_tensor(out=LBi[:], in0=t1[:], in1=t2[:], op=ALU.add), reads=[rr], writes=[rr, r_LB])
                    fw.dma(sp, bz[:], s5_cz[l].rearrange("k p n -> p k n"), reads=[], writes=[r_bz, rr])
                    fw.op(act, lambda e: e.copy(out=LCr[:], in_=bz[:, 0, :]), reads=[r_bz], writes=[r_LC])
                    fw.op(act, lambda e: e.mul(out=LCi[:], in_=bz[:, 1, :], mul=-1.0), reads=[r_bz], writes=[r_LC])
                    fw.op(act, lambda e: e.mul(out=nLCr[:], in_=bz[:, 0, :], mul=-1.0), reads=[r_bz], writes=[r_LC])
                    ppt = sbt(st2, [128, 48], F32); r_ppt = Reg()
                    fw.dma(sp, ppt[:], s5_pp[l], writes=[r_ppt])
                    rho2, c2, s2, rr2, (u1, u2, _, _) = lam_bar(st2, ppt[:, 0:16], ppt[:, 16:32], ppt[:, 32:48], 16, r_ppt)
                    fw.op(dve, lambda e: e.tensor_copy(out=rho_pp[:], in_=rho2[:]), reads=[rr2], writes=[r_pp])
                    fw.op(dve, lambda e: e.tensor_copy(out=ppc[:, :, 0], in_=c2[:]), reads=[rr2], writes=[r_pp])
                    fw.op(dve, lambda e: e.tensor_copy(out=pps[:, :, 0], in_=s2[:]), reads=[rr2], writes=[r_pp])
                    for k in range(11):
                        fw.op(dve, lambda e: e.tensor_tensor(out=u1[:], in0=ppc[:, :, k], in1=ppc[:, :, k], op=ALU.mult), reads=[r_pp, rr2], writes=[rr2])
                        fw.op(dve, lambda e: e.tensor_tensor(out=u2[:], in0=pps[:, :, k], in1=pps[:, :, k], op=ALU.mult), reads=[r_pp, rr2], writes=[rr2])
                        fw.op(dve, lambda e: e.tensor_tensor(out=ppc[:, :, k + 1], in0=u1[:], in1=u2[:], op=ALU.subtract), reads=[rr2], writes=[r_pp])
                        fw.op(dve, lambda e: e.scalar_tensor_tensor(out=pps[:, :, k + 1], in0=ppc[:, :, k], scalar=2.0, in1=pps[:, :, k],
                                                                    op0=ALU.mult, op1=ALU.mult), reads=[r_pp], writes=[r_pp])

                fw.barrier()
                tcb = sbt(st, [128, L], F32); tsb = sbt(st, [128, L], F32); r_tab = Reg()
                wre = sbt(st, [128, L], F32); wim = sbt(st, [128, L], F32)
                r_wre = [Reg() for _ in range(NP)]; r_wim = [Reg() for _ in range(NP)]
                tq = [sbt(st, [128, TT], F32) for _ in range(8)]; r_tq = [Reg() for _ in range(8)]
                vq = [sbt(st, [128, TT], BF16) for _ in range(8)]; r_vq = [Reg() for _ in range(8)]
                ub = sbt(st, [128, L], BF16); r_ub = Reg()
                yacc = sbt(st, [128, L], F32); r_ya = [Reg() for _ in range(NP)]
                G = sbt(st, [128, 4, L], BF16); r_G = [[Reg() for _ in range(NP)] for _ in range(4)]
                ytmp = [sbt(st, [128, TT], F32) for _ in range(2)]; r_yt = [Reg(), Reg()]
                stg = Rot([(sbt(st, [128, TT], BF16), Reg()) for _ in range(3)])
                for pt in range(4):
                    fw.dma(sp, ub[:], projU[pt * 128:(pt + 1) * 128, :], reads=[r_projU], writes=[r_ub])
                    for qq in range(4):
                        q = pt * 4 + qq
                        qs = slice(q * 128, (q + 1) * 128)
                        fw.op(dve, lambda e: e.memset(tcb[:, 0:1], 1.0), writes=[r_tab])
                        fw.op(dve, lambda e: e.memset(tsb[:, 0:1], 0.0), writes=[r_tab])
                        n = 1
                        k = 0
                        while n < L:
                            cr = ppc[:, q, k:k + 1]
                            ci = pps[:, q, k:k + 1]
                            tmpa, tmpb = wre, wim
                            if n >= 64:
                                fw.op(act, lambda e: e.activation(out=tmpa[:, 0:n], in_=tsb[:, 0:n], func=AF.Copy, scale=ci),
                                      reads=[r_tab, r_pp], writes=r_wre)
                                fw.op(act, lambda e: e.activation(out=tmpb[:, 0:n], in_=tsb[:, 0:n], func=AF.Copy, scale=cr),
                                      reads=[r_tab, r_pp], writes=r_wim)
                            else:
                                fw.op(dve, lambda e: e.tensor_scalar(out=tmpa[:, 0:n], in0=tsb[:, 0:n], scalar1=ci, scalar2=None, op0=ALU.mult),
                                      reads=[r_tab, r_pp], writes=r_wre)
                                fw.op(dve, lambda e: e.tensor_scalar(out=tmpb[:, 0:n], in0=tsb[:, 0:n], scalar1=cr, scalar2=None, op0=ALU.mult),
                                      reads=[r_tab, r_pp], writes=r_wim)
                            fw.op(dve, lambda e: e.scalar_tensor_tensor(out=tsb[:, n:2 * n], in0=tcb[:, 0:n], scalar=ci, in1=tmpb[:, 0:n],
                                                                        op0=ALU.mult, op1=ALU.add), reads=[r_tab, r_pp] + r_wre + r_wim, writes=[r_tab])
                            fw.op(dve, lambda e: e.scalar_tensor_tensor(out=tcb[:, n:2 * n], in0=tcb[:, 0:n], scalar=cr, in1=tmpa[:, 0:n],
                                                                        op0=ALU.mult, op1=ALU.subtract), reads=[r_tab, r_pp] + r_wre + r_wim, writes=[r_tab])
                            n *= 2
                            k += 1
                        for p in range(NP):
                            cs = slice(p * TT, (p + 1) * TT)
                            psr_, psrr = PS.next()
                            psi_, psir = PS.next()
                            fw.op(pe, lambda e: e.matmul(psr_[:], lhsT=LBr[:, qs], rhs=ub[:, cs], start=True, stop=True),
                                  reads=[r_LB, r_ub], writes=[psrr])
                            fw.op(pe, lambda e: e.matmul(psi_[:], lhsT=LBi[:, qs], rhs=ub[:, cs], start=True, stop=True),
                                  reads=[r_LB, r_ub], writes=[psir])
                            TT_ = lambda o, orr, a, ar, b, op: fw.op(dve, lambda e: e.tensor_tensor(out=o, in0=a, in1=b, op=op),
                                                                     reads=[ar, r_tab], writes=[orr])
                            tb = (p % 2) * 4
                            TT_(tq[tb][:], r_tq[tb], psr_[:], psrr, tcb[:, cs], ALU.mult)
                            TT_(tq[tb + 1][:], r_tq[tb + 1], psi_[:], psir, tsb[:, cs], ALU.mult)
                            TT_(tq[tb + 2][:], r_tq[tb + 2], psi_[:], psir, tcb[:, cs], ALU.mult)
                            TT_(tq[tb + 3][:], r_tq[tb + 3], psr_[:], psrr, tsb[:, cs], ALU.mult)
                            fw.op(pool, lambda e: e.tensor_tensor(out=wre[:, cs], in0=tq[tb][:], in1=tq[tb + 1][:], op=ALU.add),
                                  reads=[r_tq[tb], r_tq[tb + 1]], writes=[r_wre[p]])
                            fw.op(pool, lambda e: e.tensor_tensor(out=wim[:, cs], in0=tq[tb + 2][:], in1=tq[tb + 3][:], op=ALU.subtract),
                                  reads=[r_tq[tb + 2], r_tq[tb + 3]], writes=[r_wim[p]])
                        rho_b = rho_pp[:, q:q + 1].to_broadcast([128, L])
                        fw.op(dve, lambda e: e.tensor_tensor_scan(out=wre[:], data0=rho_b, data1=wre[:], initial=0.0, op0=ALU.mult, op1=ALU.add),
                              reads=r_wre + [r_pp], writes=r_wre)
                        fw.op(dve, lambda e: e.tensor_tensor_scan(out=wim[:], data0=rho_b, data1=wim[:], initial=0.0, op0=ALU.mult, op1=ALU.add),
                              reads=r_wim + [r_pp], writes=r_wim)
                        for p in range(NP):
                            cs = slice(p * TT, (p + 1) * TT)
                            b4 = (p % 2) * 4
                            v1, v2, v3, v4 = vq[b4], vq[b4 + 1], vq[b4 + 2], vq[b4 + 3]
                            rv = r_vq[b4:b4 + 4]
                            fw.op(dve, lambda e: e.tensor_tensor(out=v1[:], in0=wre[:, cs], in1=tcb[:, cs], op=ALU.mult), reads=[r_wre[p], r_tab], writes=[rv[0]])
                            fw.op(dve, lambda e: e.tensor_tensor(out=v2[:], in0=wre[:, cs], in1=tsb[:, cs], op=ALU.mult), reads=[r_wre[p], r_tab], writes=[rv[1]])
                            fw.op(dve, lambda e: e.tensor_tensor(out=v3[:], in0=wim[:, cs], in1=tcb[:, cs], op=ALU.mult), reads=[r_wim[p], r_tab], writes=[rv[2]])
                            fw.op(pool, lambda e: e.tensor_tensor(out=v4[:], in0=wim[:, cs], in1=tsb[:, cs], op=ALU.mult), reads=[r_wim[p], r_tab], writes=[rv[3]])
                            ps, psr = PS.next()
                            fw.op(pe, lambda e: e.matmul(ps[:], lhsT=LCr[:, qs], rhs=v1[:], start=True, stop=False), reads=[r_LC, rv[0]], writes=[psr], inc=False)
                            fw.op(pe, lambda e: e.matmul(ps[:], lhsT=nLCr[:, qs], rhs=v4[:], start=False, stop=False), reads=[r_LC, rv[3]], writes=[psr], inc=False)
                            fw.op(pe, lambda e: e.matmul(ps[:], lhsT=LCi[:, qs], rhs=v2[:], start=False, stop=False), reads=[r_LC, rv[1]], writes=[psr], inc=False)
                            fw.op(pe, lambda e: e.matmul(ps[:], lhsT=LCi[:, qs], rhs=v3[:], start=False, stop=True), reads=[r_LC, rv[2]], writes=[psr], inc=True)
                            if qq == 0:
                                evac(yacc[:, cs], ps[:], [psr], [r_ya[p]], eng=act)
                            else:
                                fw.op(dve, lambda e: e.tensor_tensor(out=yacc[:, cs], in0=yacc[:, cs], in1=ps[:], op=ALU.add),
                                      reads=[psr, r_ya[p]], writes=[r_ya[p]])
                    for p in range(NP):
                        cs = slice(p * TT, (p + 1) * TT)
                        b = p % 2
                        fw.op(dve, lambda e: e.scalar_tensor_tensor(out=ytmp[b][:], in0=ub[:, cs], scalar=vcol(l, "s5_d", pt), in1=yacc[:, cs],
                                                                    op0=ALU.mult, op1=ALU.add), reads=[r_ub, r_ya[p], r_vec[l]], writes=[r_yt[b]])
                        fw.op(act, lambda e: e.activation(out=G[:, pt, cs], in_=ytmp[b][:], func=AF.Gelu_apprx_tanh),
                              reads=[r_yt[b]], writes=[r_G[pt][p]])
                Wg = [load_w(w_glu[l, mt], ncol=512) for mt in range(4)]
                for p in range(NP):
                    cs = slice(p * TT, (p + 1) * TT)
                    for mt in range(4):
                        W, wr = Wg[mt]
                        ps, psr = PS.next()
                        for kc in range(4):
                            fw.op(pe, lambda e: e.matmul(ps[:], lhsT=W[:, kc * 128:(kc + 1) * 128], rhs=G[:, kc, cs], start=(kc == 0), stop=(kc == 3)),
                                  reads=[wr, r_G[kc][p]], writes=[psr], inc=(kc == 3))
                        b = mt % 2
                        evac(ytmp[b][:], ps[:], [psr, r_vec[l]], [r_yt[b]], bias=vcol(l, "s5_bglu", mt), func=AF.Sigmoid)
                        sg, sr = stg.next()
                        fw.op(dve, lambda e: e.tensor_tensor(out=sg[:], in0=ytmp[b][:], in1=G[:, mt, cs], op=ALU.mult),
                              reads=[r_yt[b], r_G[mt][p]], writes=[sr])
                        fw.dma(sp, chunk_view(mixT, p, mt), sg[:], reads=[sr], writes=[r_mixT])

            fw.barrier()
            with contextlib.ExitStack() as st:
                nb = norm_bufs(st)
                nxb = 2
                xts = [(sbt(st, [128, 16, TT], F32), [Reg() for _ in range(16)]) for _ in range(nxb)]
                mts = [(sbt(st, [128, 16, TT], BF16), Reg()) for _ in range(1)]
                ht, hr = sbt(st, [128, 16, TT], BF16), [Reg() for _ in range(16)]
                ostg = Rot([(sbt(st, [128, TT], F32), Reg()) for _ in range(2)])
                actb = sbt(st, [128, NFT, TT], BF16); r_act = [Reg() for _ in range(NFT)]
                gbuf = [sbt(st, [128, 2 + TT], F32) for _ in range(2)]; r_gb = [Reg(), Reg()]
                cb = [sbt(st, [128, TT], F32) for _ in range(2)]; r_cb = [Reg(), Reg()]
                halo = sbt(st, [128, NFT, 2], F32); r_halo = [Reg() for _ in range(NFT)]
                wbig = Rot([(sbt(st, [128, NFT * 128], BF16), Reg()) for _ in range(2)])
                fw.op(dve, lambda e: e.memset(halo[:], 0.0), writes=r_halo)
                last = (l == depth - 1)
                gat = last and split
                if gat:
                    assert l > 0
                    hx = sbt(st, [128, 16, 2], F32); r_hx = [Reg() for _ in range(16)]
                    hm = sbt(st, [128, 16, 2], BF16); r_hm = Reg()
                    hh = sbt(st, [128, 16, 2], BF16); r_hh = [Reg() for _ in range(16)]
                for j in range(NTL if gat else NT):
                    cs = slice(j * TT, (j + 1) * TT)
                    xt, xr = xts[j % nxb]
                    mtile, mr = mts[0]
                    dohalo = gat and j == 0
                    if gat:
                        if j == 0:
                            fw.gather(xt[:].rearrange("p c n -> p (c n)"), xs, gidx[:, j:j + 1], reads=[r_xs, r_gidx], writes=xr)
                            fw.gather(mtile[:].rearrange("p c n -> p (c n)"), mixT, gidx[:, j:j + 1], reads=[r_mixT, r_gidx], writes=[mr])
                    elif j == 0:
                        if l == 0:
                            fw.dma(sp, xt[:], xview(xT)[:, :, cs], writes=xr)
                        else:
                            fw.dma(sp, xt[:], tile_view(xs, j), reads=[r_xs], writes=xr)
                        fw.dma(sp, mtile[:], tile_view(mixT, j), reads=[r_mixT], writes=[mr])
                    if dohalo:
                        fw.dma(sp, hx[:], tile_view(xs, NTL - 1)[:, :, TT - 2:TT], reads=[r_xs], writes=r_hx)
                        fw.dma(sp, hm[:], tile_view(mixT, NTL - 1)[:, :, TT - 2:TT], reads=[r_mixT], writes=[r_hm])
                    for mt in range(16):
                        W, wr = load_w(w_out[l, mt])
                        ps, psr = PS.next()
                        for kc in range(16):
                            fw.op(pe, lambda e: e.matmul(ps[:], lhsT=W[:, kc * 128:(kc + 1) * 128], rhs=mtile[:, kc, :],
                                                         start=(kc == 0), stop=(kc == 15)), reads=[wr, mr], writes=[psr], inc=(kc == 15))
                        fw.op(dve, lambda e: e.tensor_tensor(out=xt[:, mt, :], in0=xt[:, mt, :], in1=ps[:], op=ALU.add),
                              reads=[psr, xr[mt]], writes=[xr[mt]])
                        if dohalo:
                            psh, pshr = PS.next()
                            for kc in range(16):
                                fw.op(pe, lambda e: e.matmul(psh[:, 0:2], lhsT=W[:, kc * 128:(kc + 1) * 128], rhs=hm[:, kc, :],
                                                             start=(kc == 0), stop=(kc == 15)), reads=[wr, r_hm], writes=[pshr], inc=(kc == 15))
                            fw.op(dve, lambda e: e.tensor_tensor(out=hx[:, mt, :], in0=hx[:, mt, :], in1=psh[:, 0:2], op=ALU.add),
                                  reads=[pshr, r_hx[mt]], writes=[r_hx[mt]])
                            fw.op(dve, lambda e: e.tensor_scalar(out=hx[:, mt, :], in0=hx[:, mt, :], scalar1=flagT[:, 0:1], scalar2=None, op0=ALU.mult),
                                  reads=[r_hx[mt], r_gidx], writes=[r_hx[mt]])
                    if gat and j + 1 < NTL:
                        xt2, xr2 = xts[(j + 1) % nxb]
                        fw.gather(xt2[:].rearrange("p c n -> p (c n)"), xs, gidx[:, j + 1:j + 2], reads=[r_xs, r_gidx], writes=xr2)
                    if (not gat) and j + 1 < NT:
                        xt2, xr2 = xts[(j + 1) % nxb]
                        cs2 = slice((j + 1) * TT, (j + 2) * TT)
                        if l == 0:
                            fw.dma(sp, xt2[:], xview(xT)[:, :, cs2], writes=xr2)
                        else:
                            fw.dma(sp, xt2[:], tile_view(xs, j + 1), reads=[r_xs], writes=xr2)
                    rmsnorm(nb, xt, xr, vcol(l, "g_ffn", 0, 16), ht, hr)
                    if dohalo:
                        rmsnorm(nb, hx, r_hx, vcol(l, "g_ffn", 0, 16), hh, r_hh, w=2)
                    for ft in range(NFT):
                        Wg_, wgr = load_w(w_up[l, ft])
                        Wv_, wvr = load_w(w_up[l, NFT + ft])
                        psg, psgr = PS.next()
                        psv, psvr = PS.next()
                        for kc in range(16):
                            fw.op(pe, lambda e: e.matmul(psg[:], lhsT=Wg_[:, kc * 128:(kc + 1) * 128], rhs=ht[:, kc, :],
                                                         start=(kc == 0), stop=(kc == 15)), reads=[wgr, hr[kc]], writes=[psgr], inc=(kc == 15))
                        for kc in range(16):
                            fw.op(pe, lambda e: e.matmul(psv[:], lhsT=Wv_[:, kc * 128:(kc + 1) * 128], rhs=ht[:, kc, :],
                                                         start=(kc == 0), stop=(kc == 15)), reads=[wvr, hr[kc]], writes=[psvr], inc=(kc == 15))
                        b = ft % 2
                        gbt, gbr = gbuf[b], r_gb[b]
                        fw.op(act, lambda e: e.copy(out=gbt[:, 2:], in_=psg[:]), reads=[psgr], writes=[gbr])
                        if dohalo:
                            psh, pshr = PS.next()
                            for kc in range(16):
                                fw.op(pe, lambda e: e.matmul(psh[:, 0:2], lhsT=Wg_[:, kc * 128:(kc + 1) * 128], rhs=hh[:, kc, :],
                                                             start=(kc == 0), stop=(kc == 15)), reads=[wgr, r_hh[kc]], writes=[pshr], inc=(kc == 15))
                            fw.op(dve, lambda e: e.tensor_copy(out=gbt[:, 0:2], in_=psh[:, 0:2]), reads=[pshr], writes=[gbr])
                        else:
                            fw.op(dve, lambda e: e.tensor_copy(out=gbt[:, 0:2], in_=halo[:, ft, :]), reads=[r_halo[ft]], writes=[gbr])
                        fw.op(dve, lambda e: e.tensor_copy(out=halo[:, ft, :], in_=gbt[:, TT:TT + 2]), reads=[gbr], writes=[r_halo[ft]])
                        wd = lambda k: vcol(l, "ffn_wdw", ft * 3 + k)
                        fw.op(dve, lambda e: e.tensor_scalar(out=cb[b][:], in0=gbt[:, 0:TT], scalar1=wd(0), scalar2=vcol(l, "ffn_bdw", ft),
                                                             op0=ALU.mult, op1=ALU.add), reads=[gbr, r_vec[l]], writes=[r_cb[b]])
                        for k in (1, 2):
                            fw.op(dve, lambda e: e.scalar_tensor_tensor(out=cb[b][:], in0=gbt[:, k:k + TT], scalar=wd(k), in1=cb[b][:],
                                                                        op0=ALU.mult, op1=ALU.add), reads=[gbr, r_cb[b], r_vec[l]], writes=[r_cb[b]])
                        fw.op(act, lambda e: e.activation(out=cb[b][:], in_=cb[b][:], func=AF.Gelu_apprx_tanh), reads=[r_cb[b]], writes=[r_cb[b]])
                        fw.op(dve, lambda e: e.tensor_tensor(out=actb[:, ft, :], in0=cb[b][:], in1=psv[:], op=ALU.mult),
                              reads=[r_cb[b], psvr], writes=[r_act[ft]])
                    if gat and j + 1 < NTL:
                        fw.gather(mtile[:].rearrange("p c n -> p (c n)"), mixT, gidx[:, j + 1:j + 2], reads=[r_mixT, r_gidx], writes=[mr])
                    if (not gat) and j + 1 < NT:
                        fw.dma(sp, mtile[:], tile_view(mixT, j + 1), reads=[r_mixT], writes=[mr])
                    for mt in range(16):
                        W, wr = load_w(w_down[l, mt], rot=wbig, ncol=NFT * 128)
                        ps, psr = PS.next()
                        for kc in range(NFT):
                            fw.op(pe, lambda e: e.matmul(ps[:], lhsT=W[:, kc * 128:(kc + 1) * 128], rhs=actb[:, kc, :],
                                                         start=(kc == 0), stop=(kc == NFT - 1)), reads=[wr, r_act[kc]], writes=[psr], inc=(kc == NFT - 1))
                        fw.op(dve, lambda e: e.tensor_tensor(out=xt[:, mt, :], in0=xt[:, mt, :], in1=ps[:], op=ALU.add),
                              reads=[psr, xr[mt]], writes=[xr[mt]])
                    if last:
                        def _of(c):
                            t_, r_ = ostg.next()
                            return t_[:], r_

                        def _pf(c, oap, oreg):
                            fw.dma(sp, outT[c * 128:(c + 1) * 128, cs], oap, reads=[oreg], writes=[r_out])
                        rmsnorm(nb, xt, xr, vcol(l, "g_fin", 0, 16), None, None, out_fn=_of, post_fn=_pf)
                    else:
                        fw.dma(sp, tile_view(xs, j), xt[:], reads=xr, writes=[r_xs])
        fw.finish()
    return nc


def _cols(v, n):
    return np.ascontiguousarray(np.asarray(v, np.float32).reshape(n, 128).T)


def _wtile(w, nk, nm):
    w = np.asarray(w, np.float32).reshape(nk, 128, nm, 128)
    return np.ascontiguousarray(w.transpose(2, 1, 0, 3).reshape(nm, 128, nk * 128))


def prep_weights(inp, depth=DEPTH):
    f = lambda k: np.asarray(inp[k], np.float32)
    out = {}
    out["w_in"] = np.stack([_wtile(f("w_in")[l], 16, 24) for l in range(depth)])
    out["w_out"] = np.stack([_wtile(f("w_out")[l], 16, 16) for l in range(depth)])
    out["w_up"] = np.stack([_wtile(f("ffn_w_up")[l], 16, 86) for l in range(depth)])
    out["w_down"] = np.stack([_wtile(f("ffn_w_down")[l], NFT, 16) for l in range(depth)])
    out["w_glu"] = np.stack([_wtile(f("s5_w_glu")[l], 4, 4) for l in range(depth)])
    out["w_pw"] = np.stack([_wtile(f("cv_w_pw")[l], 4, 4) for l in range(depth)])

    def blk(w):
        o = np.zeros((4, 128, 128), np.float32)
        for h in range(8):
            ct, hh = divmod(h, 2)
            o[ct, hh * 64:(hh + 1) * 64, hh * 64:(hh + 1) * 64] = w[h]
        return o
    out["w_lr"] = np.stack([blk(f("lru_w_r")[l]) for l in range(depth)])
    out["w_li"] = np.stack([blk(f("lru_w_i")[l]) for l in range(depth)])
    out["w_pool"] = np.ascontiguousarray(f("pool_w")[:depth])
    vec = np.zeros((depth, 128, NV), np.float32)
    invc = np.zeros((4, 16), np.float32)
    for gi in range(4):
        invc[gi] = 1.0 / np.minimum(np.arange(16) + 1.0, 2.0 ** (gi + 1))
    for l in range(depth):
        def put(name, arr):
            arr = np.asarray(arr, np.float32)
            vec[l, :, VOFF[name]:VOFF[name] + arr.shape[1]] = arr
        put("g_mix", _cols(f("norm_mix_g")[l], 16))
        put("g_ffn", _cols(f("norm_ffn_g")[l], 16))
        put("g_fin", _cols(f("norm_final_g"), 16))
        put("s5_d", _cols(f("s5_d")[l], 4))
        put("s5_bglu", _cols(f("s5_b_glu")[l], 4))
        wdw = f("cv_w_dw")[l]
        put("cv_wdw", wdw.reshape(31, 4, 128).transpose(2, 1, 0).reshape(128, 124))
        put("cv_bdw", _cols(f("cv_b_dw")[l], 4))
        put("cv_lng", _cols(f("cv_ln_g")[l], 4))
        put("cv_lnb", _cols(f("cv_ln_b")[l], 4))
        put("cv_bpw", _cols(f("cv_b_pw")[l], 4))
        put("lru_wc", f("lru_w_conv")[l].reshape(4, 4, 128).transpose(2, 1, 0).reshape(128, 16))
        put("lru_bc", _cols(f("lru_b_conv")[l], 4))
        put("lru_br", _cols(f("lru_b_r")[l], 4))
        put("lru_bi", _cols(f("lru_b_i")[l], 4))
        put("lru_lam", _cols(f("lru_lam")[l], 4))
        put("pool_scale", _cols(f("pool_scale")[l], 4))
        put("ffn_wdw", f("ffn_w_dw")[l].reshape(3, NFT, 128).transpose(2, 1, 0).reshape(128, NFT * 3))
        put("ffn_bdw", _cols(f("ffn_b_dw")[l], NFT))
        put("invc", np.broadcast_to(invc.reshape(1, 64), (128, 64)))
    out["vec"] = vec
    s5_pp = np.zeros((depth, 128, 48), np.float32)
    s5_row = np.zeros((depth, 3, 2048), np.float32)
    s5_bz = np.zeros((depth, 2, 128, 2048), np.float32)
    s5_cz = np.zeros((depth, 2, 128, 2048), np.float32)
    for l in range(depth):
        lre, lim, lst = f("s5_lam_re")[l], f("s5_lam_im")[l], f("s5_log_step")[l]
        s5_pp[l, :, 0:16] = lre.reshape(16, 128).T
        s5_pp[l, :, 16:32] = lim.reshape(16, 128).T
        s5_pp[l, :, 32:48] = np.repeat(lst, 64).reshape(16, 128).T
        s5_row[l, 0] = lre.reshape(-1)
        s5_row[l, 1] = lim.reshape(-1)
        s5_row[l, 2] = np.repeat(lst, 64)
        for k, (bk, ck) in enumerate([("s5_b_re", "s5_c_re"), ("s5_b_im", "s5_c_im")]):
            B = f(bk)[l]
            C = f(ck)[l]
            for g in range(32):
                q, gi = divmod(g, 2)
                gl = g % 8
                s5_bz[l, k, gl * 16:(gl + 1) * 16, q * 128 + gi * 64:q * 128 + (gi + 1) * 64] = B[g].T
                s5_cz[l, k, gi * 64:(gi + 1) * 64, q * 128 + gl * 16:q * 128 + (gl + 1) * 16] = C[g].T
    out["s5_pp"], out["s5_row"], out["s5_bz"], out["s5_cz"] = s5_pp, s5_row, s5_bz, s5_cz
    out["ident"] = np.eye(128, dtype=np.float32)
    return out


_NC_CACHE = {}


def core_aux(r, L):
    nt = L // TT
    ntl = nt // 2
    gi = np.zeros((128, 64), np.int32)
    for i in range(ntl):
        gi[:, i] = (r * ntl + i) * 128 + np.arange(128)
    return gi, np.full((128, 1), float(r), np.float32)


def kernel(**inputs):
    x = np.asarray(inputs["x"], np.float32)
    nb, L, _ = x.shape
    wts = prep_weights(inputs)
    if L not in _NC_CACHE:
        _NC_CACHE[L] = build(L)
    nc = _NC_CACHE[L]
    in_maps = []
    for c in range(2 * nb):
        m = dict(wts)
        m["xT"] = np.ascontiguousarray(x[c // 2].T)
        m["gidx"], m["hflag"] = core_aux(c % 2, L)
        in_maps.append(m)
    res = run_bass_kernel_spmd(nc, in_maps, core_ids=list(range(2 * nb)))
    out = np.empty((nb, L, D), np.float32)
    for c in range(2 * nb):
        r = c % 2
        out[c // 2, r * (L // 2):(r + 1) * (L // 2), :] = res.results[c]["outT"].T
    return out.astype(np.float32)
```

```python
import contextlib
import numpy as np
import concourse.bass as bass
import concourse.mybir as mybir
from concourse.bass_utils import run_bass_kernel_spmd

F32 = mybir.dt.float32
BF16 = mybir.dt.bfloat16
ALU = mybir.AluOpType
AF = mybir.ActivationFunctionType

D = 2048
NB = 4
SEQ = 4096
DEPTH = 2
FF = 5504
NFT = FF // 128
EPS = 1e-6
TT = 512
PI = float(np.pi)

VOFF = {}
_o = 0
for _n, _w in [("g_mix", 16), ("g_ffn", 16), ("g_fin", 16), ("s5_d", 4), ("s5_bglu", 4), ("cv_wdw", 124),
               ("cv_bdw", 4), ("cv_lng", 4), ("cv_lnb", 4), ("cv_bpw", 4), ("lru_wc", 16), ("lru_bc", 4),
               ("lru_br", 4), ("lru_bi", 4), ("lru_lam", 4), ("pool_scale", 4), ("ffn_wdw", 129),
               ("ffn_bdw", 43), ("invc", 64)]:
    VOFF[_n] = _o
    _o += _w
NV = _o


class Reg:
    __slots__ = ("w", "r")

    def __init__(self):
        self.w = None
        self.r = {}


class Eng:
    def __init__(self, name, e, sem):
        self.name, self.e, self.sem, self.cnt, self.seen = name, e, sem, 0, {}


class FW:
    def __init__(self, nc, es, ndq=20):
        self.nc = nc
        mk = lambda n: es.enter_context(nc.semaphore(n))
        self.pe = Eng("pe", nc.tensor, mk("s_pe"))
        self.act = Eng("act", nc.scalar, mk("s_act"))
        self.dve = Eng("dve", nc.vector, mk("s_dve"))
        self.pool = Eng("pool", nc.gpsimd, mk("s_pool"))
        self.sp = Eng("sp", nc.sync, mk("s_sp"))
        self.dq = {}
        self.dqi = {}
        for e in (self.sp, self.pool):
            self.dq[e.name] = [[mk("d_%s%d" % (e.name, i)), 0] for i in range(ndq)]
            self.dqi[e.name] = 0

    def _deps(self, eng, reads, writes):
        deps = {}

        def add(tok):
            if tok is None:
                return
            k, sem, val = tok
            if eng.name == "pe" and k == "pe":
                return
            if k not in deps or deps[k][1] < val:
                deps[k] = (sem, val)

        for r in reads:
            add(r.w)
        for w in writes:
            add(w.w)
            for tok in w.r.values():
                add(tok)
        return deps

    def _wait(self, eng, deps):
        for k, (sem, val) in deps.items():
            if eng.seen.get(k, 0) < val:
                eng.e.wait_ge(sem, val)
                eng.seen[k] = val

    def op(self, eng, fn, reads=(), writes=(), inc=True):
        self._wait(eng, self._deps(eng, reads, writes))
        inst = fn(eng.e)
        if inc:
            eng.cnt += 1
            inst.then_inc(eng.sem, 1)
            tok = (eng.name, eng.sem, eng.cnt)
        else:
            tok = (eng.name, eng.sem, eng.cnt + 1)
        for r in reads:
            old = r.r.get(eng.name)
            if old is None or old[2] < tok[2]:
                r.r[eng.name] = tok
        for w in writes:
            w.w = tok
            w.r = {}
        return inst

    def dma(self, eng, out, in_, reads=(), writes=()):
        pool = self.dq[eng.name]
        i = self.dqi[eng.name]
        self.dqi[eng.name] = (i + 1) % len(pool)
        ent = pool[i]
        key = "d_%s%d" % (eng.name, i)
        deps = self._deps(eng, reads, writes)
        if ent[1] > 0:
            deps[key] = (ent[0], ent[1])
        self._wait(eng, deps)
        ent[1] += 16
        eng.e.dma_start(out=out, in_=in_).then_inc(ent[0], 16)
        tok = (key, ent[0], ent[1])
        for r in reads:
            r.r[key] = tok
        for w in writes:
            w.w = tok
            w.r = {}

    def gather(self, out, in_, idx_ap, reads=(), writes=()):
        eng = self.pool
        pool = self.dq[eng.name]
        i = self.dqi[eng.name]
        self.dqi[eng.name] = (i + 1) % len(pool)
        ent = pool[i]
        key = "d_%s%d" % (eng.name, i)
        deps = self._deps(eng, reads, writes)
        if ent[1] > 0:
            deps[key] = (ent[0], ent[1])
        self._wait(eng, deps)
        ent[1] += 16
        eng.e.indirect_dma_start(out=out, out_offset=None, in_=in_,
                                 in_offset=bass.IndirectOffsetOnAxis(ap=idx_ap, axis=0)).then_inc(ent[0], 16)
        tok = (key, ent[0], ent[1])
        for r in reads:
            r.r[key] = tok
        for w in writes:
            w.w = tok
            w.r = {}

    def barrier(self):
        engs = [self.pe, self.act, self.dve, self.pool, self.sp]
        for a in engs:
            deps = {}
            for b in engs:
                if b is not a and b.cnt > 0:
                    deps[b.name] = (b.sem, b.cnt)
            for name, pool in self.dq.items():
                for i, (sem, val) in enumerate(pool):
                    if val > 0:
                        deps["d_%s%d" % (name, i)] = (sem, val)
            self._wait(a, deps)

    def finish(self):
        for name, pool in self.dq.items():
            for sem, val in pool:
                if val > 0:
                    self.sp.e.wait_ge(sem, val)


class Rot:
    def __init__(self, items):
        self.items = items
        self.i = 0

    def next(self):
        it = self.items[self.i]
        self.i = (self.i + 1) % len(self.items)
        return it


def build(L=SEQ, depth=DEPTH, debug=False, split=True):
    nc = bass.Bass("TRN2", target_bir_lowering=False)
    NT = L // TT
    NP = L // TT
    dt_in = lambda name, shape: nc.dram_tensor(name, shape, F32, kind="ExternalInput").ap()
    skind = "ExternalOutput" if debug else "Internal"
    xT = dt_in("xT", [D, L])
    w_in = dt_in("w_in", [depth, 24, 128, 2048])
    w_out = dt_in("w_out", [depth, 16, 128, 2048])
    w_up = dt_in("w_up", [depth, 86, 128, 2048])
    w_down = dt_in("w_down", [depth, 16, 128, NFT * 128])
    w_glu = dt_in("w_glu", [depth, 4, 128, 512])
    w_pw = dt_in("w_pw", [depth, 4, 128, 512])
    w_lr = dt_in("w_lr", [depth, 4, 128, 128])
    w_li = dt_in("w_li", [depth, 4, 128, 128])
    w_pool = dt_in("w_pool", [depth, 4, 128, 128])
    vec = dt_in("vec", [depth, 128, NV])
    s5_pp = dt_in("s5_pp", [depth, 128, 48])
    s5_row = dt_in("s5_row", [depth, 3, 2048])
    s5_bz = dt_in("s5_bz", [depth, 2, 128, 2048])
    s5_cz = dt_in("s5_cz", [depth, 2, 128, 2048])
    ident_d = dt_in("ident", [128, 128])
    LO = L // 2 if split else L
    NTL = NT // 2 if split else NT
    idx_d = nc.dram_tensor("gidx", [128, 64], mybir.dt.int32, kind="ExternalInput").ap()
    flag_d = dt_in("hflag", [128, 1])
    outT = nc.dram_tensor("outT", [D, LO], F32, kind="ExternalOutput").ap()
    projU = nc.dram_tensor("projU", [512, L], BF16, kind=skind).ap()
    projR = nc.dram_tensor("projR", [2560, L], F32, kind=skind).ap()
    mixT = nc.dram_tensor("mixT", [NT * 128, 16 * TT], BF16, kind=skind).ap()
    xs = nc.dram_tensor("xs", [NT * 128, 16 * TT], F32, kind=skind).ap()
    tile_view = lambda ap, j: ap[j * 128:(j + 1) * 128, :].rearrange("p (c n) -> p c n", n=TT)
    chunk_view = lambda ap, j, ci: ap[j * 128:(j + 1) * 128, ci * TT:(ci + 1) * TT]
    rows_view = lambda ap, ci: ap.rearrange("(j p) (c n) -> p j c n", p=128, n=TT)[:, :, ci, :]
    r_projU, r_projR, r_mixT, r_xs, r_out = Reg(), Reg(), Reg(), Reg(), Reg()

    es = contextlib.ExitStack()
    with es:
        fw = FW(nc, es)
        pe, act, dve, pool, sp = fw.pe, fw.act, fw.dve, fw.pool, fw.sp
        cnt = [0]

        def sbt(st, shape, dt, name=None):
            cnt[0] += 1
            return st.enter_context(nc.sbuf_tensor(name or ("t%d" % cnt[0]), shape, dt))

        PS = Rot([(es.enter_context(nc.psum_tensor("ps%d" % i, [128, TT], F32)), Reg()) for i in range(8)])
        ones_bf = sbt(es, [128, 128], BF16)
        ones_f = sbt(es, [128, 128], F32)
        epsT = sbt(es, [128, 1], F32)
        oneT = sbt(es, [128, 1], F32)
        hpiT = sbt(es, [128, 1], F32)
        ident_f = sbt(es, [128, 128], F32)
        ident_b = sbt(es, [128, 128], BF16)
        vecT = [sbt(es, [128, NV], F32) for _ in range(depth)]
        r_const = Reg()
        fw.op(dve, lambda e: e.memset(ones_bf[:], 1.0), writes=[r_const])
        fw.op(dve, lambda e: e.memset(ones_f[:], 1.0), writes=[r_const])
        fw.op(dve, lambda e: e.memset(epsT[:], EPS), writes=[r_const])
        fw.op(dve, lambda e: e.memset(oneT[:], 1.0), writes=[r_const])
        fw.op(dve, lambda e: e.memset(hpiT[:], PI / 2), writes=[r_const])
        r_ident = Reg()
        fw.dma(sp, ident_f[:], ident_d, writes=[r_ident])
        fw.op(dve, lambda e: e.tensor_copy(out=ident_b[:], in_=ident_f[:]), reads=[r_ident], writes=[r_const])
        gidx = sbt(es, [128, 64], mybir.dt.int32)
        flagT = sbt(es, [128, 1], F32)
        r_gidx = Reg()
        fw.dma(sp, gidx[:], idx_d, writes=[r_gidx])
        fw.dma(sp, flagT[:], flag_d, writes=[r_gidx])
        r_vec = [Reg() for _ in range(depth)]
        for l in range(depth):
            fw.dma(sp, vecT[l][:], vec[l], writes=[r_vec[l]])

        def vcol(l, name, i=0, n=1):
            o = VOFF[name] + i
            return vecT[l][:, o:o + n]

        NWS = 5
        wsm = Rot([(sbt(es, [128, 2048], BF16), Reg()) for _ in range(NWS)])

        def load_w(dram_ap, rot=None, ncol=2048):
            t, r = (rot or wsm).next()
            fw.dma(pool, t[:, 0:ncol], dram_ap, writes=[r])
            return t, r

        evac_flip = [0]

        def evac(out_ap, ps_ap, reads, writes, bias=None, scale=None, func=None, eng=None):
            if func is None and bias is None and scale is None and eng is None:
                evac_flip[0] ^= 1
                eng = act if evac_flip[0] else dve
            if eng is dve:
                fw.op(dve, lambda e: e.tensor_copy(out=out_ap, in_=ps_ap), reads=reads, writes=writes)
            else:
                kw = {}
                if bias is not None:
                    kw["bias"] = bias
                if scale is not None:
                    kw["scale"] = scale
                fw.op(act, lambda e: e.activation(out=out_ap, in_=ps_ap, func=func or AF.Identity, **kw),
                      reads=reads, writes=writes)

        def rmsnorm(st_bufs, xt, xr, gcols, hout, hregs, out_fn=None, post_fn=None, w=TT):
            sq, sqr, lnv, lnr, rstd, rsr = st_bufs
            ps, psr = PS.next()
            for c in range(16):
                b = c % 2
                fw.op(act, lambda e: e.activation(out=sq[b][:, 0:w], in_=xt[:, c, :], func=AF.Square),
                      reads=[xr[c]], writes=[sqr[b]])
                fw.op(pe, lambda e: e.matmul(ps[:, 0:w], lhsT=ones_bf[:], rhs=sq[b][:, 0:w], start=(c == 0), stop=(c == 15)),
                      reads=[sqr[b], r_const], writes=[psr], inc=True)
            fw.op(act, lambda e: e.activation(out=lnv[:, 0:w], in_=ps[:, 0:w], func=AF.Ln, scale=1.0 / D, bias=epsT[:, 0:1]),
                  reads=[psr, r_const], writes=[lnr])
            fw.op(act, lambda e: e.activation(out=rstd[:, 0:w], in_=lnv[:, 0:w], func=AF.Exp, scale=-0.5),
                  reads=[lnr], writes=[rsr])
            for c in range(16):
                if out_fn is None:
                    oap, oreg = hout[:, c, :], hregs[c]
                else:
                    oap, oreg = out_fn(c)
                fw.op(dve, lambda e: e.scalar_tensor_tensor(out=oap, in0=xt[:, c, :], scalar=gcols[:, c:c + 1],
                                                            in1=rstd[:, 0:w], op0=ALU.mult, op1=ALU.mult),
                      reads=[xr[c], rsr], writes=[oreg])
                if post_fn is not None:
                    post_fn(c, oap, oreg)

        def norm_bufs(st):
            sq = [sbt(st, [128, TT], BF16) for _ in range(2)]
            return (sq, [Reg(), Reg()], sbt(st, [128, TT], F32), Reg(), sbt(st, [128, TT], F32), Reg())

        xview = lambda ap: ap.rearrange("(c p) l -> p c l", p=128)

        for l in range(depth):
            fw.barrier()
            with contextlib.ExitStack() as st:
                nb = norm_bufs(st)
                xts = [(sbt(st, [128, 16, TT], F32), [Reg() for _ in range(16)]) for _ in range(1)]
                hts = [(sbt(st, [128, 16, TT], BF16), [Reg() for _ in range(16)]) for _ in range(2)]
                stg_b = Rot([(sbt(st, [128, TT], BF16), Reg()) for _ in range(3)])
                stg_f = Rot([(sbt(st, [128, TT], F32), Reg()) for _ in range(3)])
                wres = sbt(st, [128, 24, 2048], BF16)
                r_wres = [Reg() for _ in range(24)]
                for mt in range(24):
                    fw.dma(pool, wres[:, mt, :], w_in[l, mt], writes=[r_wres[mt]])
                def a_load(j):
                    xt, xr = xts[0]
                    if l == 0:
                        fw.dma(sp, xt[:], xview(xT)[:, :, j * TT:(j + 1) * TT], writes=xr)
                    else:
                        fw.dma(sp, xt[:], tile_view(xs, j), reads=[r_xs], writes=xr)

                def a_norm(j):
                    xt, xr = xts[0]
                    ht, hr = hts[j % 2]
                    rmsnorm(nb, xt, xr, vcol(l, "g_mix", 0, 16), ht, hr)

                a_load(0)
                a_norm(0)
                for j in range(NT):
                    cs = slice(j * TT, (j + 1) * TT)
                    ht, hr = hts[j % 2]
                    if j + 1 < NT:
                        a_load(j + 1)
                    for mt in range(24):
                        if mt == 12 and j + 1 < NT:
                            a_norm(j + 1)
                        W, wr = wres[:, mt, :], r_wres[mt]
                        ps, psr = PS.next()
                        for kc in range(16):
                            fw.op(pe, lambda e: e.matmul(ps[:], lhsT=W[:, kc * 128:(kc + 1) * 128], rhs=ht[:, kc, :],
                                                         start=(kc == 0), stop=(kc == 15)),
                                  reads=[wr, hr[kc]], writes=[psr], inc=(kc == 15))
                        if mt < 4:
                            sg, sr = stg_b.next()
                            evac(sg[:], ps[:], [psr], [sr])
                            fw.dma(sp, projU[mt * 128:(mt + 1) * 128, cs], sg[:], reads=[sr], writes=[r_projU])
                        else:
                            sg, sr = stg_f.next()
                            evac(sg[:], ps[:], [psr], [sr])
                            fw.dma(sp, projR[(mt - 4) * 128:(mt - 3) * 128, cs], sg[:], reads=[sr], writes=[r_projR])

            fw.barrier()
            with contextlib.ExitStack() as st:
                xl = sbt(st, [128, 4 + L], F32); r_xl = Reg()
                gl = sbt(st, [128, L], F32); r_gl = Reg()
                xc = sbt(st, [128, L], F32); r_xc = Reg()
                rb = sbt(st, [128, L], F32); r_rb = Reg()
                ib = sbt(st, [128, L], F32); r_ib = Reg()
                tmp = sbt(st, [128, L], F32); r_tmp = Reg()
                xcb = sbt(st, [128, L], BF16); r_xcb = Reg()
                ybf = sbt(st, [128, L], BF16); r_y = Reg()
                cc = sbt(st, [128, 4], F32); r_cc = Reg()
                ce = sbt(st, [128, 4], F32); r_ce = Reg()
                fw.op(dve, lambda e: e.memset(xl[:, 0:4], 0.0), writes=[r_xl])
                fw.op(act, lambda e: e.activation(out=ce[:], in_=vcol(l, "lru_lam", 0, 4), func=AF.Exp, scale=-1.0),
                      reads=[r_vec[l]], writes=[r_ce])
                fw.op(act, lambda e: e.activation(out=cc[:], in_=ce[:], func=AF.Ln, bias=oneT[:, 0:1]),
                      reads=[r_ce, r_const], writes=[r_cc])
                fw.op(dve, lambda e: e.tensor_scalar(out=cc[:], in0=cc[:], scalar1=-8.0, scalar2=None, op0=ALU.mult),
                      reads=[r_cc], writes=[r_cc])
                for ct in range(4):
                    rx = 1024 + ct * 128
                    rg = 1536 + ct * 128
                    fw.dma(sp, xl[:, 4:], projR[rx:rx + 128, :], reads=[r_projR], writes=[r_xl])
                    fw.dma(sp, gl[:], projR[rg:rg + 128, :], reads=[r_projR], writes=[r_gl])
                    wc = lambda k: vcol(l, "lru_wc", ct * 4 + k)
                    fw.op(dve, lambda e: e.tensor_scalar(out=xc[:], in0=xl[:, 1:1 + L], scalar1=wc(0), scalar2=vcol(l, "lru_bc", ct),
                                                         op0=ALU.mult, op1=ALU.add), reads=[r_xl, r_vec[l]], writes=[r_xc])
                    for k in range(1, 4):
                        fw.op(dve, lambda e: e.scalar_tensor_tensor(out=xc[:], in0=xl[:, 1 + k:1 + k + L], scalar=wc(k), in1=xc[:],
                                                                    op0=ALU.mult, op1=ALU.add), reads=[r_xl, r_xc], writes=[r_xc])
                    fw.op(act, lambda e: e.copy(out=xcb[:], in_=xc[:]), reads=[r_xc], writes=[r_xcb])
                    Wr, wrr = load_w(w_lr[l, ct], ncol=128)
                    Wi, wir = load_w(w_li[l, ct], ncol=128)
                    for p in range(NP):
                        cs = slice(p * TT, (p + 1) * TT)
                        ps, psr = PS.next()
                        fw.op(pe, lambda e: e.matmul(ps[:], lhsT=Wr[:, 0:128], rhs=xcb[:, cs], start=True, stop=True),
                              reads=[wrr, r_xcb], writes=[psr])
                        evac(rb[:, cs], ps[:], [psr, r_vec[l]], [r_rb], bias=vcol(l, "lru_br", ct), func=AF.Sigmoid)
                        ps, psr = PS.next()
                        fw.op(pe, lambda e: e.matmul(ps[:], lhsT=Wi[:, 0:128], rhs=xcb[:, cs], start=True, stop=True),
                              reads=[wir, r_xcb], writes=[psr])
                        evac(ib[:, cs], ps[:], [psr, r_vec[l]], [r_ib], bias=vcol(l, "lru_bi", ct), func=AF.Sigmoid)
                    fw.op(act, lambda e: e.activation(out=rb[:], in_=rb[:], func=AF.Exp, scale=cc[:, ct:ct + 1]),
                          reads=[r_rb, r_cc], writes=[r_rb])
                    fw.op(act, lambda e: e.activation(out=tmp[:], in_=rb[:], func=AF.Square), reads=[r_rb], writes=[r_tmp])
                    fw.op(act, lambda e: e.activation(out=tmp[:], in_=tmp[:], func=AF.Sqrt, scale=-1.0, bias=oneT[:, 0:1]),
                          reads=[r_tmp, r_const], writes=[r_tmp])
                    fw.op(dve, lambda e: e.tensor_tensor(out=ib[:], in0=ib[:], in1=xc[:], op=ALU.mult), reads=[r_ib, r_xc], writes=[r_ib])
                    fw.op(dve, lambda e: e.tensor_tensor(out=ib[:], in0=ib[:], in1=tmp[:], op=ALU.mult), reads=[r_ib, r_tmp], writes=[r_ib])
                    fw.op(dve, lambda e: e.tensor_tensor_scan(out=xc[:], data0=rb[:], data1=ib[:], initial=0.0,
                                                              op0=ALU.mult, op1=ALU.add), reads=[r_rb, r_ib], writes=[r_xc])
                    fw.op(act, lambda e: e.activation(out=gl[:], in_=gl[:], func=AF.Gelu_apprx_tanh), reads=[r_gl], writes=[r_gl])
                    fw.op(dve, lambda e: e.tensor_tensor(out=ybf[:], in0=xc[:], in1=gl[:], op=ALU.mult), reads=[r_xc, r_gl], writes=[r_y])
                    fw.dma(sp, rows_view(mixT, 8 + ct), ybf[:].rearrange("p (j n) -> p j n", n=TT), reads=[r_y], writes=[r_mixT])

            fw.barrier()
            with contextlib.ExitStack() as st:
                vb = sbt(st, [128, L], F32); r_vb = Reg()
                gb = sbt(st, [128, L], F32); r_gb = Reg()
                hpad = sbt(st, [128, 32 + L], BF16); r_hp = Reg()
                diag = sbt(st, [128, 31, 128], BF16); r_dg = Reg()
                hc = sbt(st, [128, 4, L], F32); r_hc = [[Reg() for _ in range(NP)] for _ in range(4)]
                sqf = [sbt(st, [128, TT], F32) for _ in range(2)]; r_sqf = [Reg(), Reg()]
                mu = sbt(st, [128, TT], F32); r_mu = Reg()
                var = sbt(st, [128, TT], F32); r_var = Reg()
                rstd = sbt(st, [128, TT], F32); r_rstd = Reg()
                tn = [sbt(st, [128, TT], F32) for _ in range(2)]; r_tn = [Reg(), Reg()]
                sbf = sbt(st, [128, 4, TT], BF16); r_sbf = [Reg() for _ in range(4)]
                stg = Rot([(sbt(st, [128, TT], BF16), Reg()) for _ in range(3)])
                fw.op(dve, lambda e: e.memset(hpad[:, 0:32], 0.0), writes=[r_hp])
                pbufs = [sbt(st, [128, 16 + L], F32) for _ in range(3)]
                pbregs = [Reg() for _ in range(3)]
                pt16 = sbt(st, [128, 16], F32)
                r_pt16 = Reg()
                for b in range(3):
                    fw.op(dve, lambda e: e.memset(pbufs[b][:, 0:16], 0.0), writes=[pbregs[b]])
                def pool_gi(gi):
                    win = 2 ** (gi + 1)
                    row0 = 2048 + gi * 128
                    xp, xpr = pbufs[0], pbregs[0]
                    fw.dma(sp, xp[:, 16:], projR[row0:row0 + 128, :], reads=[r_projR], writes=[xpr])
                    cur, curr = xp, xpr
                    m = 1
                    k = 0
                    while m < win:
                        nx, nxr = pbufs[1 + (k % 2)], pbregs[1 + (k % 2)]
                        fw.op(dve, lambda e: e.tensor_tensor(out=nx[:, 16:], in0=cur[:, 16:], in1=cur[:, 16 - m:16 - m + L],
                                                             op=ALU.add), reads=[curr], writes=[nxr])
                        cur, curr = nx, nxr
                        m *= 2
                        k += 1
                    ob, r_pd = pbufs[1 + (k % 2)], pbregs[1 + (k % 2)]
                    pdfull = ob[:].bitcast(BF16)[:, 64:64 + L]
                    fw.op(dve, lambda e: e.scalar_tensor_tensor(out=pdfull, in0=cur[:, 16:], scalar=1.0 / win, in1=xp[:, 16:],
                                                                op0=ALU.mult, op1=ALU.subtract),
                          reads=[curr, xpr], writes=[r_pd])
                    fw.op(dve, lambda e: e.tensor_tensor(out=pt16[:], in0=cur[:, 16:32], in1=vcol(l, "invc", gi * 16, 16), op=ALU.mult),
                          reads=[curr, r_vec[l]], writes=[r_pt16])
                    fw.op(dve, lambda e: e.tensor_tensor(out=pdfull[:, 0:16], in0=pt16[:], in1=xp[:, 16:32], op=ALU.subtract),
                          reads=[r_pt16, xpr], writes=[r_pd])
                    W, wr = load_w(w_pool[l, gi], ncol=128)
                    for p in range(NP):
                        cs = slice(p * TT, (p + 1) * TT)
                        ps, psr = PS.next()
                        fw.op(pe, lambda e: e.matmul(ps[:], lhsT=W[:, 0:128], rhs=pdfull[:, cs], start=True, stop=True),
                              reads=[wr, r_pd], writes=[psr])
                        sg, sr = stg.next()
                        evac(sg[:], ps[:], [psr, r_vec[l]], [sr], scale=vcol(l, "pool_scale", gi))
                        fw.dma(sp, chunk_view(mixT, p, 12 + gi), sg[:], reads=[sr], writes=[r_mixT])


                for ct in range(4):
                    rv = ct * 128
                    rg = 512 + ct * 128
                    fw.dma(sp, vb[:], projR[rv:rv + 128, :], reads=[r_projR], writes=[r_vb])
                    fw.dma(sp, gb[:], projR[rg:rg + 128, :], reads=[r_projR], writes=[r_gb])
                    fw.op(act, lambda e: e.activation(out=gb[:], in_=gb[:], func=AF.Sigmoid), reads=[r_gb], writes=[r_gb])
                    fw.op(dve, lambda e: e.tensor_tensor(out=hpad[:, 32:], in0=vb[:], in1=gb[:], op=ALU.mult),
                          reads=[r_vb, r_gb], writes=[r_hp])
                    for k in range(31):
                        fw.op(dve, lambda e: e.tensor_scalar(out=diag[:, k, :], in0=ident_f[:], scalar1=vcol(l, "cv_wdw", ct * 31 + k),
                                                             scalar2=None, op0=ALU.mult), reads=[r_ident, r_vec[l]], writes=[r_dg])
                    for p in range(NP):
                        ps, psr = PS.next()
                        for k in range(31):
                            o = 2 + p * TT + k
                            fw.op(pe, lambda e: e.matmul(ps[:], lhsT=diag[:, k, :], rhs=hpad[:, o:o + TT], start=(k == 0), stop=(k == 30)),
                                  reads=[r_dg, r_hp], writes=[psr], inc=(k == 30))
                        evac(hc[:, ct, p * TT:(p + 1) * TT], ps[:], [psr, r_vec[l]], [r_hc[ct][p]], bias=vcol(l, "cv_bdw", ct))
                    pool_gi(ct)
                Wp = [load_w(w_pw[l, mt], ncol=512) for mt in range(4)]
                for p in range(NP):
                    cs = slice(p * TT, (p + 1) * TT)
                    ps1, ps1r = PS.next()
                    ps2, ps2r = PS.next()
                    for ct in range(4):
                        b = ct % 2
                        fw.op(pe, lambda e: e.matmul(ps1[:], lhsT=ones_f[:], rhs=hc[:, ct, cs], start=(ct == 0), stop=(ct == 3)),
                              reads=[r_hc[ct][p], r_const], writes=[ps1r], inc=(ct == 3))
                        fw.op(act, lambda e: e.activation(out=sqf[b][:], in_=hc[:, ct, cs], func=AF.Square),
                              reads=[r_hc[ct][p]], writes=[r_sqf[b]])
                        fw.op(pe, lambda e: e.matmul(ps2[:], lhsT=ones_f[:], rhs=sqf[b][:], start=(ct == 0), stop=(ct == 3)),
                              reads=[r_sqf[b], r_const], writes=[ps2r], inc=True)
                    fw.op(act, lambda e: e.mul(out=mu[:], in_=ps1[:], mul=1.0 / 512), reads=[ps1r], writes=[r_mu])
                    fw.op(dve, lambda e: e.tensor_tensor(out=var[:], in0=mu[:], in1=mu[:], op=ALU.mult), reads=[r_mu], writes=[r_var])
                    fw.op(dve, lambda e: e.scalar_tensor_tensor(out=var[:], in0=ps2[:], scalar=1.0 / 512, in1=var[:],
                                                                op0=ALU.mult, op1=ALU.subtract), reads=[ps2r, r_var], writes=[r_var])
                    fw.op(act, lambda e: e.activation(out=var[:], in_=var[:], func=AF.Ln, bias=epsT[:, 0:1]),
                          reads=[r_var, r_const], writes=[r_var])
                    fw.op(act, lambda e: e.activation(out=rstd[:], in_=var[:], func=AF.Exp, scale=-0.5), reads=[r_var], writes=[r_rstd])
                    for ct in range(4):
                        b = ct % 2
                        fw.op(dve, lambda e: e.tensor_tensor(out=tn[b][:], in0=hc[:, ct, cs], in1=mu[:], op=ALU.subtract),
                              reads=[r_hc[ct][p], r_mu], writes=[r_tn[b]])
                        fw.op(dve, lambda e: e.tensor_tensor(out=tn[b][:], in0=tn[b][:], in1=rstd[:], op=ALU.mult),
                              reads=[r_tn[b], r_rstd], writes=[r_tn[b]])
                        fw.op(act, lambda e: e.activation(out=sbf[:, ct, :], in_=tn[b][:], func=AF.Silu,
                                                          scale=vcol(l, "cv_lng", ct), bias=vcol(l, "cv_lnb", ct)),
                              reads=[r_tn[b], r_vec[l]], writes=[r_sbf[ct]])
                    for mt in range(4):
                        W, wr = Wp[mt]
                        ps, psr = PS.next()
                        for kc in range(4):
                            fw.op(pe, lambda e: e.matmul(ps[:], lhsT=W[:, kc * 128:(kc + 1) * 128], rhs=sbf[:, kc, :],
                                                         start=(kc == 0), stop=(kc == 3)),
                                  reads=[wr, r_sbf[kc]], writes=[psr], inc=(kc == 3))
                        sg, sr = stg.next()
                        evac(sg[:], ps[:], [psr, r_vec[l]], [sr], bias=vcol(l, "cv_bpw", mt))
                        fw.dma(sp, chunk_view(mixT, p, 4 + mt), sg[:], reads=[sr], writes=[r_mixT])

            fw.barrier()
            with contextlib.ExitStack() as st:
                LBr = sbt(st, [128, 2048], BF16); LBi = sbt(st, [128, 2048], BF16); r_LB = Reg()
                LCr = sbt(st, [128, 2048], BF16); nLCr = sbt(st, [128, 2048], BF16); LCi = sbt(st, [128, 2048], BF16); r_LC = Reg()
                ppc = sbt(st, [128, 16, 12], F32); pps = sbt(st, [128, 16, 12], F32); r_pp = Reg()
                rho_pp = sbt(st, [128, 16], F32)

                def lam_bar(st2, lre, lim, lst, n, r_in):
                    mk = lambda: sbt(st2, [128, n], F32)
                    stp, a, rho, th, c, s, t1, t2 = mk(), mk(), mk(), mk(), mk(), mk(), mk(), mk()
                    rr = Reg()
                    fw.op(act, lambda e: e.activation(out=stp[:], in_=lst, func=AF.Exp), reads=[r_in], writes=[rr])
                    fw.op(dve, lambda e: e.tensor_tensor(out=a[:], in0=lre, in1=stp[:], op=ALU.mult), reads=[r_in, rr], writes=[rr])
                    fw.op(act, lambda e: e.activation(out=rho[:], in_=a[:], func=AF.Exp), reads=[rr], writes=[rr])
                    fw.op(dve, lambda e: e.tensor_tensor(out=th[:], in0=lim, in1=stp[:], op=ALU.mult), reads=[r_in, rr], writes=[rr])
                    fw.op(act, lambda e: e.activation(out=s[:], in_=th[:], func=AF.Sin, scale=1.0 / 16), reads=[rr], writes=[rr])
                    fw.op(act, lambda e: e.activation(out=c[:], in_=th[:], func=AF.Sin, scale=1.0 / 16, bias=hpiT[:, 0:1]),
                          reads=[rr, r_const], writes=[rr])
                    for _ in range(4):
                        fw.op(dve, lambda e: e.tensor_tensor(out=t1[:], in0=c[:], in1=c[:], op=ALU.mult), reads=[rr], writes=[rr])
                        fw.op(dve, lambda e: e.tensor_tensor(out=t2[:], in0=s[:], in1=s[:], op=ALU.mult), reads=[rr], writes=[rr])
                        fw.op(dve, lambda e: e.scalar_tensor_tensor(out=s[:], in0=c[:], scalar=2.0, in1=s[:], op0=ALU.mult, op1=ALU.mult),
                              reads=[rr], writes=[rr])
                        fw.op(dve, lambda e: e.tensor_tensor(out=c[:], in0=t1[:], in1=t2[:], op=ALU.subtract), reads=[rr], writes=[rr])
                    return rho, c, s, rr, (t1, t2, a, th)

                with contextlib.ExitStack() as st2:
                    rowt = sbt(st2, [128, 3, 2048], F32); r_row = Reg()
                    fw.dma(sp, rowt[:], s5_row[l].partition_broadcast(128), writes=[r_row])
                    rho, c, s, rr, (t1, t2, t3, t4) = lam_bar(st2, rowt[:, 0, :], rowt[:, 1, :], rowt[:, 2, :], 2048, r_row)
                    lre, lim = rowt[:, 0, :], rowt[:, 1, :]
                    lbr = sbt(st2, [128, 2048], F32); lbi = sbt(st2, [128, 2048], F32)
                    qr = sbt(st2, [128, 2048], F32); qi = sbt(st2, [128, 2048], F32)
                    T = lambda o, a, b, op, rd=(): fw.op(dve, lambda e: e.tensor_tensor(out=o, in0=a, in1=b, op=op),
                                                         reads=[rr, r_row] + list(rd), writes=[rr])
                    T(lbr[:], rho[:], c[:], ALU.mult)
                    T(lbi[:], rho[:], s[:], ALU.mult)
                    fw.op(dve, lambda e: e.tensor_scalar(out=lbr[:], in0=lbr[:], scalar1=-1.0, scalar2=None, op0=ALU.add),
                          reads=[rr], writes=[rr])
                    T(t1[:], lre, lre, ALU.mult)
                    T(t2[:], lim, lim, ALU.mult)
                    T(t1[:], t1[:], t2[:], ALU.add)
                    fw.op(dve, lambda e: e.reciprocal(out=t1[:], in_=t1[:]), reads=[rr], writes=[rr])
                    T(t2[:], lbr[:], lre, ALU.mult)
                    T(t3[:], lbi[:], lim, ALU.mult)
                    T(t2[:], t2[:], t3[:], ALU.add)
                    T(qr[:], t2[:], t1[:], ALU.mult)
                    T(t2[:], lbi[:], lre, ALU.mult)
                    T(t3[:], lbr[:], lim, ALU.mult)
                    T(t2[:], t2[:], t3[:], ALU.subtract)
                    T(qi[:], t2[:], t1[:], ALU.mult)
                    bz = sbt(st2, [128, 2, 2048], F32); r_bz = Reg()
                    fw.dma(sp, bz[:], s5_bz[l].rearrange("k p n -> p k n"), writes=[r_bz])
                    T(t1[:], qr[:], bz[:, 0, :], ALU.mult, [r_bz])
                    T(t2[:], qi[:], bz[:, 1, :], ALU.mult, [r_bz])
                    fw.op(dve, lambda e: e.tensor_tensor(out=LBr[:], in0=t1[:], in1=t2[:], op=ALU.subtract), reads=[rr], writes=[rr, r_LB])
                    T(t1[:], qr[:], bz[:, 1, :], ALU.mult, [r_bz])
                    T(t2[:], qi[:], bz[:, 0, :], ALU.mult, [r_bz])
                    fw.op(dve, lambda e: e.tensor_tensor(out=LBi[:], in0=t1[:], in1=t2[:], op=ALU.add), reads=[rr], writes=[rr, r_LB])
                    fw.dma(sp, bz[:], s5_cz[l].rearrange("k p n -> p k n"), reads=[], writes=[r_bz, rr])
                    fw.op(act, lambda e: e.copy(out=LCr[:], in_=bz[:, 0, :]), reads=[r_bz], writes=[r_LC])
                    fw.op(act, lambda e: e.mul(out=LCi[:], in_=bz[:, 1, :], mul=-1.0), reads=[r_bz], writes=[r_LC])
                    fw.op(act, lambda e: e.mul(out=nLCr[:], in_=bz[:, 0, :], mul=-1.0), reads=[r_bz], writes=[r_LC])
                    ppt = sbt(st2, [128, 48], F32); r_ppt = Reg()
                    fw.dma(sp, ppt[:], s5_pp[l], writes=[r_ppt])
                    rho2, c2, s2, rr2, (u1, u2, _, _) = lam_bar(st2, ppt[:, 0:16], ppt[:, 16:32], ppt[:, 32:48], 16, r_ppt)
                    fw.op(dve, lambda e: e.tensor_copy(out=rho_pp[:], in_=rho2[:]), reads=[rr2], writes=[r_pp])
                    fw.op(dve, lambda e: e.tensor_copy(out=ppc[:, :, 0], in_=c2[:]), reads=[rr2], writes=[r_pp])
                    fw.op(dve, lambda e: e.tensor_copy(out=pps[:, :, 0], in_=s2[:]), reads=[rr2], writes=[r_pp])
                    for k in range(11):
                        fw.op(dve, lambda e: e.tensor_tensor(out=u1[:], in0=ppc[:, :, k], in1=ppc[:, :, k], op=ALU.mult), reads=[r_pp, rr2], writes=[rr2])
                        fw.op(dve, lambda e: e.tensor_tensor(out=u2[:], in0=pps[:, :, k], in1=pps[:, :, k], op=ALU.mult), reads=[r_pp, rr2], writes=[rr2])
                        fw.op(dve, lambda e: e.tensor_tensor(out=ppc[:, :, k + 1], in0=u1[:], in1=u2[:], op=ALU.subtract), reads=[rr2], writes=[r_pp])
                        fw.op(dve, lambda e: e.scalar_tensor_tensor(out=pps[:, :, k + 1], in0=ppc[:, :, k], scalar=2.0, in1=pps[:, :, k],
                                                                    op0=ALU.mult, op1=ALU.mult), reads=[r_pp], writes=[r_pp])

                fw.barrier()
                tcb = sbt(st, [128, L], F32); tsb = sbt(st, [128, L], F32); r_tab = Reg()
                wre = sbt(st, [128, L], F32); wim = sbt(st, [128, L], F32)
                r_wre = [Reg() for _ in range(NP)]; r_wim = [Reg() for _ in range(NP)]
                tq = [sbt(st, [128, TT], F32) for _ in range(8)]; r_tq = [Reg() for _ in range(8)]
                vq = [sbt(st, [128, TT], BF16) for _ in range(8)]; r_vq = [Reg() for _ in range(8)]
                ub = sbt(st, [128, L], BF16); r_ub = Reg()
                yacc = sbt(st, [128, L], F32); r_ya = [Reg() for _ in range(NP)]
                G = sbt(st, [128, 4, L], BF16); r_G = [[Reg() for _ in range(NP)] for _ in range(4)]
                ytmp = [sbt(st, [128, TT], F32) for _ in range(2)]; r_yt = [Reg(), Reg()]
                stg = Rot([(sbt(st, [128, TT], BF16), Reg()) for _ in range(3)])
                for pt in range(4):
                    fw.dma(sp, ub[:], projU[pt * 128:(pt + 1) * 128, :], reads=[r_projU], writes=[r_ub])
                    for qq in range(4):
                        q = pt * 4 + qq
                        qs = slice(q * 128, (q + 1) * 128)
                        fw.op(dve, lambda e: e.memset(tcb[:, 0:1], 1.0), writes=[r_tab])
                        fw.op(dve, lambda e: e.memset(tsb[:, 0:1], 0.0), writes=[r_tab])
                        n = 1
                        k = 0
                        while n < L:
                            cr = ppc[:, q, k:k + 1]
                            ci = pps[:, q, k:k + 1]
                            tmpa, tmpb = wre, wim
                            if n >= 64:
                                fw.op(act, lambda e: e.activation(out=tmpa[:, 0:n], in_=tsb[:, 0:n], func=AF.Copy, scale=ci),
                                      reads=[r_tab, r_pp], writes=r_wre)
                                fw.op(act, lambda e: e.activation(out=tmpb[:, 0:n], in_=tsb[:, 0:n], func=AF.Copy, scale=cr),
                                      reads=[r_tab, r_pp], writes=r_wim)
                            else:
                                fw.op(dve, lambda e: e.tensor_scalar(out=tmpa[:, 0:n], in0=tsb[:, 0:n], scalar1=ci, scalar2=None, op0=ALU.mult),
                                      reads=[r_tab, r_pp], writes=r_wre)
                                fw.op(dve, lambda e: e.tensor_scalar(out=tmpb[:, 0:n], in0=tsb[:, 0:n], scalar1=cr, scalar2=None, op0=ALU.mult),
                                      reads=[r_tab, r_pp], writes=r_wim)
                            fw.op(dve, lambda e: e.scalar_tensor_tensor(out=tsb[:, n:2 * n], in0=tcb[:, 0:n], scalar=ci, in1=tmpb[:, 0:n],
                                                                        op0=ALU.mult, op1=ALU.add), reads=[r_tab, r_pp] + r_wre + r_wim, writes=[r_tab])
                            fw.op(dve, lambda e: e.scalar_tensor_tensor(out=tcb[:, n:2 * n], in0=tcb[:, 0:n], scalar=cr, in1=tmpa[:, 0:n],
                                                                        op0=ALU.mult, op1=ALU.subtract), reads=[r_tab, r_pp] + r_wre + r_wim, writes=[r_tab])
                            n *= 2
                            k += 1
                        for p in range(NP):
                            cs = slice(p * TT, (p + 1) * TT)
                            psr_, psrr = PS.next()
                            psi_, psir = PS.next()
                            fw.op(pe, lambda e: e.matmul(psr_[:], lhsT=LBr[:, qs], rhs=ub[:, cs], start=True, stop=True),
                                  reads=[r_LB, r_ub], writes=[psrr])
                            fw.op(pe, lambda e: e.matmul(psi_[:], lhsT=LBi[:, qs], rhs=ub[:, cs], start=True, stop=True),
                                  reads=[r_LB, r_ub], writes=[psir])
                            TT_ = lambda o, orr, a, ar, b, op: fw.op(dve, lambda e: e.tensor_tensor(out=o, in0=a, in1=b, op=op),
                                                                     reads=[ar, r_tab], writes=[orr])
                            tb = (p % 2) * 4
                            TT_(tq[tb][:], r_tq[tb], psr_[:], psrr, tcb[:, cs], ALU.mult)
                            TT_(tq[tb + 1][:], r_tq[tb + 1], psi_[:], psir, tsb[:, cs], ALU.mult)
                            TT_(tq[tb + 2][:], r_tq[tb + 2], psi_[:], psir, tcb[:, cs], ALU.mult)
                            TT_(tq[tb + 3][:], r_tq[tb + 3], psr_[:], psrr, tsb[:, cs], ALU.mult)
                            fw.op(pool, lambda e: e.tensor_tensor(out=wre[:, cs], in0=tq[tb][:], in1=tq[tb + 1][:], op=ALU.add),
                                  reads=[r_tq[tb], r_tq[tb + 1]], writes=[r_wre[p]])
                            fw.op(pool, lambda e: e.tensor_tensor(out=wim[:, cs], in0=tq[tb + 2][:], in1=tq[tb + 3][:], op=ALU.subtract),
                                  reads=[r_tq[tb + 2], r_tq[tb + 3]], writes=[r_wim[p]])
                        rho_b = rho_pp[:, q:q + 1].to_broadcast([128, L])
                        fw.op(dve, lambda e: e.tensor_tensor_scan(out=wre[:], data0=rho_b, data1=wre[:], initial=0.0, op0=ALU.mult, op1=ALU.add),
                              reads=r_wre + [r_pp], writes=r_wre)
                        fw.op(dve, lambda e: e.tensor_tensor_scan(out=wim[:], data0=rho_b, data1=wim[:], initial=0.0, op0=ALU.mult, op1=ALU.add),
                              reads=r_wim + [r_pp], writes=r_wim)
                        for p in range(NP):
                            cs = slice(p * TT, (p + 1) * TT)
                            b4 = (p % 2) * 4
                            v1, v2, v3, v4 = vq[b4], vq[b4 + 1], vq[b4 + 2], vq[b4 + 3]
                            rv = r_vq[b4:b4 + 4]
                            fw.op(dve, lambda e: e.tensor_tensor(out=v1[:], in0=wre[:, cs], in1=tcb[:, cs], op=ALU.mult), reads=[r_wre[p], r_tab], writes=[rv[0]])
                            fw.op(dve, lambda e: e.tensor_tensor(out=v2[:], in0=wre[:, cs], in1=tsb[:, cs], op=ALU.mult), reads=[r_wre[p], r_tab], writes=[rv[1]])
                            fw.op(dve, lambda e: e.tensor_tensor(out=v3[:], in0=wim[:, cs], in1=tcb[:, cs], op=ALU.mult), reads=[r_wim[p], r_tab], writes=[rv[2]])
                            fw.op(pool, lambda e: e.tensor_tensor(out=v4[:], in0=wim[:, cs], in1=tsb[:, cs], op=ALU.mult), reads=[r_wim[p], r_tab], writes=[rv[3]])
                            ps, psr = PS.next()
                            fw.op(pe, lambda e: e.matmul(ps[:], lhsT=LCr[:, qs], rhs=v1[:], start=True, stop=False), reads=[r_LC, rv[0]], writes=[psr], inc=False)
                            fw.op(pe, lambda e: e.matmul(ps[:], lhsT=nLCr[:, qs], rhs=v4[:], start=False, stop=False), reads=[r_LC, rv[3]], writes=[psr], inc=False)
                            fw.op(pe, lambda e: e.matmul(ps[:], lhsT=LCi[:, qs], rhs=v2[:], start=False, stop=False), reads=[r_LC, rv[1]], writes=[psr], inc=False)
                            fw.op(pe, lambda e: e.matmul(ps[:], lhsT=LCi[:, qs], rhs=v3[:], start=False, stop=True), reads=[r_LC, rv[2]], writes=[psr], inc=True)
                            if qq == 0:
                                evac(yacc[:, cs], ps[:], [psr], [r_ya[p]], eng=act)
                            else:
                                fw.op(dve, lambda e: e.tensor_tensor(out=yacc[:, cs], in0=yacc[:, cs], in1=ps[:], op=ALU.add),
                                      reads=[psr, r_ya[p]], writes=[r_ya[p]])
                    for p in range(NP):
                        cs = slice(p * TT, (p + 1) * TT)
                        b = p % 2
                        fw.op(dve, lambda e: e.scalar_tensor_tensor(out=ytmp[b][:], in0=ub[:, cs], scalar=vcol(l, "s5_d", pt), in1=yacc[:, cs],
                                                                    op0=ALU.mult, op1=ALU.add), reads=[r_ub, r_ya[p], r_vec[l]], writes=[r_yt[b]])
                        fw.op(act, lambda e: e.activation(out=G[:, pt, cs], in_=ytmp[b][:], func=AF.Gelu_apprx_tanh),
                              reads=[r_yt[b]], writes=[r_G[pt][p]])
                Wg = [load_w(w_glu[l, mt], ncol=512) for mt in range(4)]
                for p in range(NP):
                    cs = slice(p * TT, (p + 1) * TT)
                    for mt in range(4):
                        W, wr = Wg[mt]
                        ps, psr = PS.next()
                        for kc in range(4):
                            fw.op(pe, lambda e: e.matmul(ps[:], lhsT=W[:, kc * 128:(kc + 1) * 128], rhs=G[:, kc, cs], start=(kc == 0), stop=(kc == 3)),
                                  reads=[wr, r_G[kc][p]], writes=[psr], inc=(kc == 3))
                        b = mt % 2
                        evac(ytmp[b][:], ps[:], [psr, r_vec[l]], [r_yt[b]], bias=vcol(l, "s5_bglu", mt), func=AF.Sigmoid)
                        sg, sr = stg.next()
                        fw.op(dve, lambda e: e.tensor_tensor(out=sg[:], in0=ytmp[b][:], in1=G[:, mt, cs], op=ALU.mult),
                              reads=[r_yt[b], r_G[mt][p]], writes=[sr])
                        fw.dma(sp, chunk_view(mixT, p, mt), sg[:], reads=[sr], writes=[r_mixT])

            fw.barrier()
            with contextlib.ExitStack() as st:
                nb = norm_bufs(st)
                nxb = 2
                xts = [(sbt(st, [128, 16, TT], F32), [Reg() for _ in range(16)]) for _ in range(nxb)]
                mts = [(sbt(st, [128, 16, TT], BF16), Reg()) for _ in range(1)]
                ht, hr = sbt(st, [128, 16, TT], BF16), [Reg() for _ in range(16)]
                ostg = Rot([(sbt(st, [128, TT], F32), Reg()) for _ in range(2)])
                actb = sbt(st, [128, NFT, TT], BF16); r_act = [Reg() for _ in range(NFT)]
                gbuf = [sbt(st, [128, 2 + TT], F32) for _ in range(2)]; r_gb = [Reg(), Reg()]
                cb = [sbt(st, [128, TT], F32) for _ in range(2)]; r_cb = [Reg(), Reg()]
                halo = sbt(st, [128, NFT, 2], F32); r_halo = [Reg() for _ in range(NFT)]
                wbig = Rot([(sbt(st, [128, NFT * 128], BF16), Reg()) for _ in range(2)])
                fw.op(dve, lambda e: e.memset(halo[:], 0.0), writes=r_halo)
                last = (l == depth - 1)
                gat = last and split
                if gat:
                    assert l > 0
                    hx = sbt(st, [128, 16, 2], F32); r_hx = [Reg() for _ in range(16)]
                    hm = sbt(st, [128, 16, 2], BF16); r_hm = Reg()
                    hh = sbt(st, [128, 16, 2], BF16); r_hh = [Reg() for _ in range(16)]
                for j in range(NTL if gat else NT):
                    cs = slice(j * TT, (j + 1) * TT)
                    xt, xr = xts[j % nxb]
                    mtile, mr = mts[0]
                    dohalo = gat and j == 0
                    if gat:
                        if j == 0:
                            fw.gather(xt[:].rearrange("p c n -> p (c n)"), xs, gidx[:, j:j + 1], reads=[r_xs, r_gidx], writes=xr)
                            fw.gather(mtile[:].rearrange("p c n -> p (c n)"), mixT, gidx[:, j:j + 1], reads=[r_mixT, r_gidx], writes=[mr])
                    elif j == 0:
                        if l == 0:
                            fw.dma(sp, xt[:], xview(xT)[:, :, cs], writes=xr)
                        else:
                            fw.dma(sp, xt[:], tile_view(xs, j), reads=[r_xs], writes=xr)
                        fw.dma(sp, mtile[:], tile_view(mixT, j), reads=[r_mixT], writes=[mr])
                    if dohalo:
                        fw.dma(sp, hx[:], tile_view(xs, NTL - 1)[:, :, TT - 2:TT], reads=[r_xs], writes=r_hx)
                        fw.dma(sp, hm[:], tile_view(mixT, NTL - 1)[:, :, TT - 2:TT], reads=[r_mixT], writes=[r_hm])
                    for mt in range(16):
                        W, wr = load_w(w_out[l, mt])
                        ps, psr = PS.next()
                        for kc in range(16):
                            fw.op(pe, lambda e: e.matmul(ps[:], lhsT=W[:, kc * 128:(kc + 1) * 128], rhs=mtile[:, kc, :],
                                                         start=(kc == 0), stop=(kc == 15)), reads=[wr, mr], writes=[psr], inc=(kc == 15))
                        fw.op(dve, lambda e: e.tensor_tensor(out=xt[:, mt, :], in0=xt[:, mt, :], in1=ps[:], op=ALU.add),
                              reads=[psr, xr[mt]], writes=[xr[mt]])
                        if dohalo:
                            psh, pshr = PS.next()
                            for kc in range(16):
                                fw.op(pe, lambda e: e.matmul(psh[:, 0:2], lhsT=W[:, kc * 128:(kc + 1) * 128], rhs=hm[:, kc, :],
                                                             start=(kc == 0), stop=(kc == 15)), reads=[wr, r_hm], writes=[pshr], inc=(kc == 15))
                            fw.op(dve, lambda e: e.tensor_tensor(out=hx[:, mt, :], in0=hx[:, mt, :], in1=psh[:, 0:2], op=ALU.add),
                                  reads=[pshr, r_hx[mt]], writes=[r_hx[mt]])
                            fw.op(dve, lambda e: e.tensor_scalar(out=hx[:, mt, :], in0=hx[:, mt, :], scalar1=flagT[:, 0:1], scalar2=None, op0=ALU.mult),
                                  reads=[r_hx[mt], r_gidx], writes=[r_hx[mt]])
                    if gat and j + 1 < NTL:
                        xt2, xr2 = xts[(j + 1) % nxb]
                        fw.gather(xt2[:].rearrange("p c n -> p (c n)"), xs, gidx[:, j + 1:j + 2], reads=[r_xs, r_gidx], writes=xr2)
                    if (not gat) and j + 1 < NT:
                        xt2, xr2 = xts[(j + 1) % nxb]
                        cs2 = slice((j + 1) * TT, (j + 2) * TT)
                        if l == 0:
                            fw.dma(sp, xt2[:], xview(xT)[:, :, cs2], writes=xr2)
                        else:
                            fw.dma(sp, xt2[:], tile_view(xs, j + 1), reads=[r_xs], writes=xr2)
                    rmsnorm(nb, xt, xr, vcol(l, "g_ffn", 0, 16), ht, hr)
                    if dohalo:
                        rmsnorm(nb, hx, r_hx, vcol(l, "g_ffn", 0, 16), hh, r_hh, w=2)
                    for ft in range(NFT):
                        Wg_, wgr = load_w(w_up[l, ft])
                        Wv_, wvr = load_w(w_up[l, NFT + ft])
                        psg, psgr = PS.next()
                        psv, psvr = PS.next()
                        for kc in range(16):
                            fw.op(pe, lambda e: e.matmul(psg[:], lhsT=Wg_[:, kc * 128:(kc + 1) * 128], rhs=ht[:, kc, :],
                                                         start=(kc == 0), stop=(kc == 15)), reads=[wgr, hr[kc]], writes=[psgr], inc=(kc == 15))
                        for kc in range(16):
                            fw.op(pe, lambda e: e.matmul(psv[:], lhsT=Wv_[:, kc * 128:(kc + 1) * 128], rhs=ht[:, kc, :],
                                                         start=(kc == 0), stop=(kc == 15)), reads=[wvr, hr[kc]], writes=[psvr], inc=(kc == 15))
                        b = ft % 2
                        gbt, gbr = gbuf[b], r_gb[b]
                        fw.op(act, lambda e: e.copy(out=gbt[:, 2:], in_=psg[:]), reads=[psgr], writes=[gbr])
                        if dohalo:
                            psh, pshr = PS.next()
                            for kc in range(16):
                                fw.op(pe, lambda e: e.matmul(psh[:, 0:2], lhsT=Wg_[:, kc * 128:(kc + 1) * 128], rhs=hh[:, kc, :],
                                                             start=(kc == 0), stop=(kc == 15)), reads=[wgr, r_hh[kc]], writes=[pshr], inc=(kc == 15))
                            fw.op(dve, lambda e: e.tensor_copy(out=gbt[:, 0:2], in_=psh[:, 0:2]), reads=[pshr], writes=[gbr])
                        else:
                            fw.op(dve, lambda e: e.tensor_copy(out=gbt[:, 0:2], in_=halo[:, ft, :]), reads=[r_halo[ft]], writes=[gbr])
                        fw.op(dve, lambda e: e.tensor_copy(out=halo[:, ft, :], in_=gbt[:, TT:TT + 2]), reads=[gbr], writes=[r_halo[ft]])
                        wd = lambda k: vcol(l, "ffn_wdw", ft * 3 + k)
                        fw.op(dve, lambda e: e.tensor_scalar(out=cb[b][:], in0=gbt[:, 0:TT], scalar1=wd(0), scalar2=vcol(l, "ffn_bdw", ft),
                                                             op0=ALU.mult, op1=ALU.add), reads=[gbr, r_vec[l]], writes=[r_cb[b]])
                        for k in (1, 2):
                            fw.op(dve, lambda e: e.scalar_tensor_tensor(out=cb[b][:], in0=gbt[:, k:k + TT], scalar=wd(k), in1=cb[b][:],
                                                                        op0=ALU.mult, op1=ALU.add), reads=[gbr, r_cb[b], r_vec[l]], writes=[r_cb[b]])
                        fw.op(act, lambda e: e.activation(out=cb[b][:], in_=cb[b][:], func=AF.Gelu_apprx_tanh), reads=[r_cb[b]], writes=[r_cb[b]])
                        fw.op(dve, lambda e: e.tensor_tensor(out=actb[:, ft, :], in0=cb[b][:], in1=psv[:], op=ALU.mult),
                              reads=[r_cb[b], psvr], writes=[r_act[ft]])
                    if gat and j + 1 < NTL:
                        fw.gather(mtile[:].rearrange("p c n -> p (c n)"), mixT, gidx[:, j + 1:j + 2], reads=[r_mixT, r_gidx], writes=[mr])
                    if (not gat) and j + 1 < NT:
                        fw.dma(sp, mtile[:], tile_view(mixT, j + 1), reads=[r_mixT], writes=[mr])
                    for mt in range(16):
                        W, wr = load_w(w_down[l, mt], rot=wbig, ncol=NFT * 128)
                        ps, psr = PS.next()
                        for kc in range(NFT):
                            fw.op(pe, lambda e: e.matmul(ps[:], lhsT=W[:, kc * 128:(kc + 1) * 128], rhs=actb[:, kc, :],
                                                         start=(kc == 0), stop=(kc == NFT - 1)), reads=[wr, r_act[kc]], writes=[psr], inc=(kc == NFT - 1))
                        fw.op(dve, lambda e: e.tensor_tensor(out=xt[:, mt, :], in0=xt[:, mt, :], in1=ps[:], op=ALU.add),
                              reads=[psr, xr[mt]], writes=[xr[mt]])
                    if last:
                        def _of(c):
                            t_, r_ = ostg.next()
                            return t_[:], r_

                        def _pf(c, oap, oreg):
                            fw.dma(sp, outT[c * 128:(c + 1) * 128, cs], oap, reads=[oreg], writes=[r_out])
                        rmsnorm(nb, xt, xr, vcol(l, "g_fin", 0, 16), None, None, out_fn=_of, post_fn=_pf)
                    else:
                        fw.dma(sp, tile_view(xs, j), xt[:], reads=xr, writes=[r_xs])
        fw.finish()
    return nc


def _cols(v, n):
    return np.ascontiguousarray(np.asarray(v, np.float32).reshape(n, 128).T)


def _wtile(w, nk, nm):
    w = np.asarray(w, np.float32).reshape(nk, 128, nm, 128)
    return np.ascontiguousarray(w.transpose(2, 1, 0, 3).reshape(nm, 128, nk * 128))


def prep_weights(inp, depth=DEPTH):
    f = lambda k: np.asarray(inp[k], np.float32)
    out = {}
    out["w_in"] = np.stack([_wtile(f("w_in")[l], 16, 24) for l in range(depth)])
    out["w_out"] = np.stack([_wtile(f("w_out")[l], 16, 16) for l in range(depth)])
    out["w_up"] = np.stack([_wtile(f("ffn_w_up")[l], 16, 86) for l in range(depth)])
    out["w_down"] = np.stack([_wtile(f("ffn_w_down")[l], NFT, 16) for l in range(depth)])
    out["w_glu"] = np.stack([_wtile(f("s5_w_glu")[l], 4, 4) for l in range(depth)])
    out["w_pw"] = np.stack([_wtile(f("cv_w_pw")[l], 4, 4) for l in range(depth)])

    def blk(w):
        o = np.zeros((4, 128, 128), np.float32)
        for h in range(8):
            ct, hh = divmod(h, 2)
            o[ct, hh * 64:(hh + 1) * 64, hh * 64:(hh + 1) * 64] = w[h]
        return o
    out["w_lr"] = np.stack([blk(f("lru_w_r")[l]) for l in range(depth)])
    out["w_li"] = np.stack([blk(f("lru_w_i")[l]) for l in range(depth)])
    out["w_pool"] = np.ascontiguousarray(f("pool_w")[:depth])
    vec = np.zeros((depth, 128, NV), np.float32)
    invc = np.zeros((4, 16), np.float32)
    for gi in range(4):
        invc[gi] = 1.0 / np.minimum(np.arange(16) + 1.0, 2.0 ** (gi + 1))
    for l in range(depth):
        def put(name, arr):
            arr = np.asarray(arr, np.float32)
            vec[l, :, VOFF[name]:VOFF[name] + arr.shape[1]] = arr
        put("g_mix", _cols(f("norm_mix_g")[l], 16))
        put("g_ffn", _cols(f("norm_ffn_g")[l], 16))
        put("g_fin", _cols(f("norm_final_g"), 16))
        put("s5_d", _cols(f("s5_d")[l], 4))
        put("s5_bglu", _cols(f("s5_b_glu")[l], 4))
        wdw = f("cv_w_dw")[l]
        put("cv_wdw", wdw.reshape(31, 4, 128).transpose(2, 1, 0).reshape(128, 124))
        put("cv_bdw", _cols(f("cv_b_dw")[l], 4))
        put("cv_lng", _cols(f("cv_ln_g")[l], 4))
        put("cv_lnb", _cols(f("cv_ln_b")[l], 4))
        put("cv_bpw", _cols(f("cv_b_pw")[l], 4))
        put("lru_wc", f("lru_w_conv")[l].reshape(4, 4, 128).transpose(2, 1, 0).reshape(128, 16))
        put("lru_bc", _cols(f("lru_b_conv")[l], 4))
        put("lru_br", _cols(f("lru_b_r")[l], 4))
        put("lru_bi", _cols(f("lru_b_i")[l], 4))
        put("lru_lam", _cols(f("lru_lam")[l], 4))
        put("pool_scale", _cols(f("pool_scale")[l], 4))
        put("ffn_wdw", f("ffn_w_dw")[l].reshape(3, NFT, 128).transpose(2, 1, 0).reshape(128, NFT * 3))
        put("ffn_bdw", _cols(f("ffn_b_dw")[l], NFT))
        put("invc", np.broadcast_to(invc.reshape(1, 64), (128, 64)))
    out["vec"] = vec
    s5_pp = np.zeros((depth, 128, 48), np.float32)
    s5_row = np.zeros((depth, 3, 2048), np.float32)
    s5_bz = np.zeros((depth, 2, 128, 2048), np.float32)
    s5_cz = np.zeros((depth, 2, 128, 2048), np.float32)
    for l in range(depth):
        lre, lim, lst = f("s5_lam_re")[l], f("s5_lam_im")[l], f("s5_log_step")[l]
        s5_pp[l, :, 0:16] = lre.reshape(16, 128).T
        s5_pp[l, :, 16:32] = lim.reshape(16, 128).T
        s5_pp[l, :, 32:48] = np.repeat(lst, 64).reshape(16, 128).T
        s5_row[l, 0] = lre.reshape(-1)
        s5_row[l, 1] = lim.reshape(-1)
        s5_row[l, 2] = np.repeat(lst, 64)
        for k, (bk, ck) in enumerate([("s5_b_re", "s5_c_re"), ("s5_b_im", "s5_c_im")]):
            B = f(bk)[l]
            C = f(ck)[l]
            for g in range(32):
                q, gi = divmod(g, 2)
                gl = g % 8
                s5_bz[l, k, gl * 16:(gl + 1) * 16, q * 128 + gi * 64:q * 128 + (gi + 1) * 64] = B[g].T
                s5_cz[l, k, gi * 64:(gi + 1) * 64, q * 128 + gl * 16:q * 128 + (gl + 1) * 16] = C[g].T
    out["s5_pp"], out["s5_row"], out["s5_bz"], out["s5_cz"] = s5_pp, s5_row, s5_bz, s5_cz
    out["ident"] = np.eye(128, dtype=np.float32)
    return out


_NC_CACHE = {}


def core_aux(r, L):
    nt = L // TT
    ntl = nt // 2
    gi = np.zeros((128, 64), np.int32)
    for i in range(ntl):
        gi[:, i] = (r * ntl + i) * 128 + np.arange(128)
    return gi, np.full((128, 1), float(r), np.float32)


def kernel(**inputs):
    x = np.asarray(inputs["x"], np.float32)
    nb, L, _ = x.shape
    wts = prep_weights(inputs)
    if L not in _NC_CACHE:
        _NC_CACHE[L] = build(L)
    nc = _NC_CACHE[L]
    in_maps = []
    for c in range(2 * nb):
        m = dict(wts)
        m["xT"] = np.ascontiguousarray(x[c // 2].T)
        m["gidx"], m["hflag"] = core_aux(c % 2, L)
        in_maps.append(m)
    res = run_bass_kernel_spmd(nc, in_maps, core_ids=list(range(2 * nb)))
    out = np.empty((nb, L, D), np.float32)
    for c in range(2 * nb):
        r = c % 2
        out[c // 2, r * (L // 2):(r + 1) * (L // 2), :] = res.results[c]["outT"].T
    return out.astype(np.float32)
```
